# Optimizing a Trainium2 kernel written in Bass

```python
import math
import jax, jax.numpy as jnp
from jax import lax
import numpy as np

D_MODEL = 1024
BATCH = 8
SEQ = 4096
DEPTH = 1
DEC_BATCH = 8
DEC_SEQ = 32
PAST_LEN = 1024

CHUNK = 64
N_META = 16
H_ATT = 4
D_QK = 64
D_V = 2 * D_QK
W_ATT = H_ATT * D_V
W_CONV = D_MODEL - W_ATT
CONV_W = 3
N_SPLIT = 8
W_IN = W_ATT * 4 + W_CONV * 4
N_BUCKETS = 32
MAX_DIST = 128
Q_BLOCK = 128
EPS = 1e-6
NEG = -1e30
SCALE = D_QK ** -0.5

kernel_name = "hybrid_diffattn_shortconv_stream_step"

f32 = jnp.float32


def rmsnorm(x, g):
    xf = x.astype(f32)
    y = xf * lax.rsqrt(jnp.mean(xf * xf, axis=-1, keepdims=True) + EPS)
    return (y * g.astype(f32)).astype(x.dtype)


def rel_bucket(rel):
    nb = N_BUCKETS // 2
    ret = jnp.where(rel > 0, nb, 0)
    n = jnp.abs(rel)
    max_exact = nb // 2
    nf = jnp.maximum(n, 1).astype(f32)
    large = max_exact + (jnp.log(nf / max_exact) / math.log(MAX_DIST / max_exact)
                         * (nb - max_exact)).astype(jnp.int32)
    large = jnp.minimum(large, nb - 1)
    return ret + jnp.where(n < max_exact, n, large)


def chunk_id(pos):
    return jnp.where(pos < 0, -1, pos // CHUNK)


def split_proj(n, w):
    p = n @ w
    b, l = p.shape[:2]
    q, k, v, z_a, b_g, c_g, u, z_c = jnp.split(p, N_SPLIT, axis=-1)
    q = q.reshape(b, l, H_ATT, 2, D_QK)
    k = k.reshape(b, l, H_ATT, 2, D_QK)
    v = v.reshape(b, l, H_ATT, D_V)
    return q, k, v, z_a, b_g, c_g, u, z_c


def diff_lambda(lq1, lk1, lq2, lk2, lam_init):
    return (jnp.exp(jnp.sum(lq1.astype(f32) * lk1.astype(f32)))
            - jnp.exp(jnp.sum(lq2.astype(f32) * lk2.astype(f32))) + lam_init)


def diff_attend(q, k, v, q_pos, k_pos, lam, rel_bias):
    s = jnp.einsum('bqhcd,bkhcd->bchqk', q, k, preferred_element_type=f32) * SCALE
    bias = rel_bias[rel_bucket(k_pos[None, :] - q_pos[:, None])]
    s = s + jnp.transpose(bias, (2, 0, 1)).astype(f32)[None, None]
    vis = chunk_id(k_pos)[None, :] <= chunk_id(q_pos)[:, None]
    s = jnp.where(vis, s, NEG)
    p = jax.nn.softmax(s, axis=-1)
    a = p[:, 0] - lam * p[:, 1]
    return jnp.einsum('bhqk,bkhd->bqhd', a.astype(v.dtype), v)


def short_conv(u_pad, w):
    l = u_pad.shape[1] - (CONV_W - 1)
    out = w[0] * u_pad[:, 0:l]
    for j in range(1, CONV_W):
        out = out + w[j] * u_pad[:, j:j + l]
    return out


def merge_branches(o, sub_g, lam_init, z_a, conv_out, b_g, z_c, w_o):
    o = rmsnorm(o, sub_g) * (1.0 - lam_init)
    o = o.reshape(*o.shape[:2], W_ATT) * jax.nn.silu(z_a)
    y_c = jax.nn.silu(z_c) * b_g * conv_out
    return jnp.concatenate([o, y_c], axis=-1) @ w_o


def setup_inputs(seed: int = 0) -> dict:
    key = jax.random.key(seed)
    ks = jax.random.split(key, 17)
    nrm = jax.random.normal
    return {
        "x_prompt": nrm(ks[0], (BATCH, SEQ, D_MODEL), f32),
        "x_sample": nrm(ks[1], (DEC_BATCH, DEC_SEQ, D_MODEL), f32),
        "cache_k": nrm(ks[2], (DEPTH, DEC_BATCH, N_META + PAST_LEN, H_ATT, 2 * D_QK), f32),
        "cache_v": nrm(ks[3], (DEPTH, DEC_BATCH, N_META + PAST_LEN, H_ATT, D_V), f32),
        "state_conv": nrm(ks[4], (DEPTH, DEC_BATCH, CONV_W - 1, W_CONV), f32),
        "meta_tokens": nrm(ks[5], (N_META, D_MODEL), f32),
        "rel_bias": 0.1 * nrm(ks[6], (N_BUCKETS, H_ATT), f32),
        "norm_g": 1.0 + 0.1 * nrm(ks[7], (DEPTH, D_MODEL), f32),
        "w_in": nrm(ks[8], (DEPTH, D_MODEL, W_IN), f32) * D_MODEL ** -0.5,
        "conv_w": nrm(ks[9], (DEPTH, CONV_W, W_CONV), f32) * CONV_W ** -0.5,
        "lambda_q1": 0.1 * nrm(ks[10], (DEPTH, D_QK), f32),
        "lambda_k1": 0.1 * nrm(ks[11], (DEPTH, D_QK), f32),
        "lambda_q2": 0.1 * nrm(ks[12], (DEPTH, D_QK), f32),
        "lambda_k2": 0.1 * nrm(ks[13], (DEPTH, D_QK), f32),
        "subln_g": 1.0 + 0.1 * nrm(ks[14], (DEPTH, D_V), f32),
        "w_out": nrm(ks[15], (DEPTH, W_ATT + W_CONV, D_MODEL), f32) * (W_ATT + W_CONV) ** -0.5,
        "final_g": 1.0 + 0.1 * nrm(ks[16], (D_MODEL,), f32),
    }


def reference(x_prompt, x_sample, cache_k, cache_v, state_conv, meta_tokens, rel_bias,
              norm_g, w_in, conv_w, lambda_q1, lambda_k1, lambda_q2, lambda_k2,
              subln_g, w_out, final_g):
    bp, sp = x_prompt.shape[:2]
    bd, sd = x_sample.shape[:2]
    past = cache_k.shape[2] - N_META
    n_blk = sp // Q_BLOCK

    meta_pos = jnp.arange(-N_META, 0, dtype=jnp.int32)
    p_qpos = jnp.arange(sp, dtype=jnp.int32)
    p_kpos = jnp.concatenate([meta_pos, p_qpos])
    d_qpos = past + jnp.arange(sd, dtype=jnp.int32)
    d_kpos = jnp.concatenate([meta_pos, jnp.arange(past, dtype=jnp.int32), d_qpos])

    hp = jnp.concatenate(
        [jnp.broadcast_to(meta_tokens[None].astype(x_prompt.dtype), (bp, N_META, D_MODEL)), x_prompt],
        axis=1)
    hd = x_sample

    kp_l, vp_l, cp_l, kd_l, vd_l, cd_l = [], [], [], [], [], []
    for l in range(DEPTH):
        lam_init = 0.8 - 0.6 * math.exp(-0.3 * l)
        lam = diff_lambda(lambda_q1[l], lambda_k1[l], lambda_q2[l], lambda_k2[l], lam_init)

        q, k, v, z_a, b_g, c_g, u, z_c = split_proj(rmsnorm(hp, norm_g[l]), w_in[l])
        o_meta = diff_attend(q[:, :N_META], k[:, :N_META], v[:, :N_META],
                             meta_pos, meta_pos, lam, rel_bias)
        q_blk = jnp.transpose(q[:, N_META:].reshape(bp, n_blk, Q_BLOCK, H_ATT, 2, D_QK),
                              (1, 0, 2, 3, 4, 5))
        pos_blk = p_qpos.reshape(n_blk, Q_BLOCK)
        o_frm = lax.map(lambda a: diff_attend(a[0], k, v, a[1], p_kpos, lam, rel_bias),
                        (q_blk, pos_blk))
        o_frm = jnp.transpose(o_frm, (1, 0, 2, 3, 4)).reshape(bp, sp, H_ATT, D_V)
        o = jnp.concatenate([o_meta, o_frm], axis=1)
        u_pad = jnp.concatenate([jnp.zeros((bp, CONV_W - 1, W_CONV), u.dtype), c_g * u], axis=1)
        hp = hp + merge_branches(o, subln_g[l], lam_init, z_a,
                                 short_conv(u_pad, conv_w[l]), b_g, z_c, w_out[l])
        kp_l.append(k.reshape(bp, N_META + sp, H_ATT, 2 * D_QK))
        vp_l.append(v)
        cp_l.append(u_pad[:, -(CONV_W - 1):])

        q2, k2, v2, z_a2, b_g2, c_g2, u2, z_c2 = split_proj(rmsnorm(hd, norm_g[l]), w_in[l])
        k_all = jnp.concatenate(
            [cache_k[l].reshape(bd, N_META + past, H_ATT, 2, D_QK).astype(k2.dtype), k2], axis=1)
        v_all = jnp.concatenate([cache_v[l].astype(v2.dtype), v2], axis=1)
        o2 = diff_attend(q2, k_all, v_all, d_qpos, d_kpos, lam, rel_bias)
        u2_pad = jnp.concatenate([state_conv[l].astype(u2.dtype), c_g2 * u2], axis=1)
        hd = hd + merge_branches(o2, subln_g[l], lam_init, z_a2,
                                 short_conv(u2_pad, conv_w[l]), b_g2, z_c2, w_out[l])
        kd_l.append(k2.reshape(bd, sd, H_ATT, 2 * D_QK))
        vd_l.append(v2)
        cd_l.append(u2_pad[:, -(CONV_W - 1):])

    y_prompt = rmsnorm(hp[:, N_META:], final_g)
    y_sample = rmsnorm(hd, final_g)
    return (y_prompt, y_sample, jnp.stack(kp_l), jnp.stack(vp_l), jnp.stack(cp_l),
            jnp.stack(kd_l), jnp.stack(vd_l), jnp.stack(cd_l))
```

```python
import numpy as np
from contextlib import ExitStack
import concourse.bass as bass
import concourse.mybir as mybir
from concourse.bass_utils import run_bass_kernel_spmd

F32 = mybir.dt.float32
BF16 = mybir.dt.bfloat16
ALU = mybir.AluOpType
AF = mybir.ActivationFunctionType
AX = mybir.AxisListType

N_CORES = 8
SEQ = 4096
D = 1024
T = 256
NT = SEQ // T
NMETA = 16
PAST = 1024
NCACHE = NMETA + PAST
DEC = 32
EPS = 1e-6
LAM_INIT = 0.2
MASKV = -30000.0
N_DMA_SEMS = 40
STAGE = None
N_PROMPT_TILES = None
SETUP_MASK = 0xFF
FLAG_ALL = False
CACHE_EVERY = 1000
MERGE_EXP = False

BIAS_STEPS = [(-90, 14), (-63, 13), (-45, 12), (-31, 11), (-22, 10), (-15, 9), (-11, 8),
              (-7, 7), (-6, 6), (-5, 5), (-4, 4), (-3, 3), (-2, 2), (-1, 1), (0, 0),
              (1, 17), (2, 18), (3, 19), (4, 20), (5, 21), (6, 22), (7, 23), (8, 24),
              (12, 25), (16, 26), (23, 27), (32, 28), (46, 29)]


class Op:
    __slots__ = ("eng", "fn", "reads", "writes", "dma", "deps", "flag", "ev", "is_out")

    def __init__(self, eng, fn, reads, writes, dma, is_out):
        self.eng = eng
        self.fn = fn
        self.reads = reads
        self.writes = writes
        self.dma = dma
        self.deps = ()
        self.flag = False
        self.ev = None
        self.is_out = is_out


class Prog:
    ENGS = ("pe", "act", "dve", "pool", "sp")

    def __init__(self):
        self.ops = []
        self.group = 0xFF

    def add(self, eng, fn, reads=(), writes=(), dma=False, is_out=False):
        if not (SETUP_MASK & self.group):
            return
        writes = tuple(writes) + tuple(t for t in reads if t[0] == "bank" and t not in writes)
        self.ops.append(Op(eng, fn, tuple(reads), tuple(writes), dma, is_out))

    def analyze(self):
        last_w = {}
        readers = {}
        ops = self.ops
        for i, op in enumerate(ops):
            raw = set()
            other = set()
            for t in op.reads:
                w = last_w.get(t)
                if w is not None:
                    raw.add(w)
            for t in op.writes:
                w = last_w.get(t)
                if w is not None:
                    other.add(w)
                for r in readers.get(t, ()):
                    other.add(r)
            deps = set()
            for d in raw | other:
                if d == i:
                    continue
                dop = ops[d]
                if (not dop.dma) and (not op.dma) and dop.eng == op.eng:
                    if op.eng == "pe":
                        continue
                    if d not in raw:
                        continue
                deps.add(d)
            op.deps = tuple(sorted(deps))
            for d in deps:
                ops[d].flag = True
            if FLAG_ALL and not op.dma:
                op.flag = True
            for t in op.writes:
                last_w[t] = i
                readers[t] = []
            for t in op.reads:
                readers.setdefault(t, []).append(i)

    def emit(self, nc, es):
        ops = self.ops
        self.analyze()
        last_ops = []
        for e in ("pe", "act", "dve", "pool"):
            lst = [op for op in ops if op.eng == e and not op.dma]
            if lst:
                lst[-1].flag = True
                last_ops.append(lst[-1])
        sems = {e: es.enter_context(nc.semaphore("s_" + e)) for e in ("pe", "act", "dve", "pool")}
        dsems = [es.enter_context(nc.semaphore("d%d" % i)) for i in range(N_DMA_SEMS)]
        cnt = {e: 0 for e in sems}
        dcnt = [0] * N_DMA_SEMS
        dlast = [None] * N_DMA_SEMS
        nd = 0
        for i, op in enumerate(ops):
            if op.dma:
                s = nd % N_DMA_SEMS
                nd += 1
                if dlast[s] is not None:
                    op.deps = tuple(sorted(set(op.deps) | {dlast[s]}))
                dcnt[s] += 16
                op.ev = (dsems[s], dcnt[s], ("d", s))
                dlast[s] = i
                op.flag = True
            elif op.flag:
                cnt[op.eng] += 1
                op.ev = (sems[op.eng], cnt[op.eng], op.eng)
        block = es.enter_context(nc.Block())
        per_eng = {e: [op for op in ops if op.eng == e] for e in self.ENGS}
        out_events = [op.ev for op in ops if op.is_out] + [op.ev for op in last_ops]

        import os
        dump = open(os.environ["KDUMP"], "w") if os.environ.get("KDUMP") else None

        def run(eng_obj, lst, final_events=()):
            waited = {}
            for op in lst:
                if dump:
                    dump.write("%s #%d deps=%s flag=%s ev=%s r=%s w=%s\n" % (
                        op.eng, ops.index(op), [(ops[d].ev[2], ops[d].ev[1]) for d in op.deps], op.flag,
                        (op.ev[2], op.ev[1]) if op.ev else None, op.reads[:4], op.writes[:4]))
                need = {}
                for d in op.deps:
                    sem, val, key = ops[d].ev
                    if waited.get(key, 0) >= val:
                        continue
                    if key not in need or need[key][1] < val:
                        need[key] = (sem, val)
                for key, (sem, val) in need.items():
                    eng_obj.wait_ge(sem, val)
                    waited[key] = val
                ins = op.fn(eng_obj)
                if op.flag:
                    ins.then_inc(op.ev[0], 16 if op.dma else 1)
            need = {}
            for (sem, val, key) in final_events:
                if waited.get(key, 0) >= val:
                    continue
                if key not in need or need[key][1] < val:
                    need[key] = (sem, val)
            for key, (sem, val) in need.items():
                eng_obj.wait_ge(sem, val)

        @block.sync
        def _(e):
            run(e, per_eng["sp"], out_events)

        @block.tensor
        def _(e):
            run(e, per_eng["pe"])

        @block.scalar
        def _(e):
            run(e, per_eng["act"])

        @block.vector
        def _(e):
            run(e, per_eng["dve"])

        @block.gpsimd
        def _(e):
            run(e, per_eng["pool"])


def build_nc():
    nc = bass.Bass("TRN2", target_bir_lowering=False)
    es = ExitStack()
    P = Prog()

    def din(name, shape):
        return nc.dram_tensor(name, shape, F32, kind="ExternalInput").ap()

    def dout(name, shape):
        return nc.dram_tensor(name, shape, F32, kind="ExternalOutput").ap()

    x_p = din("x_p", [SEQ, D])
    x_s = din("x_s", [DEC, D])
    ck = din("ck", [NCACHE, 512])
    cv = din("cv", [NCACHE, 512])
    sconv = din("sconv", [2, 512])
    meta = din("meta", [NMETA, D])
    relb = din("relb", [32, 4])
    norm_g = din("norm_g", [D])
    w_in = din("w_in", [D, 4096])
    conv_w = din("conv_w", [3, 512])
    lam_in = [din(n, [64]) for n in ("lq1", "lk1", "lq2", "lk2")]
    subln_g = din("subln_g", [128])
    w_out = din("w_out", [D, D])
    final_g = din("final_g", [D])
    y_p = dout("y_p", [SEQ, D])
    y_s = dout("y_s", [DEC, D])
    k_p = dout("k_p", [NMETA + SEQ, 512])
    v_p = dout("v_p", [NMETA + SEQ, 512])
    conv_p = dout("conv_p", [2, 512])
    k_s = dout("k_s", [DEC, 512])
    v_s = dout("v_s", [DEC, 512])
    conv_s = dout("conv_s", [2, 512])

    def sb(name, shape, dt=F32):
        return es.enter_context(nc.sbuf_tensor(name, shape, dt))

    w_in_sb = sb("w_in_sb", [128, 8, 4096], BF16)
    w_out_sb = sb("w_out_sb", [128, 8, D], BF16)
    kT_sb = sb("kT_sb", [128, 4, NMETA + SEQ], BF16)
    v_sb = sb("v_sb", [128, 33, 512], BF16)
    ident = sb("ident", [128, 128], BF16)
    ones = sb("ones", [128, 128], BF16)
    epsc = sb("epsc", [128, 1])
    g_sb = sb("g_sb", [128, 8])
    fg = sb("fg", [128, D])
    cw = sb("cw", [128, 3, 4])
    sg = sb("sg", [128, 1])
    gsc = sb("gsc", [128, 1])
    lsum = sb("lsum", [128, 2])
    lexp = sb("lexp", [128, 2])
    ldif = sb("ldif", [128, 1])
    neglam = sb("neglam", [128, 1])
    rbf = sb("rbf", [128, 4])
    B0 = [sb("B0_%d" % h, [128, 256], BF16) for h in range(4)]
    NBIG = 3
    big = [sb("big%d" % i, [128, D]) for i in range(NBIG)]
    xns = [sb("xn%d" % i, [128, D], BF16) for i in range(2)]
    xn = xns[0]
    nT = sb("nT", [128, 8, T], BF16)
    qz = sb("qz", [128, 4, 2, T], BF16)
    sza = sb("sza", [128, 4, T], BF16)
    mrgs = [sb("mrg%d" % i, [128, 8, T], BF16) for i in range(2)]
    NKV = 2
    kvst = [sb("kvst%d" % i, [128, 512]) for i in range(NKV)]
    cu = sb("cu", [128, 4, T + 2])
    cg = sb("cg", [128, T])
    szc = sb("szc", [128, T])
    ct1 = sb("ct1", [128, T])
    NP = 4
    N_SS = 4
    Pt_all = sb("Pt_all", [128, N_SS, 2 * T], BF16)
    Pt = [Pt_all[:, i, :].rearrange("p (s f) -> p s f", s=2) for i in range(N_SS)]
    rl = sb("rl", [128, 2, T])
    ob = sb("ob", [128, T])
    sq = sb("sq", [128, T], BF16)
    rs = sb("rs", [128, T])
    tt = sb("tt", [128, T])
    stat = sb("stat", [128, 16])
    junk = xn
    ct2 = sb("ct2", [128, T])
    rb_sb = ct2[:, 0:128]
    dl = ob[:, 0:len(BIAS_STEPS) * 4].rearrange("p (a b) -> p a b", b=4)
    sza_f = sza[:].rearrange("p a b -> p (a b)").bitcast(F32)
    lam4 = sza_f[:, 0:256].rearrange("p (a b) -> p a b", a=4)
    lprod = sza_f[:, 256:384].rearrange("p (a b) -> p a b", a=2)
    Rm = cg
    stepm = [szc, ct1]
    stepm_tok = [("szc",), ("ct1",)]
    ps_all = es.enter_context(nc.psum_tensor("ps_all", [128, 4096], F32))
    banks = [ps_all[:, i * 512:(i + 1) * 512] for i in range(8)]

    N_SSLOT = 4
    free_banks = list(range(N_SSLOT, 8))

    def bank_alloc():
        return free_banks.pop(0)

    def bank_free(b):
        if b in proj_taken:
            proj_taken.discard(b)
            return
        free_banks.append(b)

    proj_rr = [0]
    proj_order = [4, 5, 6, 7, 0, 1, 2, 3]
    proj_taken = set()

    def proj_alloc():
        b = proj_order[proj_rr[0] % 8]
        proj_rr[0] += 1
        proj_taken.add(b)
        return b

    def btok(b):
        return [("bank", b, 0), ("bank", b, 1)]

    rr = {"big": 0, "kv": 0, "P": 0}

    def next_big():
        i = rr["big"] % NBIG
        rr["big"] += 1
        return i

    def next_kv():
        i = rr["kv"] % NKV
        rr["kv"] += 1
        return i

    def next_P():
        i = rr["P"] % NP
        rr["P"] += 1
        return i

    def dma(out, in_, reads, writes, is_out=False, eng="sp"):
        P.add(eng, lambda e, o=out, i=in_: e.dma_start(out=o, in_=i, allow_slow_non_contiguous=True),
              reads, writes, dma=True, is_out=is_out)

    def act(out, in_, func, reads, writes, **kw):
        P.add("act", lambda e, o=out, i=in_, f=func, k=kw: e.activation(out=o, in_=i, func=f, **k),
              reads, writes)

    def tt_op(eng, out, in0, in1, op, reads, writes):
        P.add(eng, lambda e, o=out, a=in0, b=in1, p=op: e.tensor_tensor(out=o, in0=a, in1=b, op=p),
              reads, writes)

    def ts_op(eng, out, in0, s1, s2, op0, op1, reads, writes):
        if s2 is None:
            P.add(eng, lambda e, o=out, a=in0, x=s1, p=op0: e.tensor_scalar(out=o, in0=a, scalar1=x, scalar2=None, op0=p),
                  reads, writes)
        else:
            P.add(eng, lambda e, o=out, a=in0, x=s1, y=s2, p=op0, q=op1: e.tensor_scalar(
                out=o, in0=a, scalar1=x, scalar2=y, op0=p, op1=q), reads, writes)

    def stt_op(eng, out, in0, scalar, in1, op0, op1, reads, writes):
        P.add(eng, lambda e, o=out, a=in0, s=scalar, b=in1, p=op0, q=op1: e.scalar_tensor_tensor(
            out=o, in0=a, scalar=s, in1=b, op0=p, op1=q), reads, writes)

    def copy_op(eng, out, in_, reads, writes):
        if eng == "act":
            P.add("act", lambda e, o=out, i=in_: e.activation(out=o, in_=i, func=AF.Identity), reads, writes)
        else:
            P.add(eng, lambda e, o=out, i=in_: e.tensor_copy(out=o, in_=i), reads, writes)

    def recip(out, in_, reads, writes):
        P.add("dve", lambda e, o=out, i=in_: e.reciprocal(out=o, in_=i), reads, writes)

    def memset(eng, ap, val, writes):
        P.add(eng, lambda e, a=ap, v=val: e.memset(a, v), (), writes)

    def mm(out, lhsT, rhs, start, stop, reads, writes, **kw):
        P.add("pe", lambda e, o=out, l=lhsT, r=rhs, s=start, t=stop, k=kw: e.matmul(
            o, lhsT=l, rhs=r, start=s, stop=t, **k), reads, writes)

    def transpose(out, in_, idn, reads, writes):
        P.add("pe", lambda e, o=out, i=in_, d=idn: e.transpose(o, i, d), reads, writes)

    P.group = 1
    memset("pool", ident[:], 0.0, [("ident",)])
    P.add("pool", lambda e: e.affine_select(out=ident[:], in_=ident[:], compare_op=ALU.not_equal, fill=1.0,
                                            base=0, pattern=[[-1, 128]], channel_multiplier=1),
          [("ident",)], [("ident",)])
    memset("pool", ones[:], 1.0, [("ones",)])
    memset("pool", epsc[:], EPS, [("eps",)])
    memset("pool", qz[:], 0.0, [("qz", h, s_) for h in range(4) for s_ in range(2)])

    dma(g_sb[:], norm_g.rearrange("(kc p) -> p kc", p=128), [], [("g_sb",)])
    P.group = 2
    dma(rb_sb, relb.rearrange("b h -> (b h)").partition_broadcast(128), [], [("ct2",)])
    copy_op("dve", rbf[:], rb_sb[:, 60:64], [("ct2",)], [("rb",)])
    for i in range(4):
        dma(lam4[:, i, :], lam_in[i].partition_broadcast(128), [], [("sza", i)])
    dma(sg[:], subln_g.rearrange("(p o) -> p o", o=1), [], [("sg",)])
    for jj in range(3):
        dma(cw[:, jj, :], conv_w[jj, :].rearrange("(cc p) -> p cc", p=128), [], [("cw", jj)])
    dma(fg[:], final_g.partition_broadcast(128), [], [("fg",)])
    P.group = 4

    kT_all = lambda h: [("kT", h, b) for b in ["m"] + list(range(2 * NT))]
    ntile_c = (NCACHE + 127) // 128
    def cache_tile(t):
        r = min(128, NCACHE - t * 128)
        kb = next_kv()
        xi_ = t % 2
        dma(kvst[kb][0:r, :], ck[t * 128:t * 128 + r, :], [], [("kvst", kb)])
        copy_op("dve", xns[xi_][0:r, 0:512], kvst[kb][0:r, :], [("kvst", kb)], [("xn", xi_)])
        bk = bank_alloc()
        psb = banks[bk][:].bitcast(BF16)
        for h in range(4):
            transpose(psb[:, h * 128:h * 128 + r], xns[xi_][0:r, h * 128:(h + 1) * 128], ident[:r, :r],
                      [("xn", xi_), ("ident",)], btok(bk))
        copy_op("act", kT_sb[:, :, t * 128:t * 128 + r], psb[:, 0:512].rearrange("p (h t) -> p h t", h=4)[:, :, 0:r],
                btok(bk), [tk for h in range(4) for tk in kT_all(h)])
        bank_free(bk)
        kb = next_kv()
        dma(kvst[kb][0:r, :], cv[t * 128:t * 128 + r, :], [], [("kvst", kb)])
        copy_op("dve", v_sb[0:r, t, :], kvst[kb][0:r, :], [("kvst", kb)], [("v", t)])

    cache_thunks = [(lambda t=t: cache_tile(t)) for t in range(ntile_c)]
    piece_no = [0]

    def cache_step():
        piece_no[0] += 1
        if piece_no[0] % CACHE_EVERY == 0 and cache_thunks:
            cache_thunks.pop(0)()

    b0_thunks = []
    b0_acc = [(rl[:, 0, :], ("rl", 0)), (rl[:, 1, :], ("rl", 1)), (tt[:], ("tt",)), (rs[:], ("rs",))]

    def _b0_plan():
        b0_thunks.append(lambda: P.add("pool", lambda e: e.iota(Rm[:], pattern=[[-1, 256]], base=0, channel_multiplier=1,
                                                               allow_small_or_imprecise_dtypes=True), [], [("cg",)]))
        prev_b = 15
        for i, (thr, bk) in enumerate(BIAS_STEPS):
            b0_thunks.append(lambda i=i, bk=bk, pb=prev_b: tt_op(
                "dve", dl[:, i, :], rb_sb[:, bk * 4:bk * 4 + 4], rb_sb[:, pb * 4:pb * 4 + 4], ALU.subtract,
                [("ct2",)], [("ob",)]))
            prev_b = bk
        for h in range(4):
            b0_thunks.append(lambda h=h: memset("dve", b0_acc[h][0], 0.0, [b0_acc[h][1]]))
        for i, (thr, bk) in enumerate(BIAS_STEPS):
            sm = stepm[i % 2]
            b0_thunks.append(lambda i=i, sm=sm, thr=thr: P.add(
                "dve", lambda e, o=sm, t=float(thr): e.tensor_single_scalar(out=o[:], in_=Rm[:], scalar=t, op=ALU.is_ge),
                [("cg",)], [stepm_tok[i % 2]]))
            for h in range(4):
                b0_thunks.append(lambda i=i, sm=sm, h=h: stt_op(
                    "dve", b0_acc[h][0], sm[:], dl[:, i, h:h + 1], b0_acc[h][0], ALU.mult, ALU.add,
                    [stepm_tok[i % 2], ("ob",), b0_acc[h][1]], [b0_acc[h][1]]))
        for h in range(4):
            b0_thunks.append(lambda h=h: copy_op("dve", B0[h][:], b0_acc[h][0], [b0_acc[h][1]], [("B0", h)]))
        for h in range(4):
            b0_thunks.append(lambda h=h: memset("dve", B0[h][64:128, 0:64], MASKV, [("B0", h)]))

    _b0_plan()

    def b0_step(n):
        for _ in range(n):
            if b0_thunks:
                b0_thunks.pop(0)()

    stg = [(big[i][:], [("big", i)]) for i in range(NBIG)]
    for m_ in range(2):
        stg.append((mrgs[m_][:].rearrange("p a b -> p (a b)").bitcast(F32), [("mrg", m_, c) for c in range(8)]))
    stg.append((nT[:].rearrange("p a b -> p (a b)").bitcast(F32), [("nT",)]))
    stg_i = [0]

    def next_stg():
        i = stg_i[0] % len(stg)
        stg_i[0] += 1
        return stg[i]

    for q in (0, 2, 3, 1):
        for kc in range(8):
            buf, btk = next_stg()
            dma(buf, w_in[kc * 128:(kc + 1) * 128, q * 1024:(q + 1) * 1024], [], btk)
            act(w_in_sb[:, kc, q * 1024:(q + 1) * 1024], buf[:, :], AF.Copy,
                btk + [("g_sb",)], [("w_in", kc, q, 0), ("w_in", kc, q, 1)], scale=g_sb[:, kc:kc + 1])
            b0_step(4)
            cache_step()
    for kc in range(8):
        buf, btk = next_stg()
        dma(buf, w_out[kc * 128:(kc + 1) * 128, :], [], btk)
        copy_op("act", w_out_sb[:, kc, :], buf[:, :], btk, [("w_out", kc, 0), ("w_out", kc, 1)])
        b0_step(4)
        cache_step()
    b0_step(10000)
    while cache_thunks:
        cache_thunks.pop(0)()

    def w_in_tok(kc, c0, c1):
        toks = set()
        for c in range(c0 // 512, (c1 - 1) // 512 + 1):
            toks.add(("w_in", kc, c // 2, c % 2))
        return list(toks)

    P.group = 8
    tt_op("dve", lprod[:, 0, :], lam4[:, 0, :], lam4[:, 1, :], ALU.mult, [("sza", 0), ("sza", 1)], [("lprod", 0), ("sza", 2)])
    tt_op("dve", lprod[:, 1, :], lam4[:, 2, :], lam4[:, 3, :], ALU.mult, [("sza", 2), ("sza", 3)], [("lprod", 1), ("sza", 2)])
    P.add("dve", lambda e: e.reduce_sum(out=lsum[:], in_=lprod, axis=AX.X),
          [("lprod", 0), ("lprod", 1), ("sza", 2)], [("lsum",)])
    act(lexp[:], lsum[:], AF.Exp, [("lsum",)], [("lexp",)])
    tt_op("dve", ldif[:], lexp[:, 0:1], lexp[:, 1:2], ALU.subtract, [("lexp",)], [("ldif",)])
    ts_op("dve", neglam[:], ldif[:], -1.0, -LAM_INIT, ALU.mult, ALU.add, [("ldif",)], [("neglam",)])
    ts_op("dve", gsc[:], sg[:], 1.0 - LAM_INIT, None, ALU.mult, None, [("sg",)], [("gsc",)])

    P.group = 0xFF
    def norm_load(x_src, r, b=None):
        if b is None:
            b = next_big()
        dma(big[b][:r, :], x_src, [], [("big", b)])
        return b

    def norm_a(x_src, r, xi, b=None):
        if b is None:
            b = norm_load(x_src, r)
        act(xns[xi][:r, :], big[b][:r, :], AF.Square, [("big", b)], [("xn", xi), ("stat", xi, 0)],
            accum_out=stat[:r, 4 * xi:4 * xi + 1])
        act(stat[:r, 4 * xi + 1:4 * xi + 2], stat[:r, 4 * xi:4 * xi + 1], AF.Ln, [("stat", xi, 0), ("eps",)],
            [("stat", xi, 1)], scale=1.0 / D, bias=epsc[:r, 0:1])
        act(stat[:r, 4 * xi + 2:4 * xi + 3], stat[:r, 4 * xi + 1:4 * xi + 2], AF.Exp, [("stat", xi, 1)],
            [("stat", xi, 2)], scale=-0.5)
        ts_op("dve", xns[xi][:r, :], big[b][:r, :], stat[:r, 4 * xi + 2:4 * xi + 3], None, ALU.mult, None,
              [("big", b), ("stat", xi, 2)], [("xn", xi)])

    def norm_a_act(r, xi, b):
        P.add("dve", lambda e, o=xns[xi][:r, :], i=big[b][:r, :], a=stat[:r, 4 * xi:4 * xi + 1]: e.scalar_tensor_tensor(
            out=o, in0=i, scalar=1.0, in1=i, op0=ALU.mult, op1=ALU.mult, accum_out=a),
            [("big", b)], [("xn", xi), ("stat", xi, 0)])
        act(stat[:r, 4 * xi + 1:4 * xi + 2], stat[:r, 4 * xi:4 * xi + 1], AF.Ln, [("stat", xi, 0), ("eps",)],
            [("stat", xi, 1)], scale=1.0 / D, bias=epsc[:r, 0:1])
        act(stat[:r, 4 * xi + 2:4 * xi + 3], stat[:r, 4 * xi + 1:4 * xi + 2], AF.Exp, [("stat", xi, 1)],
            [("stat", xi, 2)], scale=-0.5)

    def norm_a_dve(r, xi, b):
        ts_op("dve", xns[xi][:r, :], big[b][:r, :], stat[:r, 4 * xi + 2:4 * xi + 3], None, ALU.mult, None,
              [("big", b), ("stat", xi, 2)], [("xn", xi)])

    def norm_b(r, row0, xi):
        bk = bank_alloc()
        psb = banks[bk][:].bitcast(BF16)
        for kc in range(8):
            transpose(psb[:, kc * 128:kc * 128 + r], xns[xi][:r, kc * 128:(kc + 1) * 128], ident[:r, :r],
                      [("xn", xi), ("ident",)], btok(bk))
        copy_op("dve", nT[:, :, row0:row0 + r], psb.rearrange("p (k t) -> p k t", k=8)[:, :, 0:r],
                btok(bk), [("nT",)])
        bank_free(bk)

    def norm_subtile(x_src, r, row0, Tt, xi=0):
        norm_a(x_src, r, xi)
        norm_b(r, row0, xi)

    def proj_feature(col0, Tt, evac):
        pass

    class HalfBanks:
        def __init__(self):
            self.cur = None
            self.used = 0

        def get(self):
            if self.cur is None or self.used == 2:
                if self.cur is not None:
                    bank_free(self.cur)
                self.cur = proj_alloc()
                self.used = 0
            h = self.used
            self.used += 1
            return self.cur, h

        def done(self):
            if self.cur is not None:
                bank_free(self.cur)
                self.cur = None
                self.used = 0

    def proj_fm(hb, col0, Tt):
        bk, half = hb.get()
        ps = banks[bk][:, half * 256:half * 256 + Tt]
        tok = btok(bk)
        for kc in range(8):
            mm(ps, w_in_sb[:, kc, col0:col0 + 128], nT[:, kc, 0:Tt], kc == 0, kc == 7,
               [("nT",)] + w_in_tok(kc, col0, col0 + 128), tok)
        return ps, tok

    def project_tile(Tt, subtiles, kcol0, vtile, krow0, want_q, want_conv_out, kdst, vdst, khs=None, mb=0, mid=None, mid2=None, ph=None, ktr_out=None):
        hb = HalfBanks()

        def run_ph(i):
            if ph and i in ph:
                for f_ in ph[i]:
                    f_()

        def pair(colA, evA, colB, evB):
            psA, tokA = proj_fm(hb, colA, Tt)
            psB, tokB = proj_fm(hb, colB, Tt)
            evA(psA, tokA)
            evB(psB, tokB)

        if mid is not None:
            mid()
        run_ph(0)
        if want_q:
            def ev_q(h):
                def f(ps, tok):
                    act(qz[0:64, h, 0, 0:Tt], ps[0:64, :], AF.Copy, [tok[0]], [("qz", h, 0)], scale=0.125)
                    P.add("dve", lambda e, o=qz[64:128, h, 1, 0:Tt], i=ps[64:128, :]: e.tensor_scalar(
                        out=o, in0=i, scalar1=0.125, scalar2=None, op0=ALU.mult), [tok[1]], [("qz", h, 1)])
                return f
            pair(0, ev_q(0), 128, ev_q(1))
            run_ph(0.5)
            pair(256, ev_q(2), 384, ev_q(3))
            run_ph(1)
        for cc in range(4):
            run_ph(2 + cc) if cc > 0 else None

            def ev_cg(ps, tok):
                copy_op("act", cg[:, 0:Tt], ps, tok, [("cg",)])

            def ev_u(ps, tok, cc=cc):
                tt_op("dve", cu[:, cc, 2:2 + Tt], ps, cg[:, 0:Tt], ALU.mult, tok + [("cg",)], [("cu", cc)])
                if want_conv_out:
                    ts_op("dve", ct1[:, 0:Tt], cu[:, cc, 0:Tt], cw[:, 0, cc:cc + 1], None, ALU.mult, None,
                          [("cu", cc), ("cw", 0), ("cw", 1), ("cw", 2)], [("ct1",)])
                    stt_op("dve", ct2[:, 0:Tt], cu[:, cc, 1:1 + Tt], cw[:, 1, cc:cc + 1], ct1[:, 0:Tt], ALU.mult, ALU.add,
                           [("cu", cc), ("cw", 0), ("cw", 1), ("cw", 2), ("ct1",)], [("ct2",)])
                    stt_op("dve", ct1[:, 0:Tt], cu[:, cc, 2:2 + Tt], cw[:, 2, cc:cc + 1], ct2[:, 0:Tt], ALU.mult, ALU.add,
                           [("cu", cc), ("cw", 0), ("cw", 1), ("cw", 2), ("ct2",)], [("ct1",)])
            pair(2560 + cc * 128, ev_cg, 3072 + cc * 128, ev_u)
            if want_conv_out:
                def ev_zc(ps, tok):
                    act(szc[:, 0:Tt], ps, AF.Silu, tok, [("szc",)])

                def ev_bg(ps, tok, cc=cc):
                    tt_op("dve", ct2[:, 0:Tt], ps, szc[:, 0:Tt], ALU.mult, tok + [("szc",)], [("ct2",)])
                    tt_op("dve", mrgs[mb][:, 4 + cc, 0:Tt], ct2[:, 0:Tt], ct1[:, 0:Tt], ALU.mult,
                          [("ct2",), ("ct1",)], [("mrg", mb, 4 + cc)])
                pair(3584 + cc * 128, ev_zc, 2048 + cc * 128, ev_bg)
        run_ph(6)
        if want_q:
            ev_za = lambda h: (lambda ps, tok: act(sza[:, h, 0:Tt], ps, AF.Silu, tok, [("sza", h)]))
            for h in (0, 2):
                pair(1536 + h * 128, ev_za(h), 1536 + (h + 1) * 128, ev_za(h + 1))
        hb.done()
        run_ph(7)
        if mid2 is not None:
            mid2()
        if STAGE == 0.7:
            return
        k_tr = []
        for si, (row0, r) in enumerate(subtiles):
            for which in ("v", "k"):
                c0 = 1024 if which == "v" else 512
                bk = proj_alloc()
                ps = banks[bk][0:r, :]
                for kc in range(8):
                    mm(ps, nT[:, kc, row0:row0 + r], w_in_sb[:, kc, c0:c0 + 512], kc == 0, kc == 7,
                       [("nT",)] + w_in_tok(kc, c0, c0 + 512), btok(bk))
                kb = next_kv()
                copy_op("act", kvst[kb][0:r, :], ps, btok(bk), [("kvst", kb)])
                if which == "v":
                    copy_op("dve", v_sb[0:r, vtile + si, :], ps, btok(bk), [("v", vtile + si)])
                    dma(vdst[krow0 + row0:krow0 + row0 + r, :], kvst[kb][0:r, :], [("kvst", kb)], [], is_out=True)
                else:
                    dma(kdst[krow0 + row0:krow0 + row0 + r, :], kvst[kb][0:r, :], [("kvst", kb)], [], is_out=True)
                    ps_slot = si % N_SS
                    kb16 = Pt_all[:, ps_slot, :]
                    copy_op("dve", kb16[0:r, :], ps, btok(bk), [("P", ps_slot)])

                    def _tr(kb16=kb16, ps_slot=ps_slot, row0=row0, r=r):
                        bt = proj_alloc()
                        psb = banks[bt].bitcast(BF16)
                        for h in range(4):
                            transpose(psb[:, h * 128:h * 128 + r], kb16[0:r, h * 128:(h + 1) * 128], ident[:r, :r],
                                      [("P", ps_slot), ("ident",)], btok(bt))
                        copy_op("act", kT_sb[:, :, kcol0 + row0:kcol0 + row0 + r],
                                psb[:, 0:512].rearrange("p (h t) -> p h t", h=4)[:, :, 0:r],
                                btok(bt), [tk for h in range(4) for tk in khs[h]])
                        bank_free(bt)
                    k_tr.append(_tr)
                bank_free(bk)
        if ktr_out is not None:
            ktr_out.extend(k_tr)
        else:
            for f_ in k_tr:
                f_()

    def conv_halo(Tt):
        for cc in range(4):
            copy_op("dve", cu[:, cc, 0:2], cu[:, cc, Tt:Tt + 2], [("cu", cc)], [("cu", cc)])

    def attention(Tt, keytiles, mb=0, hooks=None, uhooks=None, ahead=3):
        units = [(h, kt) for h in range(4) for kt in range(len(keytiles))]
        state = {}

        sc = state.setdefault("sc", [0])

        def pairable(u):
            if not MERGE_EXP or Tt != T or u + 1 >= len(units):
                return False
            (h0, k0), (h1, k1) = units[u], units[u + 1]
            A, B = keytiles[k0], keytiles[k1]
            ok = lambda K: K["near"] is None and K["nk"] == 128 and K["qa"] == 0 and K["qb"] == Tt
            return h0 == h1 and ok(A) and ok(B)

        def emit_S(u):
            h, ki = units[u]
            K = keytiles[ki]
            nk, qa, qb = K["nk"], K["qa"], K["qb"]
            role = state.get(("role", u))
            if role is None:
                if pairable(u) and ("role", u + 1) not in state:
                    state[("role", u)] = ("first", sc[0] % N_SS)
                    state[("role", u + 1)] = ("second", (sc[0] + 1) % N_SS)
                    sc[0] += 2
                else:
                    state[("role", u)] = ("single", sc[0] % N_SS)
                    sc[0] += 1
                role = state[("role", u)]
            kind, slot = role
            bk = slot
            ps3 = banks[bk].rearrange("p (s f) -> p s f", s=2)
            if qa == 0 and qb == Tt:
                mm(ps3[0:nk, :, qa:qb], kT_sb[:, h, K["kc0"]:K["kc0"] + nk], qz[:, h, :, qa:qb], True, True,
                   K["ktoks"](h) + [("qz", h, 0), ("qz", h, 1)], btok(bk))
            else:
                for s_ in range(2):
                    mm(ps3[0:nk, s_, qa:qb], kT_sb[:, h, K["kc0"]:K["kc0"] + nk], qz[:, h, s_, qa:qb], True, True,
                       K["ktoks"](h) + [("qz", h, 0), ("qz", h, 1)], btok(bk))
            if K["near"] is not None:
                na, nb, b0c = K["near"]
                P.add("dve", lambda e, o=ps3[0:nk, :, na:nb], b=B0[h][0:nk, b0c:b0c + (nb - na)].unsqueeze(1).to_broadcast(
                    [nk, 2, nb - na]): e.tensor_tensor(out=o, in0=o, in1=b, op=ALU.add),
                    btok(bk) + [("B0", h)], btok(bk))
            pi = slot
            if kind == "single":
                act(Pt[pi][0:nk, :, qa:qb], ps3[0:nk, :, qa:qb], AF.Exp, btok(bk) + [("rb",)], [("P", pi)],
                    bias=rbf[0:nk, h:h + 1])
            elif kind == "second":
                other = (slot - 1) % N_SS
                lo_, hi_ = min(slot, other), max(slot, other)
                st_ = hi_ - lo_
                act(Pt_all[:, lo_:hi_ + 1:st_, :], ps_all[:].rearrange("p (b f) -> p b f", f=512)[:, lo_:hi_ + 1:st_, :],
                    AF.Exp, btok(lo_) + btok(hi_) + [("rb",)], [("P", lo_), ("P", hi_)], bias=rbf[:, h:h + 1])
            state[u] = pi

        def emit_PV(u):
            h, ki = units[u]
            K = keytiles[ki]
            nk, qa, qb = K["nk"], K["qa"], K["qb"]
            pi = state.pop(u)
            first = ki == 0
            last = ki == len(keytiles) - 1
            if first:
                state["O"] = bank_alloc()
                state["L"] = bank_alloc()
            bo, bl = state["O"], state["L"]
            o3 = banks[bo][:].rearrange("p (s f) -> p s f", s=2)
            l3 = banks[bl][:].rearrange("p (s f) -> p s f", s=2)
            if qa == 0 and qb == Tt:
                mm(o3[:, :, qa:qb], v_sb[0:nk, K["vt"], h * 128:(h + 1) * 128], Pt[pi][0:nk, :, qa:qb], first, last,
                   [("v", K["vt"]), ("P", pi)], btok(bo), skip_group_check=True)
                mm(l3[:, :, qa:qb], ones[0:nk, :], Pt[pi][0:nk, :, qa:qb], first, last,
                   [("ones",), ("P", pi)], btok(bl), skip_group_check=True)
            else:
                assert not first
                for s_ in range(2):
                    mm(o3[:, s_, qa:qb], v_sb[0:nk, K["vt"], h * 128:(h + 1) * 128], Pt[pi][0:nk, s_, qa:qb], False,
                       last and s_ == 1, [("v", K["vt"]), ("P", pi)], btok(bo), skip_group_check=True)
                for s_ in range(2):
                    mm(l3[:, s_, qa:qb], ones[0:nk, :], Pt[pi][0:nk, s_, qa:qb], False, last and s_ == 1,
                       [("ones",), ("P", pi)], btok(bl), skip_group_check=True)
            if last:
                combine_a(h, bo, bl)
                state['cb_at'] = u + DEFER

        pending = []

        def combine_a(h, bo, bl):
            flush_pending()
            o3 = banks[bo][:].rearrange("p (s f) -> p s f", s=2)
            l3 = banks[bl][:].rearrange("p (s f) -> p s f", s=2)
            copy_op("dve", rl[:, :, 0:Tt], l3[:, :, 0:Tt], btok(bl), [("rl", 0), ("rl", 1)])
            bank_free(bl)
            tt_op("dve", ob[:, 0:Tt], o3[:, 0, 0:Tt], rl[:, 1, 0:Tt], ALU.mult, btok(bo) + [("rl", 1)], [("ob",)])
            tt_op("dve", tt[:, 0:Tt], o3[:, 1, 0:Tt], rl[:, 0, 0:Tt], ALU.mult, btok(bo) + [("rl", 0)], [("tt",)])
            bank_free(bo)
            stt_op("dve", ob[:, 0:Tt], tt[:, 0:Tt], neglam[:, 0:1], ob[:, 0:Tt], ALU.mult, ALU.add,
                   [("tt",), ("ob",), ("neglam",)], [("ob",)])
            tt_op("dve", sq[:, 0:Tt], ob[:, 0:Tt], ob[:, 0:Tt], ALU.mult, [("ob",)], [("sq",)])
            tt_op("dve", rs[:, 0:Tt], rl[:, 0, 0:Tt], rl[:, 1, 0:Tt], ALU.mult, [("rl", 0), ("rl", 1)], [("rs",)])
            stt_op("dve", rs[:, 0:Tt], rs[:, 0:Tt], EPS, rs[:, 0:Tt], ALU.mult, ALU.mult, [("rs",)], [("rs",)])
            pending.append(h)

        pending2 = []

        def combine_b(h):
            bm = bank_alloc()
            mm(banks[bm][:, 0:Tt], ones[:, :], sq[:, 0:Tt], True, True, [("ones",), ("sq",)], btok(bm))
            stt_op("dve", rs[:, 0:Tt], banks[bm][:, 0:Tt], 1.0 / 128.0, rs[:, 0:Tt], ALU.mult, ALU.add,
                   btok(bm) + [("rs",)], [("rs",)])
            bank_free(bm)
            pending2.append(h)

        def combine_c_act(h):
            act(tt[:, 0:Tt], rs[:, 0:Tt], AF.Ln, [("rs",)], [("tt",)])
            act(rs[:, 0:Tt], tt[:, 0:Tt], AF.Exp, [("tt",)], [("rs",)], scale=-0.5)

        def combine_c_dve(h):
            stt_op("dve", tt[:, 0:Tt], ob[:, 0:Tt], gsc[:, 0:1], rs[:, 0:Tt], ALU.mult, ALU.mult,
                   [("ob",), ("gsc",), ("rs",)], [("tt",)])
            tt_op("dve", mrgs[mb][:, h, 0:Tt], tt[:, 0:Tt], sza[:, h, 0:Tt], ALU.mult, [("tt",), ("sza", h)],
                  [("mrg", mb, h)])

        def combine_c(h):
            combine_c_act(h)
            combine_c_dve(h)

        def flush_b():
            while pending:
                combine_b(pending.pop(0))

        def flush_c():
            while pending2:
                combine_c(pending2.pop(0))

        def flush_pending():
            flush_b()
            flush_c()

        AHEAD = ahead
        DEFER = 6
        DEFER2 = 2
        n = len(units)
        for u in range(min(AHEAD, n)):
            emit_S(u)
        for u in range(n):
            if u + AHEAD < n:
                emit_S(u + AHEAD)
            emit_PV(u)
            if pending2 and u >= state.get('cc_at', 0):
                flush_c()
            if pending and u >= state.get('cb_at', 0):
                flush_b()
                state['cc_at'] = u + DEFER2
            if uhooks and u in uhooks:
                uhooks.pop(u)()
            if hooks and (u + 1) % len(keytiles) == 0 and ((u + 1) // len(keytiles) - 1) in hooks:
                hooks[(u + 1) // len(keytiles) - 1]()
        if uhooks:
            for k_ in sorted(uhooks):
                uhooks[k_]()

        class Fin:
            def __call__(self):
                flush_pending()

            def b(self):
                flush_c()
                flush_b()

            def c_act(self):
                self.hs = list(pending2)
                for h in self.hs:
                    combine_c_act(h)

            def c_dve(self):
                for h in self.hs:
                    combine_c_dve(h)
                del pending2[:]
        return Fin()

    def final_load(x_src, r, b=None):
        if b is None:
            b = next_big()
        dma(big[b][:r, :], x_src, [], [("big", b)])
        return b

    def final_pe(r, row0, mb, b):
        for half in range(2):
            bk = proj_alloc()
            ps = banks[bk][0:r, :]
            for kc in range(8):
                mm(ps, mrgs[mb][:, kc, row0:row0 + r], w_out_sb[:, kc, half * 512:(half + 1) * 512], kc == 0, kc == 7,
                   [("mrg", mb, kc), ("w_out", kc, half)], btok(bk))
            tt_op("dve", big[b][:r, half * 512:(half + 1) * 512], ps, big[b][:r, half * 512:(half + 1) * 512], ALU.add,
                  btok(bk) + [("big", b)], [("big", b)])
            bank_free(bk)

    def final_post1(r, xi, b):
        c0 = 8 + 4 * xi
        P.add("dve", lambda e, o=xns[xi][:r, :], i=big[b][:r, :], a=stat[:r, c0:c0 + 1]: e.scalar_tensor_tensor(
            out=o, in0=i, scalar=1.0, in1=i, op0=ALU.mult, op1=ALU.mult, accum_out=a),
            [("big", b)], [("xn", xi), ("fstat", xi, 0)])

    def final_post2(r, xi, b):
        c0 = 8 + 4 * xi
        act(stat[:r, c0 + 1:c0 + 2], stat[:r, c0:c0 + 1], AF.Ln, [("fstat", xi, 0), ("eps",)], [("fstat", xi, 1)],
            scale=1.0 / D, bias=epsc[:r, 0:1])
        act(stat[:r, c0 + 2:c0 + 3], stat[:r, c0 + 1:c0 + 2], AF.Exp, [("fstat", xi, 1)], [("fstat", xi, 2)], scale=-0.5)

    def final_post3(y_dst, r, xi, b):
        c0 = 8 + 4 * xi
        stt_op("dve", big[b][:r, :], big[b][:r, :], stat[:r, c0 + 2:c0 + 3], fg[:r, :], ALU.mult, ALU.mult,
               [("big", b), ("fstat", xi, 2), ("fg",)], [("big", b)])
        dma(y_dst, big[b][:r, :], [("big", b)], [], is_out=True)

    def final_post(y_dst, r, xi, b):
        c0 = 8 + 4 * xi
        act(xns[xi][:r, :], big[b][:r, :], AF.Square, [("big", b)], [("xn", xi), ("fstat", xi, 0)],
            accum_out=stat[:r, c0:c0 + 1])
        act(stat[:r, c0 + 1:c0 + 2], stat[:r, c0:c0 + 1], AF.Ln, [("fstat", xi, 0), ("eps",)], [("fstat", xi, 1)],
            scale=1.0 / D, bias=epsc[:r, 0:1])
        act(stat[:r, c0 + 2:c0 + 3], stat[:r, c0 + 1:c0 + 2], AF.Exp, [("fstat", xi, 1)], [("fstat", xi, 2)], scale=-0.5)
        stt_op("dve", big[b][:r, :], big[b][:r, :], stat[:r, c0 + 2:c0 + 3], fg[:r, :], ALU.mult, ALU.mult,
               [("big", b), ("fstat", xi, 2), ("fg",)], [("big", b)])
        dma(y_dst, big[b][:r, :], [("big", b)], [], is_out=True)

    def final_subtile(x_src, y_dst, r, row0, mb=0, xi=0, b=None):
        if b is None:
            b = final_load(x_src, r)
        final_pe(r, row0, mb, b)
        final_post(y_dst, r, xi, b)

    def load_w_out():
        for kc in range(8):
            for half in range(2):
                kb = next_kv()
                dma(kvst[kb][:], w_out[kc * 128:(kc + 1) * 128, half * 512:(half + 1) * 512], [], [("kvst", kb)])
                copy_op("act" if half == 0 else "dve", w_out_sb[:, kc, half * 512:(half + 1) * 512], kvst[kb][:],
                        [("kvst", kb)], [("w_out", kc, half)])

    def kT_tok_fn(blk):
        return lambda h: [("kT", h, blk)]

    def finish():
        P.emit(nc, es)
        es.close()
        return nc

    if STAGE == 0:
        return finish()
    def sample_main():
        for jj in range(2):
            dma(cu[:, :, jj], sconv[jj, :].rearrange("(cc p) -> p cc", p=128), [], [("cu", cc) for cc in range(4)])
        norm_subtile(x_s[:, :], DEC, 0, DEC)
        project_tile(DEC, [(0, DEC)], NCACHE, ntile_c, 0, True, True, k_s, v_s, khs=[kT_all(h) for h in range(4)])
        for jj in range(2):
            dma(conv_s[jj, :].rearrange("(cc p) -> p cc", p=128), cu[:, :, DEC + jj],
                [("cu", cc) for cc in range(4)], [], is_out=True)
        kts = []
        for t in range(ntile_c):
            r = min(128, NCACHE - t * 128)
            if t == 7:
                near = (0, DEC, 144)
            elif t == 8:
                near = (0, DEC, 16)
            else:
                near = None
            kts.append(dict(nk=r, kc0=t * 128, ktoks=kT_all, vt=t, qa=0, qb=DEC, near=near))
        kts.append(dict(nk=DEC, kc0=NCACHE, ktoks=kT_all, vt=ntile_c, qa=0, qb=DEC, near=(0, DEC, 0)))
        attention(DEC, kts, ahead=3)()
        final_subtile(x_s[:, :], y_s[:, :], DEC, 0)

    if STAGE is None or STAGE >= 2:
        sample_main()
    norm_subtile(meta[:, :], NMETA, 0, NMETA)
    if STAGE == 0.5:
        return finish()
    project_tile(NMETA, [(0, NMETA)], 0, 0, 0, False, False, k_p, v_p, khs=[[("kT", h, "m")] for h in range(4)])
    if STAGE in (0.6, 0.7, 0.8):
        return finish()
    conv_halo(NMETA)

    if STAGE == 1:
        return finish()
    ntl = NT if N_PROMPT_TILES is None else N_PROMPT_TILES

    def tile_norm_a(j):
        for si in range(2):
            norm_a(x_p[j * T + si * 128:j * T + (si + 1) * 128, :], 128, si)

    def tile_norm_b(j):
        for si in range(2):
            norm_b(128, si * 128, si)

    def tile_project(j, mid=None, mid2=None, ph=None, ktr_out=None):
        t0 = j * T
        project_tile(T, [(0, 128), (128, 128)], NMETA + t0, 1 + 2 * j, NMETA + t0, True, True, k_p, v_p,
                     khs=[[("kT", h, 2 * j), ("kT", h, 2 * j + 1)] for h in range(4)], mb=j % 2, mid=mid, mid2=mid2, ph=ph, ktr_out=ktr_out)
        if j == NT - 1:
            for jj in range(2):
                dma(conv_p[jj, :].rearrange("(cc p) -> p cc", p=128), cu[:, :, T + jj],
                    [("cu", cc) for cc in range(4)], [], is_out=True)
        conv_halo(T)

    def tile_keytiles(j):
        kts = []
        mnear = (0, 240, 16) if j == 0 else None
        kts.append(dict(nk=NMETA, kc0=0, ktoks=kT_tok_fn("m"), vt=0, qa=0, qb=T, near=mnear))
        for kt in range(2 * j + 2):
            i = kt - 2 * j
            if i == 1:
                qa, near = 128, (128, 256, 0)
            elif i == 0:
                qa, near = 0, (0, 256, 0)
            elif i == -1:
                qa, near = 0, (0, 128, 128)
            else:
                qa, near = 0, None
            kts.append(dict(nk=128, kc0=NMETA + kt * 128, ktoks=kT_tok_fn(kt), vt=1 + kt, qa=qa, qb=T, near=near))
        return kts

    NB = 2

    def xrows(jj, si):
        return x_p[jj * T + si * 128:jj * T + (si + 1) * 128, :]

    def norm_ph(jj, ph):
        if jj >= ntl:
            return
        ph.setdefault(0, []).append(lambda: norm_load(xrows(jj, 0), 128, b=NB))
        ph.setdefault(1, []).insert(1 if ph.get(1) else 0, lambda: norm_a_act(128, 0, NB))
        ph.setdefault(3, []).insert(0, lambda: norm_a_dve(128, 0, NB))
        ph.setdefault(3, []).insert(1, lambda: norm_load(xrows(jj, 1), 128, b=NB))
        ph.setdefault(7, []).append(lambda: norm_a_act(128, 1, NB))
        ph.setdefault(7, []).append(lambda: norm_a_dve(128, 1, NB))

    tile_norm_a(0)
    tile_norm_b(0)
    ph = {}
    norm_ph(1, ph)
    tile_project(0, ph=ph)
    post = []
    for j in range(ntl):
        t0 = j * T
        nxt = j + 1 < ntl
        uhooks = {}
        if nxt:
            uhooks[0] = (lambda jj=j + 1: tile_norm_b(jj))
        for i_, pf in enumerate(post):
            uhooks[1 + i_] = pf
        post = []
        fin = attention(T, tile_keytiles(j), mb=j % 2, hooks=None, uhooks=uhooks)
        fb = [final_load(xrows(j, si), 128, b=si) for si in range(2)]
        if nxt:
            ph = {1: [fin.b, fin.c_act], 3: [fin.c_dve]}
            norm_ph(j + 2, ph)
            ktr = []
            tile_project(j + 1, ph=ph, ktr_out=ktr)
        else:
            ktr = []
            fin()
        for si in range(2):
            final_pe(128, si * 128, j % 2, fb[si])
        for f_ in ktr:
            f_()
        yd = lambda si, t0=t0: y_p[t0 + si * 128:t0 + (si + 1) * 128, :]
        post = [lambda b=fb[0]: final_post1(128, 0, b),
                lambda b=fb[1]: final_post1(128, 1, b),
                lambda b=fb[0]: final_post2(128, 0, b),
                lambda b=fb[1]: final_post2(128, 1, b),
                lambda b=fb[0], yd=yd: final_post3(yd(0), 128, 0, b),
                lambda b=fb[1], yd=yd: final_post3(yd(1), 128, 1, b)]
    for pf in post:
        pf()

    return finish()


_NC_CACHE = {}


def kernel(x_prompt, x_sample, cache_k, cache_v, state_conv, meta_tokens, rel_bias, norm_g, w_in,
           conv_w, lambda_q1, lambda_k1, lambda_q2, lambda_k2, subln_g, w_out, final_g):
    f = lambda a: np.ascontiguousarray(np.asarray(a, dtype=np.float32))
    if "nc" not in _NC_CACHE:
        _NC_CACHE["nc"] = build_nc()
    nc = _NC_CACHE["nc"]
    x_prompt = f(x_prompt); x_sample = f(x_sample)
    cache_k = f(cache_k); cache_v = f(cache_v); state_conv = f(state_conv)
    shared = {
        "meta": f(meta_tokens), "relb": f(rel_bias), "norm_g": f(norm_g).reshape(D),
        "w_in": f(w_in).reshape(D, 4096), "conv_w": f(conv_w).reshape(3, 512),
        "lq1": f(lambda_q1).reshape(64), "lk1": f(lambda_k1).reshape(64),
        "lq2": f(lambda_q2).reshape(64), "lk2": f(lambda_k2).reshape(64),
        "subln_g": f(subln_g).reshape(128), "w_out": f(w_out).reshape(D, D), "final_g": f(final_g).reshape(D),
    }
    in_maps = []
    for c in range(N_CORES):
        m = dict(shared)
        m["x_p"] = x_prompt[c]
        m["x_s"] = x_sample[c]
        m["ck"] = cache_k[0, c].reshape(NCACHE, 512)
        m["cv"] = cache_v[0, c].reshape(NCACHE, 512)
        m["sconv"] = state_conv[0, c]
        in_maps.append(m)
    res = run_bass_kernel_spmd(nc, in_maps, core_ids=list(range(N_CORES)))
    R = res.results
    st = lambda k: np.stack([np.asarray(R[c][k], dtype=np.float32) for c in range(N_CORES)])
    y_prompt = st("y_p")
    y_sample = st("y_s")
    k_prompt = st("k_p").reshape(1, N_CORES, NMETA + SEQ, 4, 128)
    v_prompt = st("v_p").reshape(1, N_CORES, NMETA + SEQ, 4, 128)
    conv_prompt = st("conv_p").reshape(1, N_CORES, 2, 512)
    k_sample = st("k_s").reshape(1, N_CORES, DEC, 4, 128)
    v_sample = st("v_s").reshape(1, N_CORES, DEC, 4, 128)
    conv_sample = st("conv_s").reshape(1, N_CORES, 2, 512)
    return (y_prompt, y_sample, k_prompt, v_prompt, conv_prompt, k_sample, v_sample, conv_sample)
```

```python
import numpy as np
from contextlib import ExitStack
import concourse.bass as bass
import concourse.mybir as mybir
from concourse.bass_utils import run_bass_kernel_spmd

F32 = mybir.dt.float32
BF16 = mybir.dt.bfloat16
ALU = mybir.AluOpType
AF = mybir.ActivationFunctionType
AX = mybir.AxisListType

N_CORES = 8
SEQ = 4096
D = 1024
T = 256
NT = SEQ // T
NMETA = 16
PAST = 1024
NCACHE = NMETA + PAST
DEC = 32
EPS = 1e-6
LAM_INIT = 0.2
MASKV = -30000.0
N_DMA_SEMS = 40
STAGE = None
N_PROMPT_TILES = None
SETUP_MASK = 0xFF
FLAG_ALL = False
MERGE_EXP = False

BIAS_STEPS = [(-90, 14), (-63, 13), (-45, 12), (-31, 11), (-22, 10), (-15, 9), (-11, 8),
              (-7, 7), (-6, 6), (-5, 5), (-4, 4), (-3, 3), (-2, 2), (-1, 1), (0, 0),
              (1, 17), (2, 18), (3, 19), (4, 20), (5, 21), (6, 22), (7, 23), (8, 24),
              (12, 25), (16, 26), (23, 27), (32, 28), (46, 29)]


class Op:
    __slots__ = ("eng", "fn", "reads", "writes", "dma", "deps", "flag", "ev", "is_out")

    def __init__(self, eng, fn, reads, writes, dma, is_out):
        self.eng = eng
        self.fn = fn
        self.reads = reads
        self.writes = writes
        self.dma = dma
        self.deps = ()
        self.flag = False
        self.ev = None
        self.is_out = is_out


class Prog:
    ENGS = ("pe", "act", "dve", "pool", "sp")

    def __init__(self):
        self.ops = []
        self.group = 0xFF

    def add(self, eng, fn, reads=(), writes=(), dma=False, is_out=False):
        if not (SETUP_MASK & self.group):
            return
        writes = tuple(writes) + tuple(t for t in reads if t[0] == "bank" and t not in writes)
        self.ops.append(Op(eng, fn, tuple(reads), tuple(writes), dma, is_out))

    def analyze(self):
        last_w = {}
        readers = {}
        ops = self.ops
        for i, op in enumerate(ops):
            raw = set()
            other = set()
            for t in op.reads:
                w = last_w.get(t)
                if w is not None:
                    raw.add(w)
            for t in op.writes:
                w = last_w.get(t)
                if w is not None:
                    other.add(w)
                for r in readers.get(t, ()):
                    other.add(r)
            deps = set()
            for d in raw | other:
                if d == i:
                    continue
                dop = ops[d]
                if (not dop.dma) and (not op.dma) and dop.eng == op.eng:
                    if op.eng == "pe":
                        continue
                    if d not in raw:
                        continue
                deps.add(d)
            op.deps = tuple(sorted(deps))
            for d in deps:
                ops[d].flag = True
            if FLAG_ALL and not op.dma:
                op.flag = True
            for t in op.writes:
                last_w[t] = i
                readers[t] = []
            for t in op.reads:
                readers.setdefault(t, []).append(i)

    def emit(self, nc, es):
        ops = self.ops
        self.analyze()
        last_ops = []
        for e in ("pe", "act", "dve", "pool"):
            lst = [op for op in ops if op.eng == e and not op.dma]
            if lst:
                lst[-1].flag = True
                last_ops.append(lst[-1])
        sems = {e: es.enter_context(nc.semaphore("s_" + e)) for e in ("pe", "act", "dve", "pool")}
        dsems = [es.enter_context(nc.semaphore("d%d" % i)) for i in range(N_DMA_SEMS)]
        cnt = {e: 0 for e in sems}
        dcnt = [0] * N_DMA_SEMS
        dlast = [None] * N_DMA_SEMS
        nd = 0
        for i, op in enumerate(ops):
            if op.dma:
                s = nd % N_DMA_SEMS
                nd += 1
                if dlast[s] is not None:
                    op.deps = tuple(sorted(set(op.deps) | {dlast[s]}))
                dcnt[s] += 16
                op.ev = (dsems[s], dcnt[s], ("d", s))
                dlast[s] = i
                op.flag = True
            elif op.flag:
                cnt[op.eng] += 1
                op.ev = (sems[op.eng], cnt[op.eng], op.eng)
        block = es.enter_context(nc.Block())
        per_eng = {e: [op for op in ops if op.eng == e] for e in self.ENGS}
        out_events = [op.ev for op in ops if op.is_out] + [op.ev for op in last_ops]

        import os
        dump = open(os.environ["KDUMP"], "w") if os.environ.get("KDUMP") else None

        def run(eng_obj, lst, final_events=()):
            waited = {}
            for op in lst:
                if dump:
                    dump.write("%s #%d deps=%s flag=%s ev=%s r=%s w=%s\n" % (
                        op.eng, ops.index(op), [(ops[d].ev[2], ops[d].ev[1]) for d in op.deps], op.flag,
                        (op.ev[2], op.ev[1]) if op.ev else None, op.reads[:4], op.writes[:4]))
                need = {}
                for d in op.deps:
                    sem, val, key = ops[d].ev
                    if waited.get(key, 0) >= val:
                        continue
                    if key not in need or need[key][1] < val:
                        need[key] = (sem, val)
                for key, (sem, val) in need.items():
                    eng_obj.wait_ge(sem, val)
                    waited[key] = val
                ins = op.fn(eng_obj)
                if op.flag:
                    ins.then_inc(op.ev[0], 16 if op.dma else 1)
            need = {}
            for (sem, val, key) in final_events:
                if waited.get(key, 0) >= val:
                    continue
                if key not in need or need[key][1] < val:
                    need[key] = (sem, val)
            for key, (sem, val) in need.items():
                eng_obj.wait_ge(sem, val)

        @block.sync
        def _(e):
            run(e, per_eng["sp"], out_events)

        @block.tensor
        def _(e):
            run(e, per_eng["pe"])

        @block.scalar
        def _(e):
            run(e, per_eng["act"])

        @block.vector
        def _(e):
            run(e, per_eng["dve"])

        @block.gpsimd
        def _(e):
            run(e, per_eng["pool"])


def build_nc():
    nc = bass.Bass("TRN2", target_bir_lowering=False)
    es = ExitStack()
    P = Prog()

    def din(name, shape):
        return nc.dram_tensor(name, shape, F32, kind="ExternalInput").ap()

    def dout(name, shape):
        return nc.dram_tensor(name, shape, F32, kind="ExternalOutput").ap()

    x_p = din("x_p", [SEQ, D])
    x_s = din("x_s", [DEC, D])
    ck = din("ck", [NCACHE, 512])
    cv = din("cv", [NCACHE, 512])
    sconv = din("sconv", [2, 512])
    meta = din("meta", [NMETA, D])
    relb = din("relb", [32, 4])
    norm_g = din("norm_g", [D])
    w_in = din("w_in", [D, 4096])
    conv_w = din("conv_w", [3, 512])
    lam_in = [din(n, [64]) for n in ("lq1", "lk1", "lq2", "lk2")]
    subln_g = din("subln_g", [128])
    w_out = din("w_out", [D, D])
    final_g = din("final_g", [D])
    y_p = dout("y_p", [SEQ, D])
    y_s = dout("y_s", [DEC, D])
    k_p = dout("k_p", [NMETA + SEQ, 512])
    v_p = dout("v_p", [NMETA + SEQ, 512])
    conv_p = dout("conv_p", [2, 512])
    k_s = dout("k_s", [DEC, 512])
    v_s = dout("v_s", [DEC, 512])
    conv_s = dout("conv_s", [2, 512])

    def sb(name, shape, dt=F32):
        return es.enter_context(nc.sbuf_tensor(name, shape, dt))

    w_in_sb = sb("w_in_sb", [128, 8, 4096], BF16)
    w_out_sb = sb("w_out_sb", [128, 8, D], BF16)
    kT_sb = sb("kT_sb", [128, 4, NMETA + SEQ], BF16)
    v_sb = sb("v_sb", [128, 33, 512], BF16)
    ident = sb("ident", [128, 128], BF16)
    ones = sb("ones", [128, 128], BF16)
    epsc = sb("epsc", [128, 1])
    g_sb = sb("g_sb", [128, 8])
    fg = sb("fg", [128, D])
    cw = sb("cw", [128, 3, 4])
    sg = sb("sg", [128, 1])
    gsc = sb("gsc", [128, 1])
    lsum = sb("lsum", [128, 2])
    lexp = sb("lexp", [128, 2])
    ldif = sb("ldif", [128, 1])
    neglam = sb("neglam", [128, 1])
    rbf = sb("rbf", [128, 4])
    B0 = [sb("B0_%d" % h, [128, 256], BF16) for h in range(4)]
    NBIG = 3
    big = [sb("big%d" % i, [128, D]) for i in range(NBIG)]
    xns = [sb("xn%d" % i, [128, D], BF16) for i in range(2)]
    xn = xns[0]
    nT = sb("nT", [128, 8, T], BF16)
    qz = sb("qz", [128, 4, 2, T], BF16)
    sza = sb("sza", [128, 4, T], BF16)
    mrgs = [sb("mrg%d" % i, [128, 8, T], BF16) for i in range(2)]
    NKV = 2
    kvst = [sb("kvst%d" % i, [128, 512]) for i in range(NKV)]
    cu = sb("cu", [128, 4, T + 2])
    cg = sb("cg", [128, T])
    szc = sb("szc", [128, T])
    ct1 = sb("ct1", [128, T])
    NP = 4
    N_SS = 4
    Pt_all = sb("Pt_all", [128, N_SS, 2 * T], BF16)
    Pt = [Pt_all[:, i, :].rearrange("p (s f) -> p s f", s=2) for i in range(N_SS)]
    rl = sb("rl", [128, 2, T])
    ob = sb("ob", [128, T])
    sq = sb("sq", [128, T], BF16)
    rs = sb("rs", [128, T])
    tt = sb("tt", [128, T])
    stat = sb("stat", [128, 16])
    junk = xn
    ct2 = sb("ct2", [128, T])
    rb_sb = ct2[:, 0:128]
    dl = ob[:, 0:len(BIAS_STEPS) * 4].rearrange("p (a b) -> p a b", b=4)
    sza_f = sza[:].rearrange("p a b -> p (a b)").bitcast(F32)
    lam4 = sza_f[:, 0:256].rearrange("p (a b) -> p a b", a=4)
    lprod = sza_f[:, 256:384].rearrange("p (a b) -> p a b", a=2)
    Rm = cg
    stepm = [szc, ct1]
    stepm_tok = [("szc",), ("ct1",)]
    ps_all = es.enter_context(nc.psum_tensor("ps_all", [128, 4096], F32))
    banks = [ps_all[:, i * 512:(i + 1) * 512] for i in range(8)]

    N_SSLOT = 4
    free_banks = list(range(N_SSLOT, 8))

    def bank_alloc():
        return free_banks.pop(0)

    def bank_free(b):
        if b in proj_taken:
            proj_taken.discard(b)
            return
        free_banks.append(b)

    proj_rr = [0]
    proj_order = [4, 5, 6, 7, 0, 1, 2, 3]
    proj_taken = set()

    def proj_alloc():
        b = proj_order[proj_rr[0] % 8]
        proj_rr[0] += 1
        proj_taken.add(b)
        return b

    def btok(b):
        return [("bank", b, 0), ("bank", b, 1)]

    rr = {"big": 0, "kv": 0, "P": 0}

    def next_big():
        i = rr["big"] % NBIG
        rr["big"] += 1
        return i

    def next_kv():
        i = rr["kv"] % NKV
        rr["kv"] += 1
        return i

    def next_P():
        i = rr["P"] % NP
        rr["P"] += 1
        return i

    def dma(out, in_, reads, writes, is_out=False, eng="sp"):
        P.add(eng, lambda e, o=out, i=in_: e.dma_start(out=o, in_=i, allow_slow_non_contiguous=True),
              reads, writes, dma=True, is_out=is_out)

    def act(out, in_, func, reads, writes, **kw):
        P.add("act", lambda e, o=out, i=in_, f=func, k=kw: e.activation(out=o, in_=i, func=f, **k),
              reads, writes)

    def tt_op(eng, out, in0, in1, op, reads, writes):
        P.add(eng, lambda e, o=out, a=in0, b=in1, p=op: e.tensor_tensor(out=o, in0=a, in1=b, op=p),
              reads, writes)

    def ts_op(eng, out, in0, s1, s2, op0, op1, reads, writes):
        if s2 is None:
            P.add(eng, lambda e, o=out, a=in0, x=s1, p=op0: e.tensor_scalar(out=o, in0=a, scalar1=x, scalar2=None, op0=p),
                  reads, writes)
        else:
            P.add(eng, lambda e, o=out, a=in0, x=s1, y=s2, p=op0, q=op1: e.tensor_scalar(
                out=o, in0=a, scalar1=x, scalar2=y, op0=p, op1=q), reads, writes)

    def stt_op(eng, out, in0, scalar, in1, op0, op1, reads, writes):
        P.add(eng, lambda e, o=out, a=in0, s=scalar, b=in1, p=op0, q=op1: e.scalar_tensor_tensor(
            out=o, in0=a, scalar=s, in1=b, op0=p, op1=q), reads, writes)

    def copy_op(eng, out, in_, reads, writes):
        if eng == "act":
            P.add("act", lambda e, o=out, i=in_: e.activation(out=o, in_=i, func=AF.Identity), reads, writes)
        else:
            P.add(eng, lambda e, o=out, i=in_: e.tensor_copy(out=o, in_=i), reads, writes)

    def recip(out, in_, reads, writes):
        P.add("dve", lambda e, o=out, i=in_: e.reciprocal(out=o, in_=i), reads, writes)

    def memset(eng, ap, val, writes):
        P.add(eng, lambda e, a=ap, v=val: e.memset(a, v), (), writes)

    def mm(out, lhsT, rhs, start, stop, reads, writes, **kw):
        P.add("pe", lambda e, o=out, l=lhsT, r=rhs, s=start, t=stop, k=kw: e.matmul(
            o, lhsT=l, rhs=r, start=s, stop=t, **k), reads, writes)

    def transpose(out, in_, idn, reads, writes):
        P.add("pe", lambda e, o=out, i=in_, d=idn: e.transpose(o, i, d), reads, writes)

    P.group = 1
    memset("pool", ident[:], 0.0, [("ident",)])
    P.add("pool", lambda e: e.affine_select(out=ident[:], in_=ident[:], compare_op=ALU.not_equal, fill=1.0,
                                            base=0, pattern=[[-1, 128]], channel_multiplier=1),
          [("ident",)], [("ident",)])
    memset("pool", ones[:], 1.0, [("ones",)])
    memset("pool", epsc[:], EPS, [("eps",)])
    memset("pool", qz[:], 0.0, [("qz", h, s_) for h in range(4) for s_ in range(2)])

    dma(g_sb[:], norm_g.rearrange("(kc p) -> p kc", p=128), [], [("g_sb",)])
    P.group = 2
    dma(rb_sb, relb.rearrange("b h -> (b h)").partition_broadcast(128), [], [("ct2",)])
    copy_op("dve", rbf[:], rb_sb[:, 60:64], [("ct2",)], [("rb",)])
    for i in range(4):
        dma(lam4[:, i, :], lam_in[i].partition_broadcast(128), [], [("sza", i)])
    dma(sg[:], subln_g.rearrange("(p o) -> p o", o=1), [], [("sg",)])
    for jj in range(3):
        dma(cw[:, jj, :], conv_w[jj, :].rearrange("(cc p) -> p cc", p=128), [], [("cw", jj)])
    dma(fg[:], final_g.partition_broadcast(128), [], [("fg",)])
    P.group = 4

    kT_all = lambda h: [("kT", h, b) for b in ["m"] + list(range(2 * NT))]
    ntile_c = (NCACHE + 127) // 128
    def cache_tile(t):
        r = min(128, NCACHE - t * 128)
        kb = next_kv()
        xi_ = t % 2
        dma(kvst[kb][0:r, :], ck[t * 128:t * 128 + r, :], [], [("kvst", kb)])
        copy_op("dve", xns[xi_][0:r, 0:512], kvst[kb][0:r, :], [("kvst", kb)], [("xn", xi_)])
        bk = bank_alloc()
        psb = banks[bk][:].bitcast(BF16)
        for h in range(4):
            transpose(psb[:, h * 128:h * 128 + r], xns[xi_][0:r, h * 128:(h + 1) * 128], ident[:r, :r],
                      [("xn", xi_), ("ident",)], btok(bk))
        copy_op("act", kT_sb[:, :, t * 128:t * 128 + r], psb[:, 0:512].rearrange("p (h t) -> p h t", h=4)[:, :, 0:r],
                btok(bk), [tk for h in range(4) for tk in kT_all(h)])
        bank_free(bk)
        kb = next_kv()
        dma(kvst[kb][0:r, :], cv[t * 128:t * 128 + r, :], [], [("kvst", kb)])
        copy_op("dve", v_sb[0:r, t, :], kvst[kb][0:r, :], [("kvst", kb)], [("v", t)])

    cache_thunks = [(lambda t=t: cache_tile(t)) for t in range(ntile_c)]
    piece_no = [0]

    def cache_step():
        piece_no[0] += 1
        if piece_no[0] % 4 == 0 and cache_thunks:
            cache_thunks.pop(0)()

    b0_thunks = []
    b0_acc = [(rl[:, 0, :], ("rl", 0)), (rl[:, 1, :], ("rl", 1)), (tt[:], ("tt",)), (rs[:], ("rs",))]

    def _b0_plan():
        b0_thunks.append(lambda: P.add("pool", lambda e: e.iota(Rm[:], pattern=[[-1, 256]], base=0, channel_multiplier=1,
                                                               allow_small_or_imprecise_dtypes=True), [], [("cg",)]))
        prev_b = 15
        for i, (thr, bk) in enumerate(BIAS_STEPS):
            b0_thunks.append(lambda i=i, bk=bk, pb=prev_b: tt_op(
                "dve", dl[:, i, :], rb_sb[:, bk * 4:bk * 4 + 4], rb_sb[:, pb * 4:pb * 4 + 4], ALU.subtract,
                [("ct2",)], [("ob",)]))
            prev_b = bk
        for h in range(4):
            b0_thunks.append(lambda h=h: memset("dve", b0_acc[h][0], 0.0, [b0_acc[h][1]]))
        for i, (thr, bk) in enumerate(BIAS_STEPS):
            sm = stepm[i % 2]
            b0_thunks.append(lambda i=i, sm=sm, thr=thr: P.add(
                "dve", lambda e, o=sm, t=float(thr): e.tensor_single_scalar(out=o[:], in_=Rm[:], scalar=t, op=ALU.is_ge),
                [("cg",)], [stepm_tok[i % 2]]))
            for h in range(4):
                b0_thunks.append(lambda i=i, sm=sm, h=h: stt_op(
                    "dve", b0_acc[h][0], sm[:], dl[:, i, h:h + 1], b0_acc[h][0], ALU.mult, ALU.add,
                    [stepm_tok[i % 2], ("ob",), b0_acc[h][1]], [b0_acc[h][1]]))
        for h in range(4):
            b0_thunks.append(lambda h=h: copy_op("dve", B0[h][:], b0_acc[h][0], [b0_acc[h][1]], [("B0", h)]))
        for h in range(4):
            b0_thunks.append(lambda h=h: memset("dve", B0[h][64:128, 0:64], MASKV, [("B0", h)]))

    _b0_plan()

    def b0_step(n):
        for _ in range(n):
            if b0_thunks:
                b0_thunks.pop(0)()

    stg = [(big[i][:], [("big", i)]) for i in range(NBIG)]
    for m_ in range(2):
        stg.append((mrgs[m_][:].rearrange("p a b -> p (a b)").bitcast(F32), [("mrg", m_, c) for c in range(8)]))
    stg.append((nT[:].rearrange("p a b -> p (a b)").bitcast(F32), [("nT",)]))
    stg_i = [0]

    def next_stg():
        i = stg_i[0] % len(stg)
        stg_i[0] += 1
        return stg[i]

    for q in (0, 2, 3, 1):
        for kc in range(8):
            buf, btk = next_stg()
            dma(buf, w_in[kc * 128:(kc + 1) * 128, q * 1024:(q + 1) * 1024], [], btk)
            act(w_in_sb[:, kc, q * 1024:(q + 1) * 1024], buf[:, :], AF.Copy,
                btk + [("g_sb",)], [("w_in", kc, q, 0), ("w_in", kc, q, 1)], scale=g_sb[:, kc:kc + 1])
            b0_step(4)
            cache_step()
    for kc in range(8):
        buf, btk = next_stg()
        dma(buf, w_out[kc * 128:(kc + 1) * 128, :], [], btk)
        copy_op("act", w_out_sb[:, kc, :], buf[:, :], btk, [("w_out", kc, 0), ("w_out", kc, 1)])
        b0_step(4)
        cache_step()
    b0_step(10000)
    while cache_thunks:
        cache_thunks.pop(0)()

    def w_in_tok(kc, c0, c1):
        toks = set()
        for c in range(c0 // 512, (c1 - 1) // 512 + 1):
            toks.add(("w_in", kc, c // 2, c % 2))
        return list(toks)

    P.group = 8
    tt_op("dve", lprod[:, 0, :], lam4[:, 0, :], lam4[:, 1, :], ALU.mult, [("sza", 0), ("sza", 1)], [("lprod", 0), ("sza", 2)])
    tt_op("dve", lprod[:, 1, :], lam4[:, 2, :], lam4[:, 3, :], ALU.mult, [("sza", 2), ("sza", 3)], [("lprod", 1), ("sza", 2)])
    P.add("dve", lambda e: e.reduce_sum(out=lsum[:], in_=lprod, axis=AX.X),
          [("lprod", 0), ("lprod", 1), ("sza", 2)], [("lsum",)])
    act(lexp[:], lsum[:], AF.Exp, [("lsum",)], [("lexp",)])
    tt_op("dve", ldif[:], lexp[:, 0:1], lexp[:, 1:2], ALU.subtract, [("lexp",)], [("ldif",)])
    ts_op("dve", neglam[:], ldif[:], -1.0, -LAM_INIT, ALU.mult, ALU.add, [("ldif",)], [("neglam",)])
    ts_op("dve", gsc[:], sg[:], 1.0 - LAM_INIT, None, ALU.mult, None, [("sg",)], [("gsc",)])

    P.group = 0xFF
    def norm_load(x_src, r, b=None):
        if b is None:
            b = next_big()
        dma(big[b][:r, :], x_src, [], [("big", b)])
        return b

    def norm_a(x_src, r, xi, b=None):
        if b is None:
            b = norm_load(x_src, r)
        act(xns[xi][:r, :], big[b][:r, :], AF.Square, [("big", b)], [("xn", xi), ("stat", xi, 0)],
            accum_out=stat[:r, 4 * xi:4 * xi + 1])
        act(stat[:r, 4 * xi + 1:4 * xi + 2], stat[:r, 4 * xi:4 * xi + 1], AF.Ln, [("stat", xi, 0), ("eps",)],
            [("stat", xi, 1)], scale=1.0 / D, bias=epsc[:r, 0:1])
        act(stat[:r, 4 * xi + 2:4 * xi + 3], stat[:r, 4 * xi + 1:4 * xi + 2], AF.Exp, [("stat", xi, 1)],
            [("stat", xi, 2)], scale=-0.5)
        ts_op("dve", xns[xi][:r, :], big[b][:r, :], stat[:r, 4 * xi + 2:4 * xi + 3], None, ALU.mult, None,
              [("big", b), ("stat", xi, 2)], [("xn", xi)])

    def norm_a_act(r, xi, b):
        P.add("dve", lambda e, o=xns[xi][:r, :], i=big[b][:r, :], a=stat[:r, 4 * xi:4 * xi + 1]: e.scalar_tensor_tensor(
            out=o, in0=i, scalar=1.0, in1=i, op0=ALU.mult, op1=ALU.mult, accum_out=a),
            [("big", b)], [("xn", xi), ("stat", xi, 0)])
        act(stat[:r, 4 * xi + 1:4 * xi + 2], stat[:r, 4 * xi:4 * xi + 1], AF.Ln, [("stat", xi, 0), ("eps",)],
            [("stat", xi, 1)], scale=1.0 / D, bias=epsc[:r, 0:1])
        act(stat[:r, 4 * xi + 2:4 * xi + 3], stat[:r, 4 * xi + 1:4 * xi + 2], AF.Exp, [("stat", xi, 1)],
            [("stat", xi, 2)], scale=-0.5)

    def norm_a_dve(r, xi, b):
        ts_op("dve", xns[xi][:r, :], big[b][:r, :], stat[:r, 4 * xi + 2:4 * xi + 3], None, ALU.mult, None,
              [("big", b), ("stat", xi, 2)], [("xn", xi)])

    def norm_b(r, row0, xi):
        bk = bank_alloc()
        psb = banks[bk][:].bitcast(BF16)
        for kc in range(8):
            transpose(psb[:, kc * 128:kc * 128 + r], xns[xi][:r, kc * 128:(kc + 1) * 128], ident[:r, :r],
                      [("xn", xi), ("ident",)], btok(bk))
        copy_op("dve", nT[:, :, row0:row0 + r], psb.rearrange("p (k t) -> p k t", k=8)[:, :, 0:r],
                btok(bk), [("nT",)])
        bank_free(bk)

    def norm_subtile(x_src, r, row0, Tt, xi=0):
        norm_a(x_src, r, xi)
        norm_b(r, row0, xi)

    def proj_feature(col0, Tt, evac):
        pass

    class HalfBanks:
        def __init__(self):
            self.cur = None
            self.used = 0

        def get(self):
            if self.cur is None or self.used == 2:
                if self.cur is not None:
                    bank_free(self.cur)
                self.cur = proj_alloc()
                self.used = 0
            h = self.used
            self.used += 1
            return self.cur, h

        def done(self):
            if self.cur is not None:
                bank_free(self.cur)
                self.cur = None
                self.used = 0

    def proj_fm(hb, col0, Tt):
        bk, half = hb.get()
        ps = banks[bk][:, half * 256:half * 256 + Tt]
        tok = btok(bk)
        for kc in range(8):
            mm(ps, w_in_sb[:, kc, col0:col0 + 128], nT[:, kc, 0:Tt], kc == 0, kc == 7,
               [("nT",)] + w_in_tok(kc, col0, col0 + 128), tok)
        return ps, tok

    def project_tile(Tt, subtiles, kcol0, vtile, krow0, want_q, want_conv_out, kdst, vdst, khs=None, mb=0, mid=None, mid2=None, ph=None, ktr_out=None):
        hb = HalfBanks()

        def run_ph(i):
            if ph and i in ph:
                for f_ in ph[i]:
                    f_()

        def pair(colA, evA, colB, evB):
            psA, tokA = proj_fm(hb, colA, Tt)
            psB, tokB = proj_fm(hb, colB, Tt)
            evA(psA, tokA)
            evB(psB, tokB)

        if mid is not None:
            mid()
        run_ph(0)
        if want_q:
            def ev_q(h):
                def f(ps, tok):
                    act(qz[0:64, h, 0, 0:Tt], ps[0:64, :], AF.Copy, [tok[0]], [("qz", h, 0)], scale=0.125)
                    P.add("dve", lambda e, o=qz[64:128, h, 1, 0:Tt], i=ps[64:128, :]: e.tensor_scalar(
                        out=o, in0=i, scalar1=0.125, scalar2=None, op0=ALU.mult), [tok[1]], [("qz", h, 1)])
                return f
            pair(0, ev_q(0), 128, ev_q(1))
            run_ph(0.5)
            pair(256, ev_q(2), 384, ev_q(3))
            run_ph(1)
        for cc in range(4):
            run_ph(2 + cc) if cc > 0 else None

            def ev_cg(ps, tok):
                copy_op("act", cg[:, 0:Tt], ps, tok, [("cg",)])

            def ev_u(ps, tok, cc=cc):
                tt_op("dve", cu[:, cc, 2:2 + Tt], ps, cg[:, 0:Tt], ALU.mult, tok + [("cg",)], [("cu", cc)])
                if want_conv_out:
                    ts_op("dve", ct1[:, 0:Tt], cu[:, cc, 0:Tt], cw[:, 0, cc:cc + 1], None, ALU.mult, None,
                          [("cu", cc), ("cw", 0), ("cw", 1), ("cw", 2)], [("ct1",)])
                    stt_op("dve", ct2[:, 0:Tt], cu[:, cc, 1:1 + Tt], cw[:, 1, cc:cc + 1], ct1[:, 0:Tt], ALU.mult, ALU.add,
                           [("cu", cc), ("cw", 0), ("cw", 1), ("cw", 2), ("ct1",)], [("ct2",)])
                    stt_op("dve", ct1[:, 0:Tt], cu[:, cc, 2:2 + Tt], cw[:, 2, cc:cc + 1], ct2[:, 0:Tt], ALU.mult, ALU.add,
                           [("cu", cc), ("cw", 0), ("cw", 1), ("cw", 2), ("ct2",)], [("ct1",)])
            pair(2560 + cc * 128, ev_cg, 3072 + cc * 128, ev_u)
            if want_conv_out:
                def ev_zc(ps, tok):
                    act(szc[:, 0:Tt], ps, AF.Silu, tok, [("szc",)])

                def ev_bg(ps, tok, cc=cc):
                    tt_op("dve", ct2[:, 0:Tt], ps, szc[:, 0:Tt], ALU.mult, tok + [("szc",)], [("ct2",)])
                    tt_op("dve", mrgs[mb][:, 4 + cc, 0:Tt], ct2[:, 0:Tt], ct1[:, 0:Tt], ALU.mult,
                          [("ct2",), ("ct1",)], [("mrg", mb, 4 + cc)])
                pair(3584 + cc * 128, ev_zc, 2048 + cc * 128, ev_bg)
        run_ph(6)
        if want_q:
            ev_za = lambda h: (lambda ps, tok: act(sza[:, h, 0:Tt], ps, AF.Silu, tok, [("sza", h)]))
            for h in (0, 2):
                pair(1536 + h * 128, ev_za(h), 1536 + (h + 1) * 128, ev_za(h + 1))
        hb.done()
        run_ph(7)
        if mid2 is not None:
            mid2()
        if STAGE == 0.7:
            return
        k_tr = []
        for si, (row0, r) in enumerate(subtiles):
            for which in ("v", "k"):
                c0 = 1024 if which == "v" else 512
                bk = proj_alloc()
                ps = banks[bk][0:r, :]
                for kc in range(8):
                    mm(ps, nT[:, kc, row0:row0 + r], w_in_sb[:, kc, c0:c0 + 512], kc == 0, kc == 7,
                       [("nT",)] + w_in_tok(kc, c0, c0 + 512), btok(bk))
                kb = next_kv()
                copy_op("act", kvst[kb][0:r, :], ps, btok(bk), [("kvst", kb)])
                if which == "v":
                    copy_op("dve", v_sb[0:r, vtile + si, :], ps, btok(bk), [("v", vtile + si)])
                    dma(vdst[krow0 + row0:krow0 + row0 + r, :], kvst[kb][0:r, :], [("kvst", kb)], [], is_out=True)
                else:
                    dma(kdst[krow0 + row0:krow0 + row0 + r, :], kvst[kb][0:r, :], [("kvst", kb)], [], is_out=True)
                    ps_slot = si % N_SS
                    kb16 = Pt_all[:, ps_slot, :]
                    copy_op("dve", kb16[0:r, :], ps, btok(bk), [("P", ps_slot)])

                    def _tr(kb16=kb16, ps_slot=ps_slot, row0=row0, r=r):
                        bt = proj_alloc()
                        psb = banks[bt].bitcast(BF16)
                        for h in range(4):
                            transpose(psb[:, h * 128:h * 128 + r], kb16[0:r, h * 128:(h + 1) * 128], ident[:r, :r],
                                      [("P", ps_slot), ("ident",)], btok(bt))
                        copy_op("act", kT_sb[:, :, kcol0 + row0:kcol0 + row0 + r],
                                psb[:, 0:512].rearrange("p (h t) -> p h t", h=4)[:, :, 0:r],
                                btok(bt), [tk for h in range(4) for tk in khs[h]])
                        bank_free(bt)
                    k_tr.append(_tr)
                bank_free(bk)
        if ktr_out is not None:
            ktr_out.extend(k_tr)
        else:
            for f_ in k_tr:
                f_()

    def conv_halo(Tt):
        for cc in range(4):
            copy_op("dve", cu[:, cc, 0:2], cu[:, cc, Tt:Tt + 2], [("cu", cc)], [("cu", cc)])

    def attention(Tt, keytiles, mb=0, hooks=None, uhooks=None, ahead=3):
        units = [(h, kt) for h in range(4) for kt in range(len(keytiles))]
        state = {}

        sc = state.setdefault("sc", [0])

        def pairable(u):
            if not MERGE_EXP or Tt != T or u + 1 >= len(units):
                return False
            (h0, k0), (h1, k1) = units[u], units[u + 1]
            A, B = keytiles[k0], keytiles[k1]
            ok = lambda K: K["near"] is None and K["nk"] == 128 and K["qa"] == 0 and K["qb"] == Tt
            return h0 == h1 and ok(A) and ok(B)

        def emit_S(u):
            h, ki = units[u]
            K = keytiles[ki]
            nk, qa, qb = K["nk"], K["qa"], K["qb"]
            role = state.get(("role", u))
            if role is None:
                if pairable(u) and ("role", u + 1) not in state:
                    state[("role", u)] = ("first", sc[0] % N_SS)
                    state[("role", u + 1)] = ("second", (sc[0] + 1) % N_SS)
                    sc[0] += 2
                else:
                    state[("role", u)] = ("single", sc[0] % N_SS)
                    sc[0] += 1
                role = state[("role", u)]
            kind, slot = role
            bk = slot
            ps3 = banks[bk].rearrange("p (s f) -> p s f", s=2)
            if qa == 0 and qb == Tt:
                mm(ps3[0:nk, :, qa:qb], kT_sb[:, h, K["kc0"]:K["kc0"] + nk], qz[:, h, :, qa:qb], True, True,
                   K["ktoks"](h) + [("qz", h, 0), ("qz", h, 1)], btok(bk))
            else:
                for s_ in range(2):
                    mm(ps3[0:nk, s_, qa:qb], kT_sb[:, h, K["kc0"]:K["kc0"] + nk], qz[:, h, s_, qa:qb], True, True,
                       K["ktoks"](h) + [("qz", h, 0), ("qz", h, 1)], btok(bk))
            if K["near"] is not None:
                na, nb, b0c = K["near"]
                P.add("dve", lambda e, o=ps3[0:nk, :, na:nb], b=B0[h][0:nk, b0c:b0c + (nb - na)].unsqueeze(1).to_broadcast(
                    [nk, 2, nb - na]): e.tensor_tensor(out=o, in0=o, in1=b, op=ALU.add),
                    btok(bk) + [("B0", h)], btok(bk))
            pi = slot
            if kind == "single":
                act(Pt[pi][0:nk, :, qa:qb], ps3[0:nk, :, qa:qb], AF.Exp, btok(bk) + [("rb",)], [("P", pi)],
                    bias=rbf[0:nk, h:h + 1])
            elif kind == "second":
                other = (slot - 1) % N_SS
                lo_, hi_ = min(slot, other), max(slot, other)
                st_ = hi_ - lo_
                act(Pt_all[:, lo_:hi_ + 1:st_, :], ps_all[:].rearrange("p (b f) -> p b f", f=512)[:, lo_:hi_ + 1:st_, :],
                    AF.Exp, btok(lo_) + btok(hi_) + [("rb",)], [("P", lo_), ("P", hi_)], bias=rbf[:, h:h + 1])
            state[u] = pi

        def emit_PV(u):
            h, ki = units[u]
            K = keytiles[ki]
            nk, qa, qb = K["nk"], K["qa"], K["qb"]
            pi = state.pop(u)
            first = ki == 0
            last = ki == len(keytiles) - 1
            if first:
                state["O"] = bank_alloc()
                state["L"] = bank_alloc()
            bo, bl = state["O"], state["L"]
            o3 = banks[bo][:].rearrange("p (s f) -> p s f", s=2)
            l3 = banks[bl][:].rearrange("p (s f) -> p s f", s=2)
            if qa == 0 and qb == Tt:
                mm(o3[:, :, qa:qb], v_sb[0:nk, K["vt"], h * 128:(h + 1) * 128], Pt[pi][0:nk, :, qa:qb], first, last,
                   [("v", K["vt"]), ("P", pi)], btok(bo), skip_group_check=True)
                mm(l3[:, :, qa:qb], ones[0:nk, :], Pt[pi][0:nk, :, qa:qb], first, last,
                   [("ones",), ("P", pi)], btok(bl), skip_group_check=True)
            else:
                assert not first
                for s_ in range(2):
                    mm(o3[:, s_, qa:qb], v_sb[0:nk, K["vt"], h * 128:(h + 1) * 128], Pt[pi][0:nk, s_, qa:qb], False,
                       last and s_ == 1, [("v", K["vt"]), ("P", pi)], btok(bo), skip_group_check=True)
                for s_ in range(2):
                    mm(l3[:, s_, qa:qb], ones[0:nk, :], Pt[pi][0:nk, s_, qa:qb], False, last and s_ == 1,
                       [("ones",), ("P", pi)], btok(bl), skip_group_check=True)
            if last:
                combine_a(h, bo, bl)
                state['cb_at'] = u + DEFER

        pending = []

        def combine_a(h, bo, bl):
            flush_pending()
            o3 = banks[bo][:].rearrange("p (s f) -> p s f", s=2)
            l3 = banks[bl][:].rearrange("p (s f) -> p s f", s=2)
            copy_op("dve", rl[:, :, 0:Tt], l3[:, :, 0:Tt], btok(bl), [("rl", 0), ("rl", 1)])
            bank_free(bl)
            tt_op("dve", ob[:, 0:Tt], o3[:, 0, 0:Tt], rl[:, 1, 0:Tt], ALU.mult, btok(bo) + [("rl", 1)], [("ob",)])
            tt_op("dve", tt[:, 0:Tt], o3[:, 1, 0:Tt], rl[:, 0, 0:Tt], ALU.mult, btok(bo) + [("rl", 0)], [("tt",)])
            bank_free(bo)
            stt_op("dve", ob[:, 0:Tt], tt[:, 0:Tt], neglam[:, 0:1], ob[:, 0:Tt], ALU.mult, ALU.add,
                   [("tt",), ("ob",), ("neglam",)], [("ob",)])
            tt_op("dve", sq[:, 0:Tt], ob[:, 0:Tt], ob[:, 0:Tt], ALU.mult, [("ob",)], [("sq",)])
            tt_op("dve", rs[:, 0:Tt], rl[:, 0, 0:Tt], rl[:, 1, 0:Tt], ALU.mult, [("rl", 0), ("rl", 1)], [("rs",)])
            stt_op("dve", rs[:, 0:Tt], rs[:, 0:Tt], EPS, rs[:, 0:Tt], ALU.mult, ALU.mult, [("rs",)], [("rs",)])
            pending.append(h)

        pending2 = []

        def combine_b(h):
            bm = bank_alloc()
            mm(banks[bm][:, 0:Tt], ones[:, :], sq[:, 0:Tt], True, True, [("ones",), ("sq",)], btok(bm))
            stt_op("dve", rs[:, 0:Tt], banks[bm][:, 0:Tt], 1.0 / 128.0, rs[:, 0:Tt], ALU.mult, ALU.add,
                   btok(bm) + [("rs",)], [("rs",)])
            bank_free(bm)
            pending2.append(h)

        def combine_c_act(h):
            act(tt[:, 0:Tt], rs[:, 0:Tt], AF.Ln, [("rs",)], [("tt",)])
            act(rs[:, 0:Tt], tt[:, 0:Tt], AF.Exp, [("tt",)], [("rs",)], scale=-0.5)

        def combine_c_dve(h):
            stt_op("dve", tt[:, 0:Tt], ob[:, 0:Tt], gsc[:, 0:1], rs[:, 0:Tt], ALU.mult, ALU.mult,
                   [("ob",), ("gsc",), ("rs",)], [("tt",)])
            tt_op("dve", mrgs[mb][:, h, 0:Tt], tt[:, 0:Tt], sza[:, h, 0:Tt], ALU.mult, [("tt",), ("sza", h)],
                  [("mrg", mb, h)])

        def combine_c(h):
            combine_c_act(h)
            combine_c_dve(h)

        def flush_b():
            while pending:
                combine_b(pending.pop(0))

        def flush_c():
            while pending2:
                combine_c(pending2.pop(0))

        def flush_pending():
            flush_b()
            flush_c()

        AHEAD = ahead
        npu = len(keytiles)
        DEFER = min(6, max(1, npu - 2))
        DEFER2 = 2 if npu >= 6 else 1
        n = len(units)
        for u in range(min(AHEAD, n)):
            emit_S(u)
        for u in range(n):
            if u + AHEAD < n:
                emit_S(u + AHEAD)
            emit_PV(u)
            if pending2 and u >= state.get('cc_at', 0):
                flush_c()
            if pending and u >= state.get('cb_at', 0):
                flush_b()
                state['cc_at'] = u + DEFER2
            if uhooks and u in uhooks:
                uhooks.pop(u)()
            if hooks and (u + 1) % len(keytiles) == 0 and ((u + 1) // len(keytiles) - 1) in hooks:
                hooks[(u + 1) // len(keytiles) - 1]()
        if uhooks:
            for k_ in sorted(uhooks):
                uhooks[k_]()

        class Fin:
            def __call__(self):
                flush_pending()

            def b(self):
                flush_c()
                flush_b()

            def c_act(self):
                self.hs = list(pending2)
                for h in self.hs:
                    combine_c_act(h)

            def c_dve(self):
                for h in self.hs:
                    combine_c_dve(h)
                del pending2[:]
        return Fin()

    def final_load(x_src, r, b=None):
        if b is None:
            b = next_big()
        dma(big[b][:r, :], x_src, [], [("big", b)])
        return b

    def final_pe(r, row0, mb, b):
        for half in range(2):
            bk = proj_alloc()
            ps = banks[bk][0:r, :]
            for kc in range(8):
                mm(ps, mrgs[mb][:, kc, row0:row0 + r], w_out_sb[:, kc, half * 512:(half + 1) * 512], kc == 0, kc == 7,
                   [("mrg", mb, kc), ("w_out", kc, half)], btok(bk))
            tt_op("dve", big[b][:r, half * 512:(half + 1) * 512], ps, big[b][:r, half * 512:(half + 1) * 512], ALU.add,
                  btok(bk) + [("big", b)], [("big", b)])
            bank_free(bk)

    def final_post1(r, xi, b):
        c0 = 8 + 4 * xi
        P.add("dve", lambda e, o=xns[xi][:r, :], i=big[b][:r, :], a=stat[:r, c0:c0 + 1]: e.scalar_tensor_tensor(
            out=o, in0=i, scalar=1.0, in1=i, op0=ALU.mult, op1=ALU.mult, accum_out=a),
            [("big", b)], [("xn", xi), ("fstat", xi, 0)])

    def final_post2(r, xi, b):
        c0 = 8 + 4 * xi
        act(stat[:r, c0 + 1:c0 + 2], stat[:r, c0:c0 + 1], AF.Ln, [("fstat", xi, 0), ("eps",)], [("fstat", xi, 1)],
            scale=1.0 / D, bias=epsc[:r, 0:1])
        act(stat[:r, c0 + 2:c0 + 3], stat[:r, c0 + 1:c0 + 2], AF.Exp, [("fstat", xi, 1)], [("fstat", xi, 2)], scale=-0.5)

    def final_post3(y_dst, r, xi, b):
        c0 = 8 + 4 * xi
        stt_op("dve", big[b][:r, :], big[b][:r, :], stat[:r, c0 + 2:c0 + 3], fg[:r, :], ALU.mult, ALU.mult,
               [("big", b), ("fstat", xi, 2), ("fg",)], [("big", b)])
        dma(y_dst, big[b][:r, :], [("big", b)], [], is_out=True)

    def final_post(y_dst, r, xi, b):
        c0 = 8 + 4 * xi
        act(xns[xi][:r, :], big[b][:r, :], AF.Square, [("big", b)], [("xn", xi), ("fstat", xi, 0)],
            accum_out=stat[:r, c0:c0 + 1])
        act(stat[:r, c0 + 1:c0 + 2], stat[:r, c0:c0 + 1], AF.Ln, [("fstat", xi, 0), ("eps",)], [("fstat", xi, 1)],
            scale=1.0 / D, bias=epsc[:r, 0:1])
        act(stat[:r, c0 + 2:c0 + 3], stat[:r, c0 + 1:c0 + 2], AF.Exp, [("fstat", xi, 1)], [("fstat", xi, 2)], scale=-0.5)
        stt_op("dve", big[b][:r, :], big[b][:r, :], stat[:r, c0 + 2:c0 + 3], fg[:r, :], ALU.mult, ALU.mult,
               [("big", b), ("fstat", xi, 2), ("fg",)], [("big", b)])
        dma(y_dst, big[b][:r, :], [("big", b)], [], is_out=True)

    def final_subtile(x_src, y_dst, r, row0, mb=0, xi=0, b=None):
        if b is None:
            b = final_load(x_src, r)
        final_pe(r, row0, mb, b)
        final_post(y_dst, r, xi, b)

    def load_w_out():
        for kc in range(8):
            for half in range(2):
                kb = next_kv()
                dma(kvst[kb][:], w_out[kc * 128:(kc + 1) * 128, half * 512:(half + 1) * 512], [], [("kvst", kb)])
                copy_op("act" if half == 0 else "dve", w_out_sb[:, kc, half * 512:(half + 1) * 512], kvst[kb][:],
                        [("kvst", kb)], [("w_out", kc, half)])

    def kT_tok_fn(blk):
        return lambda h: [("kT", h, blk)]

    def finish():
        P.emit(nc, es)
        es.close()
        return nc

    if STAGE == 0:
        return finish()
    def sample_main():
        for jj in range(2):
            dma(cu[:, :, jj], sconv[jj, :].rearrange("(cc p) -> p cc", p=128), [], [("cu", cc) for cc in range(4)])
        norm_subtile(x_s[:, :], DEC, 0, DEC)
        project_tile(DEC, [(0, DEC)], NCACHE, ntile_c, 0, True, True, k_s, v_s, khs=[kT_all(h) for h in range(4)])
        for jj in range(2):
            dma(conv_s[jj, :].rearrange("(cc p) -> p cc", p=128), cu[:, :, DEC + jj],
                [("cu", cc) for cc in range(4)], [], is_out=True)
        kts = []
        for t in range(ntile_c):
            r = min(128, NCACHE - t * 128)
            if t == 7:
                near = (0, DEC, 144)
            elif t == 8:
                near = (0, DEC, 16)
            else:
                near = None
            kts.append(dict(nk=r, kc0=t * 128, ktoks=kT_all, vt=t, qa=0, qb=DEC, near=near))
        kts.append(dict(nk=DEC, kc0=NCACHE, ktoks=kT_all, vt=ntile_c, qa=0, qb=DEC, near=(0, DEC, 0)))
        attention(DEC, kts, ahead=3)()
        final_subtile(x_s[:, :], y_s[:, :], DEC, 0)

    if STAGE is None or STAGE >= 2:
        sample_main()
    norm_subtile(meta[:, :], NMETA, 0, NMETA)
    if STAGE == 0.5:
        return finish()
    project_tile(NMETA, [(0, NMETA)], 0, 0, 0, False, False, k_p, v_p, khs=[[("kT", h, "m")] for h in range(4)])
    if STAGE in (0.6, 0.7, 0.8):
        return finish()
    conv_halo(NMETA)

    if STAGE == 1:
        return finish()
    ntl = NT if N_PROMPT_TILES is None else N_PROMPT_TILES

    def tile_norm_a(j):
        for si in range(2):
            norm_a(x_p[j * T + si * 128:j * T + (si + 1) * 128, :], 128, si)

    def tile_norm_b(j):
        for si in range(2):
            norm_b(128, si * 128, si)

    def tile_project(j, mid=None, mid2=None, ph=None, ktr_out=None):
        t0 = j * T
        project_tile(T, [(0, 128), (128, 128)], NMETA + t0, 1 + 2 * j, NMETA + t0, True, True, k_p, v_p,
                     khs=[[("kT", h, 2 * j), ("kT", h, 2 * j + 1)] for h in range(4)], mb=j % 2, mid=mid, mid2=mid2, ph=ph, ktr_out=ktr_out)
        if j == NT - 1:
            for jj in range(2):
                dma(conv_p[jj, :].rearrange("(cc p) -> p cc", p=128), cu[:, :, T + jj],
                    [("cu", cc) for cc in range(4)], [], is_out=True)
        conv_halo(T)

    def tile_keytiles(j):
        kts = []
        mnear = (0, 240, 16) if j == 0 else None
        kts.append(dict(nk=NMETA, kc0=0, ktoks=kT_tok_fn("m"), vt=0, qa=0, qb=T, near=mnear))
        for kt in range(2 * j + 2):
            i = kt - 2 * j
            if i == 1:
                qa, near = 128, (128, 256, 0)
            elif i == 0:
                qa, near = 0, (0, 256, 0)
            elif i == -1:
                qa, near = 0, (0, 128, 128)
            else:
                qa, near = 0, None
            kts.append(dict(nk=128, kc0=NMETA + kt * 128, ktoks=kT_tok_fn(kt), vt=1 + kt, qa=qa, qb=T, near=near))
        return kts

    NB = 2

    def xrows(jj, si):
        return x_p[jj * T + si * 128:jj * T + (si + 1) * 128, :]

    def norm_ph(jj, ph):
        if jj >= ntl:
            return
        ph.setdefault(0, []).append(lambda: norm_load(xrows(jj, 0), 128, b=NB))
        ph.setdefault(1, []).insert(1 if ph.get(1) else 0, lambda: norm_a_act(128, 0, NB))
        ph.setdefault(3, []).insert(0, lambda: norm_a_dve(128, 0, NB))
        ph.setdefault(3, []).insert(1, lambda: norm_load(xrows(jj, 1), 128, b=NB))
        ph.setdefault(7, []).append(lambda: norm_a_act(128, 1, NB))
        ph.setdefault(7, []).append(lambda: norm_a_dve(128, 1, NB))

    tile_norm_a(0)
    tile_norm_b(0)
    ph = {}
    norm_ph(1, ph)
    tile_project(0, ph=ph)
    post = []
    for j in range(ntl):
        t0 = j * T
        nxt = j + 1 < ntl
        uhooks = {}
        if nxt:
            uhooks[0] = (lambda jj=j + 1: tile_norm_b(jj))
        for i_, pf in enumerate(post):
            uhooks[1 + i_] = pf
        post = []
        fin = attention(T, tile_keytiles(j), mb=j % 2, hooks=None, uhooks=uhooks)
        fb = [final_load(xrows(j, si), 128, b=si) for si in range(2)]
        if nxt:
            ph = {1: [fin.b, fin.c_act], 3: [fin.c_dve]}
            norm_ph(j + 2, ph)
            ktr = []
            tile_project(j + 1, ph=ph, ktr_out=ktr)
        else:
            ktr = []
            fin()
        for si in range(2):
            final_pe(128, si * 128, j % 2, fb[si])
        for f_ in ktr:
            f_()
        yd = lambda si, t0=t0: y_p[t0 + si * 128:t0 + (si + 1) * 128, :]
        post = [lambda b=fb[0]: final_post1(128, 0, b),
                lambda b=fb[1]: final_post1(128, 1, b),
                lambda b=fb[0]: final_post2(128, 0, b),
                lambda b=fb[1]: final_post2(128, 1, b),
                lambda b=fb[0], yd=yd: final_post3(yd(0), 128, 0, b),
                lambda b=fb[1], yd=yd: final_post3(yd(1), 128, 1, b)]
    for pf in post:
        pf()

    return finish()


_NC_CACHE = {}


def kernel(x_prompt, x_sample, cache_k, cache_v, state_conv, meta_tokens, rel_bias, norm_g, w_in,
           conv_w, lambda_q1, lambda_k1, lambda_q2, lambda_k2, subln_g, w_out, final_g):
    f = lambda a: np.ascontiguousarray(np.asarray(a, dtype=np.float32))
    if "nc" not in _NC_CACHE:
        _NC_CACHE["nc"] = build_nc()
    nc = _NC_CACHE["nc"]
    x_prompt = f(x_prompt); x_sample = f(x_sample)
    cache_k = f(cache_k); cache_v = f(cache_v); state_conv = f(state_conv)
    shared = {
        "meta": f(meta_tokens), "relb": f(rel_bias), "norm_g": f(norm_g).reshape(D),
        "w_in": f(w_in).reshape(D, 4096), "conv_w": f(conv_w).reshape(3, 512),
        "lq1": f(lambda_q1).reshape(64), "lk1": f(lambda_k1).reshape(64),
        "lq2": f(lambda_q2).reshape(64), "lk2": f(lambda_k2).reshape(64),
        "subln_g": f(subln_g).reshape(128), "w_out": f(w_out).reshape(D, D), "final_g": f(final_g).reshape(D),
    }
    in_maps = []
    for c in range(N_CORES):
        m = dict(shared)
        m["x_p"] = x_prompt[c]
        m["x_s"] = x_sample[c]
        m["ck"] = cache_k[0, c].reshape(NCACHE, 512)
        m["cv"] = cache_v[0, c].reshape(NCACHE, 512)
        m["sconv"] = state_conv[0, c]
        in_maps.append(m)
    res = run_bass_kernel_spmd(nc, in_maps, core_ids=list(range(N_CORES)))
    R = res.results
    st = lambda k: np.stack([np.asarray(R[c][k], dtype=np.float32) for c in range(N_CORES)])
    y_prompt = st("y_p")
    y_sample = st("y_s")
    k_prompt = st("k_p").reshape(1, N_CORES, NMETA + SEQ, 4, 128)
    v_prompt = st("v_p").reshape(1, N_CORES, NMETA + SEQ, 4, 128)
    conv_prompt = st("conv_p").reshape(1, N_CORES, 2, 512)
    k_sample = st("k_s").reshape(1, N_CORES, DEC, 4, 128)
    v_sample = st("v_s").reshape(1, N_CORES, DEC, 4, 128)
    conv_sample = st("conv_s").reshape(1, N_CORES, 2, 512)
    return (y_prompt, y_sample, k_prompt, v_prompt, conv_prompt, k_sample, v_sample, conv_sample)
```

```python
import numpy as np
from contextlib import ExitStack
import concourse.bass as bass
import concourse.mybir as mybir
from concourse.bass_utils import run_bass_kernel_spmd

F32 = mybir.dt.float32
BF16 = mybir.dt.bfloat16
ALU = mybir.AluOpType
AF = mybir.ActivationFunctionType
AX = mybir.AxisListType

N_CORES = 8
SEQ = 4096
D = 1024
T = 256
NT = SEQ // T
NMETA = 16
PAST = 1024
NCACHE = NMETA + PAST
DEC = 32
EPS = 1e-6
LAM_INIT = 0.2
MASKV = -30000.0
N_DMA_SEMS = 40
STAGE = None
N_PROMPT_TILES = None
SETUP_MASK = 0xFF
FLAG_ALL = False
CACHE_DMA_ENG = "pool"
MERGE_EXP = False

BIAS_STEPS = [(-90, 14), (-63, 13), (-45, 12), (-31, 11), (-22, 10), (-15, 9), (-11, 8),
              (-7, 7), (-6, 6), (-5, 5), (-4, 4), (-3, 3), (-2, 2), (-1, 1), (0, 0),
              (1, 17), (2, 18), (3, 19), (4, 20), (5, 21), (6, 22), (7, 23), (8, 24),
              (12, 25), (16, 26), (23, 27), (32, 28), (46, 29)]


class Op:
    __slots__ = ("eng", "fn", "reads", "writes", "dma", "deps", "flag", "ev", "is_out")

    def __init__(self, eng, fn, reads, writes, dma, is_out):
        self.eng = eng
        self.fn = fn
        self.reads = reads
        self.writes = writes
        self.dma = dma
        self.deps = ()
        self.flag = False
        self.ev = None
        self.is_out = is_out


class Prog:
    ENGS = ("pe", "act", "dve", "pool", "sp")

    def __init__(self):
        self.ops = []
        self.group = 0xFF

    def add(self, eng, fn, reads=(), writes=(), dma=False, is_out=False):
        if not (SETUP_MASK & self.group):
            return
        writes = tuple(writes) + tuple(t for t in reads if t[0] == "bank" and t not in writes)
        self.ops.append(Op(eng, fn, tuple(reads), tuple(writes), dma, is_out))

    def analyze(self):
        last_w = {}
        readers = {}
        ops = self.ops
        for i, op in enumerate(ops):
            raw = set()
            other = set()
            for t in op.reads:
                w = last_w.get(t)
                if w is not None:
                    raw.add(w)
            for t in op.writes:
                w = last_w.get(t)
                if w is not None:
                    other.add(w)
                for r in readers.get(t, ()):
                    other.add(r)
            deps = set()
            for d in raw | other:
                if d == i:
                    continue
                dop = ops[d]
                if (not dop.dma) and (not op.dma) and dop.eng == op.eng:
                    if op.eng == "pe":
                        continue
                    if d not in raw:
                        continue
                deps.add(d)
            op.deps = tuple(sorted(deps))
            for d in deps:
                ops[d].flag = True
            if FLAG_ALL and not op.dma:
                op.flag = True
            for t in op.writes:
                last_w[t] = i
                readers[t] = []
            for t in op.reads:
                readers.setdefault(t, []).append(i)

    def emit(self, nc, es):
        ops = self.ops
        self.analyze()
        last_ops = []
        for e in ("pe", "act", "dve", "pool"):
            lst = [op for op in ops if op.eng == e and not op.dma]
            if lst:
                lst[-1].flag = True
                last_ops.append(lst[-1])
        sems = {e: es.enter_context(nc.semaphore("s_" + e)) for e in ("pe", "act", "dve", "pool")}
        dsems = [es.enter_context(nc.semaphore("d%d" % i)) for i in range(N_DMA_SEMS)]
        cnt = {e: 0 for e in sems}
        dcnt = [0] * N_DMA_SEMS
        dlast = [None] * N_DMA_SEMS
        nd = 0
        for i, op in enumerate(ops):
            if op.dma:
                s = nd % N_DMA_SEMS
                nd += 1
                if dlast[s] is not None:
                    op.deps = tuple(sorted(set(op.deps) | {dlast[s]}))
                dcnt[s] += 16
                op.ev = (dsems[s], dcnt[s], ("d", s))
                dlast[s] = i
                op.flag = True
            elif op.flag:
                cnt[op.eng] += 1
                op.ev = (sems[op.eng], cnt[op.eng], op.eng)
        block = es.enter_context(nc.Block())
        per_eng = {e: [op for op in ops if op.eng == e] for e in self.ENGS}
        out_events = [op.ev for op in ops if op.is_out] + [op.ev for op in last_ops]

        import os
        dump = open(os.environ["KDUMP"], "w") if os.environ.get("KDUMP") else None

        def run(eng_obj, lst, final_events=()):
            waited = {}
            for op in lst:
                if dump:
                    dump.write("%s #%d deps=%s flag=%s ev=%s r=%s w=%s\n" % (
                        op.eng, ops.index(op), [(ops[d].ev[2], ops[d].ev[1]) for d in op.deps], op.flag,
                        (op.ev[2], op.ev[1]) if op.ev else None, op.reads[:4], op.writes[:4]))
                need = {}
                for d in op.deps:
                    sem, val, key = ops[d].ev
                    if waited.get(key, 0) >= val:
                        continue
                    if key not in need or need[key][1] < val:
                        need[key] = (sem, val)
                for key, (sem, val) in need.items():
                    eng_obj.wait_ge(sem, val)
                    waited[key] = val
                ins = op.fn(eng_obj)
                if op.flag:
                    ins.then_inc(op.ev[0], 16 if op.dma else 1)
            need = {}
            for (sem, val, key) in final_events:
                if waited.get(key, 0) >= val:
                    continue
                if key not in need or need[key][1] < val:
                    need[key] = (sem, val)
            for key, (sem, val) in need.items():
                eng_obj.wait_ge(sem, val)

        @block.sync
        def _(e):
            run(e, per_eng["sp"], out_events)

        @block.tensor
        def _(e):
            run(e, per_eng["pe"])

        @block.scalar
        def _(e):
            run(e, per_eng["act"])

        @block.vector
        def _(e):
            run(e, per_eng["dve"])

        @block.gpsimd
        def _(e):
            run(e, per_eng["pool"])


def build_nc():
    nc = bass.Bass("TRN2", target_bir_lowering=False)
    es = ExitStack()
    P = Prog()

    def din(name, shape):
        return nc.dram_tensor(name, shape, F32, kind="ExternalInput").ap()

    def dout(name, shape):
        return nc.dram_tensor(name, shape, F32, kind="ExternalOutput").ap()

    x_p = din("x_p", [SEQ, D])
    x_s = din("x_s", [DEC, D])
    ck = din("ck", [NCACHE, 512])
    cv = din("cv", [NCACHE, 512])
    sconv = din("sconv", [2, 512])
    meta = din("meta", [NMETA, D])
    relb = din("relb", [32, 4])
    norm_g = din("norm_g", [D])
    w_in = din("w_in", [D, 4096])
    conv_w = din("conv_w", [3, 512])
    lam_in = [din(n, [64]) for n in ("lq1", "lk1", "lq2", "lk2")]
    subln_g = din("subln_g", [128])
    w_out = din("w_out", [D, D])
    final_g = din("final_g", [D])
    y_p = dout("y_p", [SEQ, D])
    y_s = dout("y_s", [DEC, D])
    k_p = dout("k_p", [NMETA + SEQ, 512])
    v_p = dout("v_p", [NMETA + SEQ, 512])
    conv_p = dout("conv_p", [2, 512])
    k_s = dout("k_s", [DEC, 512])
    v_s = dout("v_s", [DEC, 512])
    conv_s = dout("conv_s", [2, 512])

    def sb(name, shape, dt=F32):
        return es.enter_context(nc.sbuf_tensor(name, shape, dt))

    w_in_sb = sb("w_in_sb", [128, 8, 4096], BF16)
    w_out_sb = sb("w_out_sb", [128, 8, D], BF16)
    kT_sb = sb("kT_sb", [128, 4, NMETA + SEQ], BF16)
    v_sb = sb("v_sb", [128, 33, 512], BF16)
    ident = sb("ident", [128, 128], BF16)
    ones = sb("ones", [128, 128], BF16)
    epsc = sb("epsc", [128, 1])
    g_sb = sb("g_sb", [128, 8])
    fg = sb("fg", [128, D])
    cw = sb("cw", [128, 3, 4])
    sg = sb("sg", [128, 1])
    gsc = sb("gsc", [128, 1])
    lsum = sb("lsum", [128, 2])
    lexp = sb("lexp", [128, 2])
    ldif = sb("ldif", [128, 1])
    neglam = sb("neglam", [128, 1])
    rbf = sb("rbf", [128, 4])
    B0 = [sb("B0_%d" % h, [128, 256], BF16) for h in range(4)]
    NBIG = 3
    big = [sb("big%d" % i, [128, D]) for i in range(NBIG)]
    xns = [sb("xn%d" % i, [128, D], BF16) for i in range(2)]
    xn = xns[0]
    nT = sb("nT", [128, 8, T], BF16)
    qz = sb("qz", [128, 4, 2, T], BF16)
    sza = sb("sza", [128, 4, T], BF16)
    mrgs = [sb("mrg%d" % i, [128, 8, T], BF16) for i in range(2)]
    NKV = 2
    kvst = [sb("kvst%d" % i, [128, 512]) for i in range(NKV)]
    cu = sb("cu", [128, 4, T + 2])
    cg = sb("cg", [128, T])
    szc = sb("szc", [128, T])
    ct1 = sb("ct1", [128, T])
    NP = 4
    N_SS = 4
    Pt_all = sb("Pt_all", [128, N_SS, 2 * T], BF16)
    Pt = [Pt_all[:, i, :].rearrange("p (s f) -> p s f", s=2) for i in range(N_SS)]
    rl = sb("rl", [128, 2, T])
    ob = sb("ob", [128, T])
    sq = sb("sq", [128, T], BF16)
    rs = sb("rs", [128, T])
    tt = sb("tt", [128, T])
    stat = sb("stat", [128, 16])
    junk = xn
    ct2 = sb("ct2", [128, T])
    rb_sb = ct2[:, 0:128]
    dl = ob[:, 0:len(BIAS_STEPS) * 4].rearrange("p (a b) -> p a b", b=4)
    sza_f = sza[:].rearrange("p a b -> p (a b)").bitcast(F32)
    lam4 = sza_f[:, 0:256].rearrange("p (a b) -> p a b", a=4)
    lprod = sza_f[:, 256:384].rearrange("p (a b) -> p a b", a=2)
    Rm = cg
    stepm = [szc, ct1]
    stepm_tok = [("szc",), ("ct1",)]
    ps_all = es.enter_context(nc.psum_tensor("ps_all", [128, 4096], F32))
    banks = [ps_all[:, i * 512:(i + 1) * 512] for i in range(8)]

    N_SSLOT = 4
    free_banks = list(range(N_SSLOT, 8))

    def bank_alloc():
        return free_banks.pop(0)

    def bank_free(b):
        if b in proj_taken:
            proj_taken.discard(b)
            return
        free_banks.append(b)

    proj_rr = [0]
    proj_order = [4, 5, 6, 7, 0, 1, 2, 3]
    proj_taken = set()

    def proj_alloc():
        b = proj_order[proj_rr[0] % 8]
        proj_rr[0] += 1
        proj_taken.add(b)
        return b

    def btok(b):
        return [("bank", b, 0), ("bank", b, 1)]

    rr = {"big": 0, "kv": 0, "P": 0}

    def next_big():
        i = rr["big"] % NBIG
        rr["big"] += 1
        return i

    def next_kv():
        i = rr["kv"] % NKV
        rr["kv"] += 1
        return i

    def next_P():
        i = rr["P"] % NP
        rr["P"] += 1
        return i

    def dma(out, in_, reads, writes, is_out=False, eng="sp"):
        P.add(eng, lambda e, o=out, i=in_: e.dma_start(out=o, in_=i, allow_slow_non_contiguous=True),
              reads, writes, dma=True, is_out=is_out)

    def act(out, in_, func, reads, writes, **kw):
        P.add("act", lambda e, o=out, i=in_, f=func, k=kw: e.activation(out=o, in_=i, func=f, **k),
              reads, writes)

    def tt_op(eng, out, in0, in1, op, reads, writes):
        P.add(eng, lambda e, o=out, a=in0, b=in1, p=op: e.tensor_tensor(out=o, in0=a, in1=b, op=p),
              reads, writes)

    def ts_op(eng, out, in0, s1, s2, op0, op1, reads, writes):
        if s2 is None:
            P.add(eng, lambda e, o=out, a=in0, x=s1, p=op0: e.tensor_scalar(out=o, in0=a, scalar1=x, scalar2=None, op0=p),
                  reads, writes)
        else:
            P.add(eng, lambda e, o=out, a=in0, x=s1, y=s2, p=op0, q=op1: e.tensor_scalar(
                out=o, in0=a, scalar1=x, scalar2=y, op0=p, op1=q), reads, writes)

    def stt_op(eng, out, in0, scalar, in1, op0, op1, reads, writes):
        P.add(eng, lambda e, o=out, a=in0, s=scalar, b=in1, p=op0, q=op1: e.scalar_tensor_tensor(
            out=o, in0=a, scalar=s, in1=b, op0=p, op1=q), reads, writes)

    def copy_op(eng, out, in_, reads, writes):
        if eng == "act":
            P.add("act", lambda e, o=out, i=in_: e.activation(out=o, in_=i, func=AF.Identity), reads, writes)
        else:
            P.add(eng, lambda e, o=out, i=in_: e.tensor_copy(out=o, in_=i), reads, writes)

    def recip(out, in_, reads, writes):
        P.add("dve", lambda e, o=out, i=in_: e.reciprocal(out=o, in_=i), reads, writes)

    def memset(eng, ap, val, writes):
        P.add(eng, lambda e, a=ap, v=val: e.memset(a, v), (), writes)

    def mm(out, lhsT, rhs, start, stop, reads, writes, **kw):
        P.add("pe", lambda e, o=out, l=lhsT, r=rhs, s=start, t=stop, k=kw: e.matmul(
            o, lhsT=l, rhs=r, start=s, stop=t, **k), reads, writes)

    def transpose(out, in_, idn, reads, writes):
        P.add("pe", lambda e, o=out, i=in_, d=idn: e.transpose(o, i, d), reads, writes)

    P.group = 1
    memset("pool", ident[:], 0.0, [("ident",)])
    P.add("pool", lambda e: e.affine_select(out=ident[:], in_=ident[:], compare_op=ALU.not_equal, fill=1.0,
                                            base=0, pattern=[[-1, 128]], channel_multiplier=1),
          [("ident",)], [("ident",)])
    memset("pool", ones[:], 1.0, [("ones",)])
    memset("pool", epsc[:], EPS, [("eps",)])
    memset("pool", qz[:], 0.0, [("qz", h, s_) for h in range(4) for s_ in range(2)])

    dma(g_sb[:], norm_g.rearrange("(kc p) -> p kc", p=128), [], [("g_sb",)])
    P.group = 2
    dma(rb_sb, relb.rearrange("b h -> (b h)").partition_broadcast(128), [], [("ct2",)])
    copy_op("dve", rbf[:], rb_sb[:, 60:64], [("ct2",)], [("rb",)])
    for i in range(4):
        dma(lam4[:, i, :], lam_in[i].partition_broadcast(128), [], [("sza", i)])
    dma(sg[:], subln_g.rearrange("(p o) -> p o", o=1), [], [("sg",)])
    for jj in range(3):
        dma(cw[:, jj, :], conv_w[jj, :].rearrange("(cc p) -> p cc", p=128), [], [("cw", jj)])
    dma(fg[:], final_g.partition_broadcast(128), [], [("fg",)])
    P.group = 4

    kT_all = lambda h: [("kT", h, b) for b in ["m"] + list(range(2 * NT))]
    ntile_c = (NCACHE + 127) // 128
    def cache_tile(t):
        r = min(128, NCACHE - t * 128)
        kb = next_kv()
        xi_ = t % 2
        dma(kvst[kb][0:r, :], ck[t * 128:t * 128 + r, :], [], [("kvst", kb)], eng=CACHE_DMA_ENG)
        copy_op("dve", xns[xi_][0:r, 0:512], kvst[kb][0:r, :], [("kvst", kb)], [("xn", xi_)])
        bk = bank_alloc()
        psb = banks[bk][:].bitcast(BF16)
        for h in range(4):
            transpose(psb[:, h * 128:h * 128 + r], xns[xi_][0:r, h * 128:(h + 1) * 128], ident[:r, :r],
                      [("xn", xi_), ("ident",)], btok(bk))
        copy_op("act", kT_sb[:, :, t * 128:t * 128 + r], psb[:, 0:512].rearrange("p (h t) -> p h t", h=4)[:, :, 0:r],
                btok(bk), [tk for h in range(4) for tk in kT_all(h)])
        bank_free(bk)
        kb = next_kv()
        dma(kvst[kb][0:r, :], cv[t * 128:t * 128 + r, :], [], [("kvst", kb)], eng=CACHE_DMA_ENG)
        copy_op("dve", v_sb[0:r, t, :], kvst[kb][0:r, :], [("kvst", kb)], [("v", t)])

    cache_thunks = [(lambda t=t: cache_tile(t)) for t in range(ntile_c)]
    piece_no = [0]

    def cache_step():
        piece_no[0] += 1
        if piece_no[0] % 4 == 0 and cache_thunks:
            cache_thunks.pop(0)()

    b0_thunks = []
    b0_acc = [(rl[:, 0, :], ("rl", 0)), (rl[:, 1, :], ("rl", 1)), (tt[:], ("tt",)), (rs[:], ("rs",))]

    def _b0_plan():
        b0_thunks.append(lambda: P.add("pool", lambda e: e.iota(Rm[:], pattern=[[-1, 256]], base=0, channel_multiplier=1,
                                                               allow_small_or_imprecise_dtypes=True), [], [("cg",)]))
        prev_b = 15
        for i, (thr, bk) in enumerate(BIAS_STEPS):
            b0_thunks.append(lambda i=i, bk=bk, pb=prev_b: tt_op(
                "dve", dl[:, i, :], rb_sb[:, bk * 4:bk * 4 + 4], rb_sb[:, pb * 4:pb * 4 + 4], ALU.subtract,
                [("ct2",)], [("ob",)]))
            prev_b = bk
        for h in range(4):
            b0_thunks.append(lambda h=h: memset("dve", b0_acc[h][0], 0.0, [b0_acc[h][1]]))
        for i, (thr, bk) in enumerate(BIAS_STEPS):
            sm = stepm[i % 2]
            b0_thunks.append(lambda i=i, sm=sm, thr=thr: P.add(
                "dve", lambda e, o=sm, t=float(thr): e.tensor_single_scalar(out=o[:], in_=Rm[:], scalar=t, op=ALU.is_ge),
                [("cg",)], [stepm_tok[i % 2]]))
            for h in range(4):
                b0_thunks.append(lambda i=i, sm=sm, h=h: stt_op(
                    "dve", b0_acc[h][0], sm[:], dl[:, i, h:h + 1], b0_acc[h][0], ALU.mult, ALU.add,
                    [stepm_tok[i % 2], ("ob",), b0_acc[h][1]], [b0_acc[h][1]]))
        for h in range(4):
            b0_thunks.append(lambda h=h: copy_op("dve", B0[h][:], b0_acc[h][0], [b0_acc[h][1]], [("B0", h)]))
        for h in range(4):
            b0_thunks.append(lambda h=h: memset("dve", B0[h][64:128, 0:64], MASKV, [("B0", h)]))

    _b0_plan()

    def b0_step(n):
        for _ in range(n):
            if b0_thunks:
                b0_thunks.pop(0)()

    stg = [(big[i][:], [("big", i)]) for i in range(NBIG)]
    for m_ in range(2):
        stg.append((mrgs[m_][:].rearrange("p a b -> p (a b)").bitcast(F32), [("mrg", m_, c) for c in range(8)]))
    stg.append((nT[:].rearrange("p a b -> p (a b)").bitcast(F32), [("nT",)]))
    stg_i = [0]

    def next_stg():
        i = stg_i[0] % len(stg)
        stg_i[0] += 1
        return stg[i]

    for q in (0, 2, 3, 1):
        for kc in range(8):
            buf, btk = next_stg()
            dma(buf, w_in[kc * 128:(kc + 1) * 128, q * 1024:(q + 1) * 1024], [], btk)
            act(w_in_sb[:, kc, q * 1024:(q + 1) * 1024], buf[:, :], AF.Copy,
                btk + [("g_sb",)], [("w_in", kc, q, 0), ("w_in", kc, q, 1)], scale=g_sb[:, kc:kc + 1])
            b0_step(4)
            cache_step()
    for kc in range(8):
        buf, btk = next_stg()
        dma(buf, w_out[kc * 128:(kc + 1) * 128, :], [], btk)
        copy_op("act", w_out_sb[:, kc, :], buf[:, :], btk, [("w_out", kc, 0), ("w_out", kc, 1)])
        b0_step(4)
        cache_step()
    b0_step(10000)
    while cache_thunks:
        cache_thunks.pop(0)()

    def w_in_tok(kc, c0, c1):
        toks = set()
        for c in range(c0 // 512, (c1 - 1) // 512 + 1):
            toks.add(("w_in", kc, c // 2, c % 2))
        return list(toks)

    P.group = 8
    tt_op("dve", lprod[:, 0, :], lam4[:, 0, :], lam4[:, 1, :], ALU.mult, [("sza", 0), ("sza", 1)], [("lprod", 0), ("sza", 2)])
    tt_op("dve", lprod[:, 1, :], lam4[:, 2, :], lam4[:, 3, :], ALU.mult, [("sza", 2), ("sza", 3)], [("lprod", 1), ("sza", 2)])
    P.add("dve", lambda e: e.reduce_sum(out=lsum[:], in_=lprod, axis=AX.X),
          [("lprod", 0), ("lprod", 1), ("sza", 2)], [("lsum",)])
    act(lexp[:], lsum[:], AF.Exp, [("lsum",)], [("lexp",)])
    tt_op("dve", ldif[:], lexp[:, 0:1], lexp[:, 1:2], ALU.subtract, [("lexp",)], [("ldif",)])
    ts_op("dve", neglam[:], ldif[:], -1.0, -LAM_INIT, ALU.mult, ALU.add, [("ldif",)], [("neglam",)])
    ts_op("dve", gsc[:], sg[:], 1.0 - LAM_INIT, None, ALU.mult, None, [("sg",)], [("gsc",)])

    P.group = 0xFF
    def norm_load(x_src, r, b=None):
        if b is None:
            b = next_big()
        dma(big[b][:r, :], x_src, [], [("big", b)])
        return b

    def norm_a(x_src, r, xi, b=None):
        if b is None:
            b = norm_load(x_src, r)
        act(xns[xi][:r, :], big[b][:r, :], AF.Square, [("big", b)], [("xn", xi), ("stat", xi, 0)],
            accum_out=stat[:r, 4 * xi:4 * xi + 1])
        act(stat[:r, 4 * xi + 1:4 * xi + 2], stat[:r, 4 * xi:4 * xi + 1], AF.Ln, [("stat", xi, 0), ("eps",)],
            [("stat", xi, 1)], scale=1.0 / D, bias=epsc[:r, 0:1])
        act(stat[:r, 4 * xi + 2:4 * xi + 3], stat[:r, 4 * xi + 1:4 * xi + 2], AF.Exp, [("stat", xi, 1)],
            [("stat", xi, 2)], scale=-0.5)
        ts_op("dve", xns[xi][:r, :], big[b][:r, :], stat[:r, 4 * xi + 2:4 * xi + 3], None, ALU.mult, None,
              [("big", b), ("stat", xi, 2)], [("xn", xi)])

    def norm_a_act(r, xi, b):
        P.add("dve", lambda e, o=xns[xi][:r, :], i=big[b][:r, :], a=stat[:r, 4 * xi:4 * xi + 1]: e.scalar_tensor_tensor(
            out=o, in0=i, scalar=1.0, in1=i, op0=ALU.mult, op1=ALU.mult, accum_out=a),
            [("big", b)], [("xn", xi), ("stat", xi, 0)])
        act(stat[:r, 4 * xi + 1:4 * xi + 2], stat[:r, 4 * xi:4 * xi + 1], AF.Ln, [("stat", xi, 0), ("eps",)],
            [("stat", xi, 1)], scale=1.0 / D, bias=epsc[:r, 0:1])
        act(stat[:r, 4 * xi + 2:4 * xi + 3], stat[:r, 4 * xi + 1:4 * xi + 2], AF.Exp, [("stat", xi, 1)],
            [("stat", xi, 2)], scale=-0.5)

    def norm_a_dve(r, xi, b):
        ts_op("dve", xns[xi][:r, :], big[b][:r, :], stat[:r, 4 * xi + 2:4 * xi + 3], None, ALU.mult, None,
              [("big", b), ("stat", xi, 2)], [("xn", xi)])

    def norm_b(r, row0, xi):
        bk = bank_alloc()
        psb = banks[bk][:].bitcast(BF16)
        for kc in range(8):
            transpose(psb[:, kc * 128:kc * 128 + r], xns[xi][:r, kc * 128:(kc + 1) * 128], ident[:r, :r],
                      [("xn", xi), ("ident",)], btok(bk))
        copy_op("dve", nT[:, :, row0:row0 + r], psb.rearrange("p (k t) -> p k t", k=8)[:, :, 0:r],
                btok(bk), [("nT",)])
        bank_free(bk)

    def norm_subtile(x_src, r, row0, Tt, xi=0):
        norm_a(x_src, r, xi)
        norm_b(r, row0, xi)

    def proj_feature(col0, Tt, evac):
        pass

    class HalfBanks:
        def __init__(self):
            self.cur = None
            self.used = 0

        def get(self):
            if self.cur is None or self.used == 2:
                if self.cur is not None:
                    bank_free(self.cur)
                self.cur = proj_alloc()
                self.used = 0
            h = self.used
            self.used += 1
            return self.cur, h

        def done(self):
            if self.cur is not None:
                bank_free(self.cur)
                self.cur = None
                self.used = 0

    def proj_fm(hb, col0, Tt):
        bk, half = hb.get()
        ps = banks[bk][:, half * 256:half * 256 + Tt]
        tok = btok(bk)
        for kc in range(8):
            mm(ps, w_in_sb[:, kc, col0:col0 + 128], nT[:, kc, 0:Tt], kc == 0, kc == 7,
               [("nT",)] + w_in_tok(kc, col0, col0 + 128), tok)
        return ps, tok

    def project_tile(Tt, subtiles, kcol0, vtile, krow0, want_q, want_conv_out, kdst, vdst, khs=None, mb=0, mid=None, mid2=None, ph=None, ktr_out=None):
        hb = HalfBanks()

        def run_ph(i):
            if ph and i in ph:
                for f_ in ph[i]:
                    f_()

        def pair(colA, evA, colB, evB):
            psA, tokA = proj_fm(hb, colA, Tt)
            psB, tokB = proj_fm(hb, colB, Tt)
            evA(psA, tokA)
            evB(psB, tokB)

        if mid is not None:
            mid()
        run_ph(0)
        if want_q:
            def ev_q(h):
                def f(ps, tok):
                    act(qz[0:64, h, 0, 0:Tt], ps[0:64, :], AF.Copy, [tok[0]], [("qz", h, 0)], scale=0.125)
                    P.add("dve", lambda e, o=qz[64:128, h, 1, 0:Tt], i=ps[64:128, :]: e.tensor_scalar(
                        out=o, in0=i, scalar1=0.125, scalar2=None, op0=ALU.mult), [tok[1]], [("qz", h, 1)])
                return f
            pair(0, ev_q(0), 128, ev_q(1))
            run_ph(0.5)
            pair(256, ev_q(2), 384, ev_q(3))
            run_ph(1)
        for cc in range(4):
            run_ph(2 + cc) if cc > 0 else None

            def ev_cg(ps, tok):
                copy_op("act", cg[:, 0:Tt], ps, tok, [("cg",)])

            def ev_u(ps, tok, cc=cc):
                tt_op("dve", cu[:, cc, 2:2 + Tt], ps, cg[:, 0:Tt], ALU.mult, tok + [("cg",)], [("cu", cc)])
                if want_conv_out:
                    ts_op("dve", ct1[:, 0:Tt], cu[:, cc, 0:Tt], cw[:, 0, cc:cc + 1], None, ALU.mult, None,
                          [("cu", cc), ("cw", 0), ("cw", 1), ("cw", 2)], [("ct1",)])
                    stt_op("dve", ct2[:, 0:Tt], cu[:, cc, 1:1 + Tt], cw[:, 1, cc:cc + 1], ct1[:, 0:Tt], ALU.mult, ALU.add,
                           [("cu", cc), ("cw", 0), ("cw", 1), ("cw", 2), ("ct1",)], [("ct2",)])
                    stt_op("dve", ct1[:, 0:Tt], cu[:, cc, 2:2 + Tt], cw[:, 2, cc:cc + 1], ct2[:, 0:Tt], ALU.mult, ALU.add,
                           [("cu", cc), ("cw", 0), ("cw", 1), ("cw", 2), ("ct2",)], [("ct1",)])
            pair(2560 + cc * 128, ev_cg, 3072 + cc * 128, ev_u)
            if want_conv_out:
                def ev_zc(ps, tok):
                    act(szc[:, 0:Tt], ps, AF.Silu, tok, [("szc",)])

                def ev_bg(ps, tok, cc=cc):
                    tt_op("dve", ct2[:, 0:Tt], ps, szc[:, 0:Tt], ALU.mult, tok + [("szc",)], [("ct2",)])
                    tt_op("dve", mrgs[mb][:, 4 + cc, 0:Tt], ct2[:, 0:Tt], ct1[:, 0:Tt], ALU.mult,
                          [("ct2",), ("ct1",)], [("mrg", mb, 4 + cc)])
                pair(3584 + cc * 128, ev_zc, 2048 + cc * 128, ev_bg)
        run_ph(6)
        if want_q:
            ev_za = lambda h: (lambda ps, tok: act(sza[:, h, 0:Tt], ps, AF.Silu, tok, [("sza", h)]))
            for h in (0, 2):
                pair(1536 + h * 128, ev_za(h), 1536 + (h + 1) * 128, ev_za(h + 1))
        hb.done()
        run_ph(7)
        if mid2 is not None:
            mid2()
        if STAGE == 0.7:
            return
        k_tr = []
        for si, (row0, r) in enumerate(subtiles):
            for which in ("v", "k"):
                c0 = 1024 if which == "v" else 512
                bk = proj_alloc()
                ps = banks[bk][0:r, :]
                for kc in range(8):
                    mm(ps, nT[:, kc, row0:row0 + r], w_in_sb[:, kc, c0:c0 + 512], kc == 0, kc == 7,
                       [("nT",)] + w_in_tok(kc, c0, c0 + 512), btok(bk))
                kb = next_kv()
                copy_op("act", kvst[kb][0:r, :], ps, btok(bk), [("kvst", kb)])
                if which == "v":
                    copy_op("dve", v_sb[0:r, vtile + si, :], ps, btok(bk), [("v", vtile + si)])
                    dma(vdst[krow0 + row0:krow0 + row0 + r, :], kvst[kb][0:r, :], [("kvst", kb)], [], is_out=True)
                else:
                    dma(kdst[krow0 + row0:krow0 + row0 + r, :], kvst[kb][0:r, :], [("kvst", kb)], [], is_out=True)
                    ps_slot = si % N_SS
                    kb16 = Pt_all[:, ps_slot, :]
                    copy_op("dve", kb16[0:r, :], ps, btok(bk), [("P", ps_slot)])

                    def _tr(kb16=kb16, ps_slot=ps_slot, row0=row0, r=r):
                        bt = proj_alloc()
                        psb = banks[bt].bitcast(BF16)
                        for h in range(4):
                            transpose(psb[:, h * 128:h * 128 + r], kb16[0:r, h * 128:(h + 1) * 128], ident[:r, :r],
                                      [("P", ps_slot), ("ident",)], btok(bt))
                        copy_op("act", kT_sb[:, :, kcol0 + row0:kcol0 + row0 + r],
                                psb[:, 0:512].rearrange("p (h t) -> p h t", h=4)[:, :, 0:r],
                                btok(bt), [tk for h in range(4) for tk in khs[h]])
                        bank_free(bt)
                    k_tr.append(_tr)
                bank_free(bk)
        if ktr_out is not None:
            ktr_out.extend(k_tr)
        else:
            for f_ in k_tr:
                f_()

    def conv_halo(Tt):
        for cc in range(4):
            copy_op("dve", cu[:, cc, 0:2], cu[:, cc, Tt:Tt + 2], [("cu", cc)], [("cu", cc)])

    def attention(Tt, keytiles, mb=0, hooks=None, uhooks=None, ahead=3):
        units = [(h, kt) for h in range(4) for kt in range(len(keytiles))]
        state = {}

        sc = state.setdefault("sc", [0])

        def pairable(u):
            if not MERGE_EXP or Tt != T or u + 1 >= len(units):
                return False
            (h0, k0), (h1, k1) = units[u], units[u + 1]
            A, B = keytiles[k0], keytiles[k1]
            ok = lambda K: K["near"] is None and K["nk"] == 128 and K["qa"] == 0 and K["qb"] == Tt
            return h0 == h1 and ok(A) and ok(B)

        def emit_S(u):
            h, ki = units[u]
            K = keytiles[ki]
            nk, qa, qb = K["nk"], K["qa"], K["qb"]
            role = state.get(("role", u))
            if role is None:
                if pairable(u) and ("role", u + 1) not in state:
                    state[("role", u)] = ("first", sc[0] % N_SS)
                    state[("role", u + 1)] = ("second", (sc[0] + 1) % N_SS)
                    sc[0] += 2
                else:
                    state[("role", u)] = ("single", sc[0] % N_SS)
                    sc[0] += 1
                role = state[("role", u)]
            kind, slot = role
            bk = slot
            ps3 = banks[bk].rearrange("p (s f) -> p s f", s=2)
            if qa == 0 and qb == Tt:
                mm(ps3[0:nk, :, qa:qb], kT_sb[:, h, K["kc0"]:K["kc0"] + nk], qz[:, h, :, qa:qb], True, True,
                   K["ktoks"](h) + [("qz", h, 0), ("qz", h, 1)], btok(bk))
            else:
                for s_ in range(2):
                    mm(ps3[0:nk, s_, qa:qb], kT_sb[:, h, K["kc0"]:K["kc0"] + nk], qz[:, h, s_, qa:qb], True, True,
                       K["ktoks"](h) + [("qz", h, 0), ("qz", h, 1)], btok(bk))
            if K["near"] is not None:
                na, nb, b0c = K["near"]
                P.add("dve", lambda e, o=ps3[0:nk, :, na:nb], b=B0[h][0:nk, b0c:b0c + (nb - na)].unsqueeze(1).to_broadcast(
                    [nk, 2, nb - na]): e.tensor_tensor(out=o, in0=o, in1=b, op=ALU.add),
                    btok(bk) + [("B0", h)], btok(bk))
            pi = slot
            if kind == "single":
                act(Pt[pi][0:nk, :, qa:qb], ps3[0:nk, :, qa:qb], AF.Exp, btok(bk) + [("rb",)], [("P", pi)],
                    bias=rbf[0:nk, h:h + 1])
            elif kind == "second":
                other = (slot - 1) % N_SS
                lo_, hi_ = min(slot, other), max(slot, other)
                st_ = hi_ - lo_
                act(Pt_all[:, lo_:hi_ + 1:st_, :], ps_all[:].rearrange("p (b f) -> p b f", f=512)[:, lo_:hi_ + 1:st_, :],
                    AF.Exp, btok(lo_) + btok(hi_) + [("rb",)], [("P", lo_), ("P", hi_)], bias=rbf[:, h:h + 1])
            state[u] = pi

        def emit_PV(u):
            h, ki = units[u]
            K = keytiles[ki]
            nk, qa, qb = K["nk"], K["qa"], K["qb"]
            pi = state.pop(u)
            first = ki == 0
            last = ki == len(keytiles) - 1
            if first:
                state["O"] = bank_alloc()
                state["L"] = bank_alloc()
            bo, bl = state["O"], state["L"]
            o3 = banks[bo][:].rearrange("p (s f) -> p s f", s=2)
            l3 = banks[bl][:].rearrange("p (s f) -> p s f", s=2)
            if qa == 0 and qb == Tt:
                mm(o3[:, :, qa:qb], v_sb[0:nk, K["vt"], h * 128:(h + 1) * 128], Pt[pi][0:nk, :, qa:qb], first, last,
                   [("v", K["vt"]), ("P", pi)], btok(bo), skip_group_check=True)
                mm(l3[:, :, qa:qb], ones[0:nk, :], Pt[pi][0:nk, :, qa:qb], first, last,
                   [("ones",), ("P", pi)], btok(bl), skip_group_check=True)
            else:
                assert not first
                for s_ in range(2):
                    mm(o3[:, s_, qa:qb], v_sb[0:nk, K["vt"], h * 128:(h + 1) * 128], Pt[pi][0:nk, s_, qa:qb], False,
                       last and s_ == 1, [("v", K["vt"]), ("P", pi)], btok(bo), skip_group_check=True)
                for s_ in range(2):
                    mm(l3[:, s_, qa:qb], ones[0:nk, :], Pt[pi][0:nk, s_, qa:qb], False, last and s_ == 1,
                       [("ones",), ("P", pi)], btok(bl), skip_group_check=True)
            if last:
                combine_a(h, bo, bl)
                state['cb_at'] = u + DEFER

        pending = []

        def combine_a(h, bo, bl):
            flush_pending()
            o3 = banks[bo][:].rearrange("p (s f) -> p s f", s=2)
            l3 = banks[bl][:].rearrange("p (s f) -> p s f", s=2)
            copy_op("dve", rl[:, :, 0:Tt], l3[:, :, 0:Tt], btok(bl), [("rl", 0), ("rl", 1)])
            bank_free(bl)
            tt_op("dve", ob[:, 0:Tt], o3[:, 0, 0:Tt], rl[:, 1, 0:Tt], ALU.mult, btok(bo) + [("rl", 1)], [("ob",)])
            tt_op("dve", tt[:, 0:Tt], o3[:, 1, 0:Tt], rl[:, 0, 0:Tt], ALU.mult, btok(bo) + [("rl", 0)], [("tt",)])
            bank_free(bo)
            stt_op("dve", ob[:, 0:Tt], tt[:, 0:Tt], neglam[:, 0:1], ob[:, 0:Tt], ALU.mult, ALU.add,
                   [("tt",), ("ob",), ("neglam",)], [("ob",)])
            tt_op("dve", sq[:, 0:Tt], ob[:, 0:Tt], ob[:, 0:Tt], ALU.mult, [("ob",)], [("sq",)])
            tt_op("dve", rs[:, 0:Tt], rl[:, 0, 0:Tt], rl[:, 1, 0:Tt], ALU.mult, [("rl", 0), ("rl", 1)], [("rs",)])
            stt_op("dve", rs[:, 0:Tt], rs[:, 0:Tt], EPS, rs[:, 0:Tt], ALU.mult, ALU.mult, [("rs",)], [("rs",)])
            pending.append(h)

        pending2 = []

        def combine_b(h):
            bm = bank_alloc()
            mm(banks[bm][:, 0:Tt], ones[:, :], sq[:, 0:Tt], True, True, [("ones",), ("sq",)], btok(bm))
            stt_op("dve", rs[:, 0:Tt], banks[bm][:, 0:Tt], 1.0 / 128.0, rs[:, 0:Tt], ALU.mult, ALU.add,
                   btok(bm) + [("rs",)], [("rs",)])
            bank_free(bm)
            pending2.append(h)

        def combine_c_act(h):
            act(tt[:, 0:Tt], rs[:, 0:Tt], AF.Ln, [("rs",)], [("tt",)])
            act(rs[:, 0:Tt], tt[:, 0:Tt], AF.Exp, [("tt",)], [("rs",)], scale=-0.5)

        def combine_c_dve(h):
            stt_op("dve", tt[:, 0:Tt], ob[:, 0:Tt], gsc[:, 0:1], rs[:, 0:Tt], ALU.mult, ALU.mult,
                   [("ob",), ("gsc",), ("rs",)], [("tt",)])
            tt_op("dve", mrgs[mb][:, h, 0:Tt], tt[:, 0:Tt], sza[:, h, 0:Tt], ALU.mult, [("tt",), ("sza", h)],
                  [("mrg", mb, h)])

        def combine_c(h):
            combine_c_act(h)
            combine_c_dve(h)

        def flush_b():
            while pending:
                combine_b(pending.pop(0))

        def flush_c():
            while pending2:
                combine_c(pending2.pop(0))

        def flush_pending():
            flush_b()
            flush_c()

        AHEAD = ahead
        npu = len(keytiles)
        DEFER = min(6, max(1, npu - 2))
        DEFER2 = 2 if npu >= 6 else 1
        n = len(units)
        for u in range(min(AHEAD, n)):
            emit_S(u)
        for u in range(n):
            if u + AHEAD < n:
                emit_S(u + AHEAD)
            emit_PV(u)
            if pending2 and u >= state.get('cc_at', 0):
                flush_c()
            if pending and u >= state.get('cb_at', 0):
                flush_b()
                state['cc_at'] = u + DEFER2
            if uhooks and u in uhooks:
                uhooks.pop(u)()
            if hooks and (u + 1) % len(keytiles) == 0 and ((u + 1) // len(keytiles) - 1) in hooks:
                hooks[(u + 1) // len(keytiles) - 1]()
        if uhooks:
            for k_ in sorted(uhooks):
                uhooks[k_]()

        class Fin:
            def __call__(self):
                flush_pending()

            def b(self):
                flush_c()
                flush_b()

            def c_act(self):
                self.hs = list(pending2)
                for h in self.hs:
                    combine_c_act(h)

            def c_dve(self):
                for h in self.hs:
                    combine_c_dve(h)
                del pending2[:]
        return Fin()

    def final_load(x_src, r, b=None):
        if b is None:
            b = next_big()
        dma(big[b][:r, :], x_src, [], [("big", b)])
        return b

    def final_pe(r, row0, mb, b):
        for half in range(2):
            bk = proj_alloc()
            ps = banks[bk][0:r, :]
            for kc in range(8):
                mm(ps, mrgs[mb][:, kc, row0:row0 + r], w_out_sb[:, kc, half * 512:(half + 1) * 512], kc == 0, kc == 7,
                   [("mrg", mb, kc), ("w_out", kc, half)], btok(bk))
            tt_op("dve", big[b][:r, half * 512:(half + 1) * 512], ps, big[b][:r, half * 512:(half + 1) * 512], ALU.add,
                  btok(bk) + [("big", b)], [("big", b)])
            bank_free(bk)

    def final_post1(r, xi, b):
        c0 = 8 + 4 * xi
        P.add("dve", lambda e, o=xns[xi][:r, :], i=big[b][:r, :], a=stat[:r, c0:c0 + 1]: e.scalar_tensor_tensor(
            out=o, in0=i, scalar=1.0, in1=i, op0=ALU.mult, op1=ALU.mult, accum_out=a),
            [("big", b)], [("xn", xi), ("fstat", xi, 0)])

    def final_post2(r, xi, b):
        c0 = 8 + 4 * xi
        act(stat[:r, c0 + 1:c0 + 2], stat[:r, c0:c0 + 1], AF.Ln, [("fstat", xi, 0), ("eps",)], [("fstat", xi, 1)],
            scale=1.0 / D, bias=epsc[:r, 0:1])
        act(stat[:r, c0 + 2:c0 + 3], stat[:r, c0 + 1:c0 + 2], AF.Exp, [("fstat", xi, 1)], [("fstat", xi, 2)], scale=-0.5)

    def final_post3(y_dst, r, xi, b):
        c0 = 8 + 4 * xi
        stt_op("dve", big[b][:r, :], big[b][:r, :], stat[:r, c0 + 2:c0 + 3], fg[:r, :], ALU.mult, ALU.mult,
               [("big", b), ("fstat", xi, 2), ("fg",)], [("big", b)])
        dma(y_dst, big[b][:r, :], [("big", b)], [], is_out=True)

    def final_post(y_dst, r, xi, b):
        c0 = 8 + 4 * xi
        act(xns[xi][:r, :], big[b][:r, :], AF.Square, [("big", b)], [("xn", xi), ("fstat", xi, 0)],
            accum_out=stat[:r, c0:c0 + 1])
        act(stat[:r, c0 + 1:c0 + 2], stat[:r, c0:c0 + 1], AF.Ln, [("fstat", xi, 0), ("eps",)], [("fstat", xi, 1)],
            scale=1.0 / D, bias=epsc[:r, 0:1])
        act(stat[:r, c0 + 2:c0 + 3], stat[:r, c0 + 1:c0 + 2], AF.Exp, [("fstat", xi, 1)], [("fstat", xi, 2)], scale=-0.5)
        stt_op("dve", big[b][:r, :], big[b][:r, :], stat[:r, c0 + 2:c0 + 3], fg[:r, :], ALU.mult, ALU.mult,
               [("big", b), ("fstat", xi, 2), ("fg",)], [("big", b)])
        dma(y_dst, big[b][:r, :], [("big", b)], [], is_out=True)

    def final_subtile(x_src, y_dst, r, row0, mb=0, xi=0, b=None):
        if b is None:
            b = final_load(x_src, r)
        final_pe(r, row0, mb, b)
        final_post(y_dst, r, xi, b)

    def load_w_out():
        for kc in range(8):
            for half in range(2):
                kb = next_kv()
                dma(kvst[kb][:], w_out[kc * 128:(kc + 1) * 128, half * 512:(half + 1) * 512], [], [("kvst", kb)])
                copy_op("act" if half == 0 else "dve", w_out_sb[:, kc, half * 512:(half + 1) * 512], kvst[kb][:],
                        [("kvst", kb)], [("w_out", kc, half)])

    def kT_tok_fn(blk):
        return lambda h: [("kT", h, blk)]

    def finish():
        P.emit(nc, es)
        es.close()
        return nc

    if STAGE == 0:
        return finish()
    early = {}

    def sample_main():
        for jj in range(2):
            dma(cu[:, :, jj], sconv[jj, :].rearrange("(cc p) -> p cc", p=128), [], [("cu", cc) for cc in range(4)])
        norm_subtile(x_s[:, :], DEC, 0, DEC)
        norm_a(meta[:, :], NMETA, 1)
        early["meta"] = True
        project_tile(DEC, [(0, DEC)], NCACHE, ntile_c, 0, True, True, k_s, v_s, khs=[kT_all(h) for h in range(4)])
        for jj in range(2):
            dma(conv_s[jj, :].rearrange("(cc p) -> p cc", p=128), cu[:, :, DEC + jj],
                [("cu", cc) for cc in range(4)], [], is_out=True)
        kts = []
        for t in range(ntile_c):
            r = min(128, NCACHE - t * 128)
            if t == 7:
                near = (0, DEC, 144)
            elif t == 8:
                near = (0, DEC, 16)
            else:
                near = None
            kts.append(dict(nk=r, kc0=t * 128, ktoks=kT_all, vt=t, qa=0, qb=DEC, near=near))
        kts.append(dict(nk=DEC, kc0=NCACHE, ktoks=kT_all, vt=ntile_c, qa=0, qb=DEC, near=(0, DEC, 0)))
        attention(DEC, kts, ahead=3)()
        final_subtile(x_s[:, :], y_s[:, :], DEC, 0)

    if STAGE is None or STAGE >= 2:
        sample_main()
    if not early.get("meta"):
        norm_a(meta[:, :], NMETA, 1)
    norm_b(NMETA, 0, 1)
    if STAGE is None or STAGE >= 2:
        for si_ in range(2):
            norm_a(x_p[si_ * 128:(si_ + 1) * 128, :], 128, si_)
        early["t0"] = True
    if STAGE == 0.5:
        return finish()
    project_tile(NMETA, [(0, NMETA)], 0, 0, 0, False, False, k_p, v_p, khs=[[("kT", h, "m")] for h in range(4)])
    if STAGE in (0.6, 0.7, 0.8):
        return finish()
    conv_halo(NMETA)

    if STAGE == 1:
        return finish()
    ntl = NT if N_PROMPT_TILES is None else N_PROMPT_TILES

    def tile_norm_a(j):
        for si in range(2):
            norm_a(x_p[j * T + si * 128:j * T + (si + 1) * 128, :], 128, si)

    def tile_norm_b(j):
        for si in range(2):
            norm_b(128, si * 128, si)

    def tile_project(j, mid=None, mid2=None, ph=None, ktr_out=None):
        t0 = j * T
        project_tile(T, [(0, 128), (128, 128)], NMETA + t0, 1 + 2 * j, NMETA + t0, True, True, k_p, v_p,
                     khs=[[("kT", h, 2 * j), ("kT", h, 2 * j + 1)] for h in range(4)], mb=j % 2, mid=mid, mid2=mid2, ph=ph, ktr_out=ktr_out)
        if j == NT - 1:
            for jj in range(2):
                dma(conv_p[jj, :].rearrange("(cc p) -> p cc", p=128), cu[:, :, T + jj],
                    [("cu", cc) for cc in range(4)], [], is_out=True)
        conv_halo(T)

    def tile_keytiles(j):
        kts = []
        mnear = (0, 240, 16) if j == 0 else None
        kts.append(dict(nk=NMETA, kc0=0, ktoks=kT_tok_fn("m"), vt=0, qa=0, qb=T, near=mnear))
        for kt in range(2 * j + 2):
            i = kt - 2 * j
            if i == 1:
                qa, near = 128, (128, 256, 0)
            elif i == 0:
                qa, near = 0, (0, 256, 0)
            elif i == -1:
                qa, near = 0, (0, 128, 128)
            else:
                qa, near = 0, None
            kts.append(dict(nk=128, kc0=NMETA + kt * 128, ktoks=kT_tok_fn(kt), vt=1 + kt, qa=qa, qb=T, near=near))
        return kts

    NB = 2

    def xrows(jj, si):
        return x_p[jj * T + si * 128:jj * T + (si + 1) * 128, :]

    def norm_ph(jj, ph):
        if jj >= ntl:
            return
        ph.setdefault(0, []).append(lambda: norm_load(xrows(jj, 0), 128, b=NB))
        ph.setdefault(1, []).insert(1 if ph.get(1) else 0, lambda: norm_a_act(128, 0, NB))
        ph.setdefault(3, []).insert(0, lambda: norm_a_dve(128, 0, NB))
        ph.setdefault(3, []).insert(1, lambda: norm_load(xrows(jj, 1), 128, b=NB))
        ph.setdefault(7, []).append(lambda: norm_a_act(128, 1, NB))
        ph.setdefault(7, []).append(lambda: norm_a_dve(128, 1, NB))

    if not early.get("t0"):
        tile_norm_a(0)
    tile_norm_b(0)
    ph = {}
    norm_ph(1, ph)
    tile_project(0, ph=ph)
    post = []
    for j in range(ntl):
        t0 = j * T
        nxt = j + 1 < ntl
        uhooks = {}
        if nxt:
            uhooks[0] = (lambda jj=j + 1: tile_norm_b(jj))
        for i_, pf in enumerate(post):
            uhooks[1 + i_] = pf
        post = []
        fin = attention(T, tile_keytiles(j), mb=j % 2, hooks=None, uhooks=uhooks)
        fb = [final_load(xrows(j, si), 128, b=si) for si in range(2)]
        if nxt:
            ph = {1: [fin.b, fin.c_act], 3: [fin.c_dve]}
            norm_ph(j + 2, ph)
            ktr = []
            tile_project(j + 1, ph=ph, ktr_out=ktr)
        else:
            ktr = []
            fin()
        for si in range(2):
            final_pe(128, si * 128, j % 2, fb[si])
        for f_ in ktr:
            f_()
        yd = lambda si, t0=t0: y_p[t0 + si * 128:t0 + (si + 1) * 128, :]
        post = [lambda b=fb[0]: final_post1(128, 0, b),
                lambda b=fb[1]: final_post1(128, 1, b),
                lambda b=fb[0]: final_post2(128, 0, b),
                lambda b=fb[1]: final_post2(128, 1, b),
                lambda b=fb[0], yd=yd: final_post3(yd(0), 128, 0, b),
                lambda b=fb[1], yd=yd: final_post3(yd(1), 128, 1, b)]
    for pf in post:
        pf()

    return finish()


_NC_CACHE = {}


def kernel(x_prompt, x_sample, cache_k, cache_v, state_conv, meta_tokens, rel_bias, norm_g, w_in,
           conv_w, lambda_q1, lambda_k1, lambda_q2, lambda_k2, subln_g, w_out, final_g):
    f = lambda a: np.ascontiguousarray(np.asarray(a, dtype=np.float32))
    if "nc" not in _NC_CACHE:
        _NC_CACHE["nc"] = build_nc()
    nc = _NC_CACHE["nc"]
    x_prompt = f(x_prompt); x_sample = f(x_sample)
    cache_k = f(cache_k); cache_v = f(cache_v); state_conv = f(state_conv)
    shared = {
        "meta": f(meta_tokens), "relb": f(rel_bias), "norm_g": f(norm_g).reshape(D),
        "w_in": f(w_in).reshape(D, 4096), "conv_w": f(conv_w).reshape(3, 512),
        "lq1": f(lambda_q1).reshape(64), "lk1": f(lambda_k1).reshape(64),
        "lq2": f(lambda_q2).reshape(64), "lk2": f(lambda_k2).reshape(64),
        "subln_g": f(subln_g).reshape(128), "w_out": f(w_out).reshape(D, D), "final_g": f(final_g).reshape(D),
    }
    in_maps = []
    for c in range(N_CORES):
        m = dict(shared)
        m["x_p"] = x_prompt[c]
        m["x_s"] = x_sample[c]
        m["ck"] = cache_k[0, c].reshape(NCACHE, 512)
        m["cv"] = cache_v[0, c].reshape(NCACHE, 512)
        m["sconv"] = state_conv[0, c]
        in_maps.append(m)
    res = run_bass_kernel_spmd(nc, in_maps, core_ids=list(range(N_CORES)))
    R = res.results
    st = lambda k: np.stack([np.asarray(R[c][k], dtype=np.float32) for c in range(N_CORES)])
    y_prompt = st("y_p")
    y_sample = st("y_s")
    k_prompt = st("k_p").reshape(1, N_CORES, NMETA + SEQ, 4, 128)
    v_prompt = st("v_p").reshape(1, N_CORES, NMETA + SEQ, 4, 128)
    conv_prompt = st("conv_p").reshape(1, N_CORES, 2, 512)
    k_sample = st("k_s").reshape(1, N_CORES, DEC, 4, 128)
    v_sample = st("v_s").reshape(1, N_CORES, DEC, 4, 128)
    conv_sample = st("conv_s").reshape(1, N_CORES, 2, 512)
    return (y_prompt, y_sample, k_prompt, v_prompt, conv_prompt, k_sample, v_sample, conv_sample)
```

```python
import numpy as np
from contextlib import ExitStack
import concourse.bass as bass
import concourse.mybir as mybir
from concourse.bass_utils import run_bass_kernel_spmd

F32 = mybir.dt.float32
BF16 = mybir.dt.bfloat16
ALU = mybir.AluOpType
AF = mybir.ActivationFunctionType
AX = mybir.AxisListType

N_CORES = 8
SEQ = 4096
D = 1024
T = 256
NT = SEQ // T
NMETA = 16
PAST = 1024
NCACHE = NMETA + PAST
DEC = 32
EPS = 1e-6
LAM_INIT = 0.2
MASKV = -30000.0
N_DMA_SEMS = 40
STAGE = None
N_PROMPT_TILES = None
SETUP_MASK = 0xFF
FLAG_ALL = False
CACHE_DMA_ENG = "pool"
MERGE_EXP = False

BIAS_STEPS = [(-90, 14), (-63, 13), (-45, 12), (-31, 11), (-22, 10), (-15, 9), (-11, 8),
              (-7, 7), (-6, 6), (-5, 5), (-4, 4), (-3, 3), (-2, 2), (-1, 1), (0, 0),
              (1, 17), (2, 18), (3, 19), (4, 20), (5, 21), (6, 22), (7, 23), (8, 24),
              (12, 25), (16, 26), (23, 27), (32, 28), (46, 29)]


class Op:
    __slots__ = ("eng", "fn", "reads", "writes", "dma", "deps", "flag", "ev", "is_out")

    def __init__(self, eng, fn, reads, writes, dma, is_out):
        self.eng = eng
        self.fn = fn
        self.reads = reads
        self.writes = writes
        self.dma = dma
        self.deps = ()
        self.flag = False
        self.ev = None
        self.is_out = is_out


class Prog:
    ENGS = ("pe", "act", "dve", "pool", "sp")

    def __init__(self):
        self.ops = []
        self.group = 0xFF

    def add(self, eng, fn, reads=(), writes=(), dma=False, is_out=False):
        if not (SETUP_MASK & self.group):
            return
        writes = tuple(writes) + tuple(t for t in reads if t[0] == "bank" and t not in writes)
        self.ops.append(Op(eng, fn, tuple(reads), tuple(writes), dma, is_out))

    def analyze(self):
        last_w = {}
        readers = {}
        ops = self.ops
        for i, op in enumerate(ops):
            raw = set()
            other = set()
            for t in op.reads:
                w = last_w.get(t)
                if w is not None:
                    raw.add(w)
            for t in op.writes:
                w = last_w.get(t)
                if w is not None:
                    other.add(w)
                for r in readers.get(t, ()):
                    other.add(r)
            deps = set()
            for d in raw | other:
                if d == i:
                    continue
                dop = ops[d]
                if (not dop.dma) and (not op.dma) and dop.eng == op.eng:
                    if op.eng == "pe":
                        continue
                    if d not in raw:
                        continue
                deps.add(d)
            op.deps = tuple(sorted(deps))
            for d in deps:
                ops[d].flag = True
            if FLAG_ALL and not op.dma:
                op.flag = True
            for t in op.writes:
                last_w[t] = i
                readers[t] = []
            for t in op.reads:
                readers.setdefault(t, []).append(i)

    def emit(self, nc, es):
        ops = self.ops
        self.analyze()
        last_ops = []
        for e in ("pe", "act", "dve", "pool"):
            lst = [op for op in ops if op.eng == e and not op.dma]
            if lst:
                lst[-1].flag = True
                last_ops.append(lst[-1])
        sems = {e: es.enter_context(nc.semaphore("s_" + e)) for e in ("pe", "act", "dve", "pool")}
        dsems = [es.enter_context(nc.semaphore("d%d" % i)) for i in range(N_DMA_SEMS)]
        cnt = {e: 0 for e in sems}
        dcnt = [0] * N_DMA_SEMS
        dlast = [None] * N_DMA_SEMS
        nd = 0
        for i, op in enumerate(ops):
            if op.dma:
                s = nd % N_DMA_SEMS
                nd += 1
                if dlast[s] is not None:
                    op.deps = tuple(sorted(set(op.deps) | {dlast[s]}))
                dcnt[s] += 16
                op.ev = (dsems[s], dcnt[s], ("d", s))
                dlast[s] = i
                op.flag = True
            elif op.flag:
                cnt[op.eng] += 1
                op.ev = (sems[op.eng], cnt[op.eng], op.eng)
        block = es.enter_context(nc.Block())
        per_eng = {e: [op for op in ops if op.eng == e] for e in self.ENGS}
        out_events = [op.ev for op in ops if op.is_out] + [op.ev for op in last_ops]

        import os
        dump = open(os.environ["KDUMP"], "w") if os.environ.get("KDUMP") else None

        def run(eng_obj, lst, final_events=()):
            waited = {}
            for op in lst:
                if dump:
                    dump.write("%s #%d deps=%s flag=%s ev=%s r=%s w=%s\n" % (
                        op.eng, ops.index(op), [(ops[d].ev[2], ops[d].ev[1]) for d in op.deps], op.flag,
                        (op.ev[2], op.ev[1]) if op.ev else None, op.reads[:4], op.writes[:4]))
                need = {}
                for d in op.deps:
                    sem, val, key = ops[d].ev
                    if waited.get(key, 0) >= val:
                        continue
                    if key not in need or need[key][1] < val:
                        need[key] = (sem, val)
                for key, (sem, val) in need.items():
                    eng_obj.wait_ge(sem, val)
                    waited[key] = val
                ins = op.fn(eng_obj)
                if op.flag:
                    ins.then_inc(op.ev[0], 16 if op.dma else 1)
            need = {}
            for (sem, val, key) in final_events:
                if waited.get(key, 0) >= val:
                    continue
                if key not in need or need[key][1] < val:
                    need[key] = (sem, val)
            for key, (sem, val) in need.items():
                eng_obj.wait_ge(sem, val)

        @block.sync
        def _(e):
            run(e, per_eng["sp"], out_events)

        @block.tensor
        def _(e):
            run(e, per_eng["pe"])

        @block.scalar
        def _(e):
            run(e, per_eng["act"])

        @block.vector
        def _(e):
            run(e, per_eng["dve"])

        @block.gpsimd
        def _(e):
            run(e, per_eng["pool"])


def build_nc():
    nc = bass.Bass("TRN2", target_bir_lowering=False)
    es = ExitStack()
    P = Prog()

    def din(name, shape):
        return nc.dram_tensor(name, shape, F32, kind="ExternalInput").ap()

    def dout(name, shape):
        return nc.dram_tensor(name, shape, F32, kind="ExternalOutput").ap()

    x_p = din("x_p", [SEQ, D])
    x_s = din("x_s", [DEC, D])
    ck = din("ck", [NCACHE, 512])
    cv = din("cv", [NCACHE, 512])
    sconv = din("sconv", [2, 512])
    meta = din("meta", [NMETA, D])
    relb = din("relb", [32, 4])
    norm_g = din("norm_g", [D])
    w_in = din("w_in", [D, 4096])
    conv_w = din("conv_w", [3, 512])
    lam_in = [din(n, [64]) for n in ("lq1", "lk1", "lq2", "lk2")]
    subln_g = din("subln_g", [128])
    w_out = din("w_out", [D, D])
    final_g = din("final_g", [D])
    y_p = dout("y_p", [SEQ, D])
    y_s = dout("y_s", [DEC, D])
    k_p = dout("k_p", [NMETA + SEQ, 512])
    v_p = dout("v_p", [NMETA + SEQ, 512])
    conv_p = dout("conv_p", [2, 512])
    k_s = dout("k_s", [DEC, 512])
    v_s = dout("v_s", [DEC, 512])
    conv_s = dout("conv_s", [2, 512])

    def sb(name, shape, dt=F32):
        return es.enter_context(nc.sbuf_tensor(name, shape, dt))

    w_in_sb = sb("w_in_sb", [128, 8, 4096], BF16)
    w_out_sb = sb("w_out_sb", [128, 8, D], BF16)
    kT_sb = sb("kT_sb", [128, 4, NMETA + SEQ], BF16)
    v_sb = sb("v_sb", [128, 33, 512], BF16)
    ident = sb("ident", [128, 128], BF16)
    ones = sb("ones", [128, 128], BF16)
    epsc = sb("epsc", [128, 1])
    g_sb = sb("g_sb", [128, 8])
    fg = sb("fg", [128, D])
    cw = sb("cw", [128, 3, 4])
    sg = sb("sg", [128, 1])
    gsc = sb("gsc", [128, 1])
    lsum = sb("lsum", [128, 2])
    lexp = sb("lexp", [128, 2])
    ldif = sb("ldif", [128, 1])
    neglam = sb("neglam", [128, 1])
    rbf = sb("rbf", [128, 4])
    B0 = [sb("B0_%d" % h, [128, 256], BF16) for h in range(4)]
    NBIG = 3
    big = [sb("big%d" % i, [128, D]) for i in range(NBIG)]
    xns = [sb("xn%d" % i, [128, D], BF16) for i in range(2)]
    xn = xns[0]
    nT = sb("nT", [128, 8, T], BF16)
    qz = sb("qz", [128, 4, 2, T], BF16)
    sza = sb("sza", [128, 4, T], BF16)
    mrgs = [sb("mrg%d" % i, [128, 8, T], BF16) for i in range(2)]
    NKV = 2
    kvst = [sb("kvst%d" % i, [128, 512]) for i in range(NKV)]
    cu = sb("cu", [128, 4, T + 2])
    cg = sb("cg", [128, T])
    szc = sb("szc", [128, T])
    ct1 = sb("ct1", [128, T])
    NP = 4
    N_SS = 4
    Pt_all = sb("Pt_all", [128, N_SS, 2 * T], BF16)
    Pt = [Pt_all[:, i, :].rearrange("p (s f) -> p s f", s=2) for i in range(N_SS)]
    rl = sb("rl", [128, 2, T])
    ob = sb("ob", [128, T])
    sq = sb("sq", [128, T], BF16)
    rs = sb("rs", [128, T])
    tt = sb("tt", [128, T])
    stat = sb("stat", [128, 16])
    junk = xn
    ct2 = sb("ct2", [128, T])
    rb_sb = ct2[:, 0:128]
    dl = ob[:, 0:len(BIAS_STEPS) * 4].rearrange("p (a b) -> p a b", b=4)
    sza_f = sza[:].rearrange("p a b -> p (a b)").bitcast(F32)
    lam4 = sza_f[:, 0:256].rearrange("p (a b) -> p a b", a=4)
    lprod = sza_f[:, 256:384].rearrange("p (a b) -> p a b", a=2)
    Rm = cg
    stepm = [szc, ct1]
    stepm_tok = [("szc",), ("ct1",)]
    ps_all = es.enter_context(nc.psum_tensor("ps_all", [128, 4096], F32))
    banks = [ps_all[:, i * 512:(i + 1) * 512] for i in range(8)]

    N_SSLOT = 4
    free_banks = list(range(N_SSLOT, 8))

    def bank_alloc():
        return free_banks.pop(0)

    def bank_free(b):
        if b in proj_taken:
            proj_taken.discard(b)
            return
        free_banks.append(b)

    proj_rr = [0]
    proj_order = [4, 5, 6, 7, 0, 1, 2, 3]
    proj_taken = set()

    def proj_alloc():
        b = proj_order[proj_rr[0] % 8]
        proj_rr[0] += 1
        proj_taken.add(b)
        return b

    def btok(b):
        return [("bank", b, 0), ("bank", b, 1)]

    rr = {"big": 0, "kv": 0, "P": 0}

    def next_big():
        i = rr["big"] % NBIG
        rr["big"] += 1
        return i

    def next_kv():
        i = rr["kv"] % NKV
        rr["kv"] += 1
        return i

    def next_P():
        i = rr["P"] % NP
        rr["P"] += 1
        return i

    def dma(out, in_, reads, writes, is_out=False, eng="sp"):
        P.add(eng, lambda e, o=out, i=in_: e.dma_start(out=o, in_=i, allow_slow_non_contiguous=True),
              reads, writes, dma=True, is_out=is_out)

    def act(out, in_, func, reads, writes, **kw):
        P.add("act", lambda e, o=out, i=in_, f=func, k=kw: e.activation(out=o, in_=i, func=f, **k),
              reads, writes)

    def tt_op(eng, out, in0, in1, op, reads, writes):
        P.add(eng, lambda e, o=out, a=in0, b=in1, p=op: e.tensor_tensor(out=o, in0=a, in1=b, op=p),
              reads, writes)

    def ts_op(eng, out, in0, s1, s2, op0, op1, reads, writes):
        if s2 is None:
            P.add(eng, lambda e, o=out, a=in0, x=s1, p=op0: e.tensor_scalar(out=o, in0=a, scalar1=x, scalar2=None, op0=p),
                  reads, writes)
        else:
            P.add(eng, lambda e, o=out, a=in0, x=s1, y=s2, p=op0, q=op1: e.tensor_scalar(
                out=o, in0=a, scalar1=x, scalar2=y, op0=p, op1=q), reads, writes)

    def stt_op(eng, out, in0, scalar, in1, op0, op1, reads, writes):
        P.add(eng, lambda e, o=out, a=in0, s=scalar, b=in1, p=op0, q=op1: e.scalar_tensor_tensor(
            out=o, in0=a, scalar=s, in1=b, op0=p, op1=q), reads, writes)

    def copy_op(eng, out, in_, reads, writes):
        if eng == "act":
            P.add("act", lambda e, o=out, i=in_: e.activation(out=o, in_=i, func=AF.Identity), reads, writes)
        else:
            P.add(eng, lambda e, o=out, i=in_: e.tensor_copy(out=o, in_=i), reads, writes)

    def recip(out, in_, reads, writes):
        P.add("dve", lambda e, o=out, i=in_: e.reciprocal(out=o, in_=i), reads, writes)

    def memset(eng, ap, val, writes):
        P.add(eng, lambda e, a=ap, v=val: e.memset(a, v), (), writes)

    def mm(out, lhsT, rhs, start, stop, reads, writes, **kw):
        P.add("pe", lambda e, o=out, l=lhsT, r=rhs, s=start, t=stop, k=kw: e.matmul(
            o, lhsT=l, rhs=r, start=s, stop=t, **k), reads, writes)

    def transpose(out, in_, idn, reads, writes):
        P.add("pe", lambda e, o=out, i=in_, d=idn: e.transpose(o, i, d), reads, writes)

    P.group = 1
    memset("pool", ident[:], 0.0, [("ident",)])
    P.add("pool", lambda e: e.affine_select(out=ident[:], in_=ident[:], compare_op=ALU.not_equal, fill=1.0,
                                            base=0, pattern=[[-1, 128]], channel_multiplier=1),
          [("ident",)], [("ident",)])
    memset("pool", ones[:], 1.0, [("ones",)])
    memset("pool", epsc[:], EPS, [("eps",)])
    memset("pool", qz[:], 0.0, [("qz", h, s_) for h in range(4) for s_ in range(2)])

    dma(g_sb[:], norm_g.rearrange("(kc p) -> p kc", p=128), [], [("g_sb",)])
    P.group = 2
    dma(rb_sb, relb.rearrange("b h -> (b h)").partition_broadcast(128), [], [("ct2",)])
    copy_op("dve", rbf[:], rb_sb[:, 60:64], [("ct2",)], [("rb",)])
    for i in range(4):
        dma(lam4[:, i, :], lam_in[i].partition_broadcast(128), [], [("sza", i)])
    dma(sg[:], subln_g.rearrange("(p o) -> p o", o=1), [], [("sg",)])
    for jj in range(3):
        dma(cw[:, jj, :], conv_w[jj, :].rearrange("(cc p) -> p cc", p=128), [], [("cw", jj)])
    dma(fg[:], final_g.partition_broadcast(128), [], [("fg",)])
    P.group = 4

    kT_all = lambda h: [("kT", h, b) for b in ["m"] + list(range(2 * NT))]
    ntile_c = (NCACHE + 127) // 128
    def cache_tile(t):
        r = min(128, NCACHE - t * 128)
        kb = next_kv()
        xi_ = t % 2
        dma(kvst[kb][0:r, :], ck[t * 128:t * 128 + r, :], [], [("kvst", kb)], eng=CACHE_DMA_ENG)
        copy_op("dve", xns[xi_][0:r, 0:512], kvst[kb][0:r, :], [("kvst", kb)], [("xn", xi_)])
        bk = bank_alloc()
        psb = banks[bk][:].bitcast(BF16)
        for h in range(4):
            transpose(psb[:, h * 128:h * 128 + r], xns[xi_][0:r, h * 128:(h + 1) * 128], ident[:r, :r],
                      [("xn", xi_), ("ident",)], btok(bk))
        copy_op("act", kT_sb[:, :, t * 128:t * 128 + r], psb[:, 0:512].rearrange("p (h t) -> p h t", h=4)[:, :, 0:r],
                btok(bk), [tk for h in range(4) for tk in kT_all(h)])
        bank_free(bk)
        kb = next_kv()
        dma(kvst[kb][0:r, :], cv[t * 128:t * 128 + r, :], [], [("kvst", kb)], eng=CACHE_DMA_ENG)
        copy_op("dve", v_sb[0:r, t, :], kvst[kb][0:r, :], [("kvst", kb)], [("v", t)])

    cache_thunks = [(lambda t=t: cache_tile(t)) for t in range(ntile_c)]
    piece_no = [0]

    def cache_step():
        piece_no[0] += 1
        if piece_no[0] % 4 == 0 and cache_thunks:
            cache_thunks.pop(0)()

    b0_thunks = []
    b0_acc = [(rl[:, 0, :], ("rl", 0)), (rl[:, 1, :], ("rl", 1)), (tt[:], ("tt",)), (rs[:], ("rs",))]

    def _b0_plan():
        b0_thunks.append(lambda: P.add("pool", lambda e: e.iota(Rm[:], pattern=[[-1, 256]], base=0, channel_multiplier=1,
                                                               allow_small_or_imprecise_dtypes=True), [], [("cg",)]))
        prev_b = 15
        for i, (thr, bk) in enumerate(BIAS_STEPS):
            b0_thunks.append(lambda i=i, bk=bk, pb=prev_b: tt_op(
                "dve", dl[:, i, :], rb_sb[:, bk * 4:bk * 4 + 4], rb_sb[:, pb * 4:pb * 4 + 4], ALU.subtract,
                [("ct2",)], [("ob",)]))
            prev_b = bk
        for h in range(4):
            b0_thunks.append(lambda h=h: memset("dve", b0_acc[h][0], 0.0, [b0_acc[h][1]]))
        for i, (thr, bk) in enumerate(BIAS_STEPS):
            sm = stepm[i % 2]
            b0_thunks.append(lambda i=i, sm=sm, thr=thr: P.add(
                "dve", lambda e, o=sm, t=float(thr): e.tensor_single_scalar(out=o[:], in_=Rm[:], scalar=t, op=ALU.is_ge),
                [("cg",)], [stepm_tok[i % 2]]))
            for h in range(4):
                b0_thunks.append(lambda i=i, sm=sm, h=h: stt_op(
                    "dve", b0_acc[h][0], sm[:], dl[:, i, h:h + 1], b0_acc[h][0], ALU.mult, ALU.add,
                    [stepm_tok[i % 2], ("ob",), b0_acc[h][1]], [b0_acc[h][1]]))
        for h in range(4):
            b0_thunks.append(lambda h=h: copy_op("dve", B0[h][:], b0_acc[h][0], [b0_acc[h][1]], [("B0", h)]))
        for h in range(4):
            b0_thunks.append(lambda h=h: memset("dve", B0[h][64:128, 0:64], MASKV, [("B0", h)]))

    _b0_plan()

    def b0_step(n):
        for _ in range(n):
            if b0_thunks:
                b0_thunks.pop(0)()

    stg = [(big[i][:], [("big", i)]) for i in range(NBIG)]
    for m_ in range(2):
        stg.append((mrgs[m_][:].rearrange("p a b -> p (a b)").bitcast(F32), [("mrg", m_, c) for c in range(8)]))
    stg.append((nT[:].rearrange("p a b -> p (a b)").bitcast(F32), [("nT",)]))
    stg_i = [0]

    def next_stg():
        i = stg_i[0] % len(stg)
        stg_i[0] += 1
        return stg[i]

    for q in (0, 2, 3, 1):
        for kc in range(8):
            buf, btk = next_stg()
            dma(buf, w_in[kc * 128:(kc + 1) * 128, q * 1024:(q + 1) * 1024], [], btk)
            act(w_in_sb[:, kc, q * 1024:(q + 1) * 1024], buf[:, :], AF.Copy,
                btk + [("g_sb",)], [("w_in", kc, q, 0), ("w_in", kc, q, 1)], scale=g_sb[:, kc:kc + 1])
            b0_step(4)
            cache_step()
    for kc in range(8):
        buf, btk = next_stg()
        dma(buf, w_out[kc * 128:(kc + 1) * 128, :], [], btk)
        copy_op("act", w_out_sb[:, kc, :], buf[:, :], btk, [("w_out", kc, 0), ("w_out", kc, 1)])
        b0_step(4)
        cache_step()
    b0_step(10000)
    while cache_thunks:
        cache_thunks.pop(0)()

    def w_in_tok(kc, c0, c1):
        toks = set()
        for c in range(c0 // 512, (c1 - 1) // 512 + 1):
            toks.add(("w_in", kc, c // 2, c % 2))
        return list(toks)

    P.group = 8
    tt_op("dve", lprod[:, 0, :], lam4[:, 0, :], lam4[:, 1, :], ALU.mult, [("sza", 0), ("sza", 1)], [("lprod", 0), ("sza", 2)])
    tt_op("dve", lprod[:, 1, :], lam4[:, 2, :], lam4[:, 3, :], ALU.mult, [("sza", 2), ("sza", 3)], [("lprod", 1), ("sza", 2)])
    P.add("dve", lambda e: e.reduce_sum(out=lsum[:], in_=lprod, axis=AX.X),
          [("lprod", 0), ("lprod", 1), ("sza", 2)], [("lsum",)])
    act(lexp[:], lsum[:], AF.Exp, [("lsum",)], [("lexp",)])
    tt_op("dve", ldif[:], lexp[:, 0:1], lexp[:, 1:2], ALU.subtract, [("lexp",)], [("ldif",)])
    ts_op("dve", neglam[:], ldif[:], -1.0, -LAM_INIT, ALU.mult, ALU.add, [("ldif",)], [("neglam",)])
    ts_op("dve", gsc[:], sg[:], 1.0 - LAM_INIT, None, ALU.mult, None, [("sg",)], [("gsc",)])

    P.group = 0xFF
    def norm_load(x_src, r, b=None):
        if b is None:
            b = next_big()
        dma(big[b][:r, :], x_src, [], [("big", b)])
        return b

    def norm_a(x_src, r, xi, b=None):
        if b is None:
            b = norm_load(x_src, r)
        act(xns[xi][:r, :], big[b][:r, :], AF.Square, [("big", b)], [("xn", xi), ("stat", xi, 0)],
            accum_out=stat[:r, 4 * xi:4 * xi + 1])
        act(stat[:r, 4 * xi + 1:4 * xi + 2], stat[:r, 4 * xi:4 * xi + 1], AF.Ln, [("stat", xi, 0), ("eps",)],
            [("stat", xi, 1)], scale=1.0 / D, bias=epsc[:r, 0:1])
        act(stat[:r, 4 * xi + 2:4 * xi + 3], stat[:r, 4 * xi + 1:4 * xi + 2], AF.Exp, [("stat", xi, 1)],
            [("stat", xi, 2)], scale=-0.5)
        ts_op("dve", xns[xi][:r, :], big[b][:r, :], stat[:r, 4 * xi + 2:4 * xi + 3], None, ALU.mult, None,
              [("big", b), ("stat", xi, 2)], [("xn", xi)])

    def norm_a_act(r, xi, b):
        P.add("dve", lambda e, o=xns[xi][:r, :], i=big[b][:r, :], a=stat[:r, 4 * xi:4 * xi + 1]: e.scalar_tensor_tensor(
            out=o, in0=i, scalar=1.0, in1=i, op0=ALU.mult, op1=ALU.mult, accum_out=a),
            [("big", b)], [("xn", xi), ("stat", xi, 0)])
        act(stat[:r, 4 * xi + 1:4 * xi + 2], stat[:r, 4 * xi:4 * xi + 1], AF.Ln, [("stat", xi, 0), ("eps",)],
            [("stat", xi, 1)], scale=1.0 / D, bias=epsc[:r, 0:1])
        act(stat[:r, 4 * xi + 2:4 * xi + 3], stat[:r, 4 * xi + 1:4 * xi + 2], AF.Exp, [("stat", xi, 1)],
            [("stat", xi, 2)], scale=-0.5)

    def norm_a_dve(r, xi, b):
        ts_op("dve", xns[xi][:r, :], big[b][:r, :], stat[:r, 4 * xi + 2:4 * xi + 3], None, ALU.mult, None,
              [("big", b), ("stat", xi, 2)], [("xn", xi)])

    def norm_b(r, row0, xi):
        bk = bank_alloc()
        psb = banks[bk][:].bitcast(BF16)
        for kc in range(8):
            transpose(psb[:, kc * 128:kc * 128 + r], xns[xi][:r, kc * 128:(kc + 1) * 128], ident[:r, :r],
                      [("xn", xi), ("ident",)], btok(bk))
        copy_op("dve", nT[:, :, row0:row0 + r], psb.rearrange("p (k t) -> p k t", k=8)[:, :, 0:r],
                btok(bk), [("nT",)])
        bank_free(bk)

    def norm_subtile(x_src, r, row0, Tt, xi=0):
        norm_a(x_src, r, xi)
        norm_b(r, row0, xi)

    def proj_feature(col0, Tt, evac):
        pass

    class HalfBanks:
        def __init__(self):
            self.cur = None
            self.used = 0

        def get(self):
            if self.cur is None or self.used == 2:
                if self.cur is not None:
                    bank_free(self.cur)
                self.cur = proj_alloc()
                self.used = 0
            h = self.used
            self.used += 1
            return self.cur, h

        def done(self):
            if self.cur is not None:
                bank_free(self.cur)
                self.cur = None
                self.used = 0

    def proj_fm(hb, col0, Tt):
        bk, half = hb.get()
        ps = banks[bk][:, half * 256:half * 256 + Tt]
        tok = btok(bk)
        for kc in range(8):
            mm(ps, w_in_sb[:, kc, col0:col0 + 128], nT[:, kc, 0:Tt], kc == 0, kc == 7,
               [("nT",)] + w_in_tok(kc, col0, col0 + 128), tok)
        return ps, tok

    def project_tile(Tt, subtiles, kcol0, vtile, krow0, want_q, want_conv_out, kdst, vdst, khs=None, mb=0, mid=None, mid2=None, ph=None, ktr_out=None):
        hb = HalfBanks()

        def run_ph(i):
            if ph and i in ph:
                for f_ in ph[i]:
                    f_()

        def pair(colA, evA, colB, evB):
            psA, tokA = proj_fm(hb, colA, Tt)
            psB, tokB = proj_fm(hb, colB, Tt)
            evA(psA, tokA)
            evB(psB, tokB)

        if mid is not None:
            mid()
        run_ph(0)
        if want_q:
            def ev_q(h):
                def f(ps, tok):
                    act(qz[0:64, h, 0, 0:Tt], ps[0:64, :], AF.Copy, [tok[0]], [("qz", h, 0)], scale=0.125)
                    P.add("dve", lambda e, o=qz[64:128, h, 1, 0:Tt], i=ps[64:128, :]: e.tensor_scalar(
                        out=o, in0=i, scalar1=0.125, scalar2=None, op0=ALU.mult), [tok[1]], [("qz", h, 1)])
                return f
            pair(0, ev_q(0), 128, ev_q(1))
            run_ph(0.5)
            pair(256, ev_q(2), 384, ev_q(3))
            run_ph(1)
        for cc in range(4):
            run_ph(2 + cc) if cc > 0 else None

            def ev_cg(ps, tok):
                copy_op("act", cg[:, 0:Tt], ps, tok, [("cg",)])

            def ev_u(ps, tok, cc=cc):
                tt_op("dve", cu[:, cc, 2:2 + Tt], ps, cg[:, 0:Tt], ALU.mult, tok + [("cg",)], [("cu", cc)])
                if want_conv_out:
                    ts_op("dve", ct1[:, 0:Tt], cu[:, cc, 0:Tt], cw[:, 0, cc:cc + 1], None, ALU.mult, None,
                          [("cu", cc), ("cw", 0), ("cw", 1), ("cw", 2)], [("ct1",)])
                    stt_op("dve", ct2[:, 0:Tt], cu[:, cc, 1:1 + Tt], cw[:, 1, cc:cc + 1], ct1[:, 0:Tt], ALU.mult, ALU.add,
                           [("cu", cc), ("cw", 0), ("cw", 1), ("cw", 2), ("ct1",)], [("ct2",)])
                    stt_op("dve", ct1[:, 0:Tt], cu[:, cc, 2:2 + Tt], cw[:, 2, cc:cc + 1], ct2[:, 0:Tt], ALU.mult, ALU.add,
                           [("cu", cc), ("cw", 0), ("cw", 1), ("cw", 2), ("ct2",)], [("ct1",)])
            pair(2560 + cc * 128, ev_cg, 3072 + cc * 128, ev_u)
            if want_conv_out:
                def ev_zc(ps, tok):
                    act(szc[:, 0:Tt], ps, AF.Silu, tok, [("szc",)])

                def ev_bg(ps, tok, cc=cc):
                    tt_op("dve", ct2[:, 0:Tt], ps, szc[:, 0:Tt], ALU.mult, tok + [("szc",)], [("ct2",)])
                    tt_op("dve", mrgs[mb][:, 4 + cc, 0:Tt], ct2[:, 0:Tt], ct1[:, 0:Tt], ALU.mult,
                          [("ct2",), ("ct1",)], [("mrg", mb, 4 + cc)])
                pair(3584 + cc * 128, ev_zc, 2048 + cc * 128, ev_bg)
        run_ph(6)
        if want_q:
            ev_za = lambda h: (lambda ps, tok: act(sza[:, h, 0:Tt], ps, AF.Silu, tok, [("sza", h)]))
            for h in (0, 2):
                pair(1536 + h * 128, ev_za(h), 1536 + (h + 1) * 128, ev_za(h + 1))
        hb.done()
        run_ph(7)
        if mid2 is not None:
            mid2()
        if STAGE == 0.7:
            return
        k_tr = []
        for si, (row0, r) in enumerate(subtiles):
            for which in ("v", "k"):
                c0 = 1024 if which == "v" else 512
                bk = proj_alloc()
                ps = banks[bk][0:r, :]
                for kc in range(8):
                    mm(ps, nT[:, kc, row0:row0 + r], w_in_sb[:, kc, c0:c0 + 512], kc == 0, kc == 7,
                       [("nT",)] + w_in_tok(kc, c0, c0 + 512), btok(bk))
                kb = next_kv()
                copy_op("act", kvst[kb][0:r, :], ps, btok(bk), [("kvst", kb)])
                if which == "v":
                    copy_op("dve", v_sb[0:r, vtile + si, :], ps, btok(bk), [("v", vtile + si)])
                    dma(vdst[krow0 + row0:krow0 + row0 + r, :], kvst[kb][0:r, :], [("kvst", kb)], [], is_out=True)
                else:
                    dma(kdst[krow0 + row0:krow0 + row0 + r, :], kvst[kb][0:r, :], [("kvst", kb)], [], is_out=True)
                    ps_slot = si % N_SS
                    kb16 = Pt_all[:, ps_slot, :]
                    copy_op("dve", kb16[0:r, :], ps, btok(bk), [("P", ps_slot)])

                    def _tr(kb16=kb16, ps_slot=ps_slot, row0=row0, r=r):
                        bt = proj_alloc()
                        psb = banks[bt].bitcast(BF16)
                        for h in range(4):
                            transpose(psb[:, h * 128:h * 128 + r], kb16[0:r, h * 128:(h + 1) * 128], ident[:r, :r],
                                      [("P", ps_slot), ("ident",)], btok(bt))
                        copy_op("act", kT_sb[:, :, kcol0 + row0:kcol0 + row0 + r],
                                psb[:, 0:512].rearrange("p (h t) -> p h t", h=4)[:, :, 0:r],
                                btok(bt), [tk for h in range(4) for tk in khs[h]])
                        bank_free(bt)
                    k_tr.append(_tr)
                bank_free(bk)
        if ktr_out is not None:
            ktr_out.extend(k_tr)
        else:
            for f_ in k_tr:
                f_()

    def conv_halo(Tt):
        for cc in range(4):
            copy_op("dve", cu[:, cc, 0:2], cu[:, cc, Tt:Tt + 2], [("cu", cc)], [("cu", cc)])

    def attention(Tt, keytiles, mb=0, hooks=None, uhooks=None, ahead=3):
        units = [(h, kt) for h in range(4) for kt in range(len(keytiles))]
        state = {}

        sc = state.setdefault("sc", [0])

        def pairable(u):
            if not MERGE_EXP or Tt != T or u + 1 >= len(units):
                return False
            (h0, k0), (h1, k1) = units[u], units[u + 1]
            A, B = keytiles[k0], keytiles[k1]
            ok = lambda K: K["near"] is None and K["nk"] == 128 and K["qa"] == 0 and K["qb"] == Tt
            return h0 == h1 and ok(A) and ok(B)

        def emit_S(u):
            h, ki = units[u]
            K = keytiles[ki]
            nk, qa, qb = K["nk"], K["qa"], K["qb"]
            role = state.get(("role", u))
            if role is None:
                if pairable(u) and ("role", u + 1) not in state:
                    state[("role", u)] = ("first", sc[0] % N_SS)
                    state[("role", u + 1)] = ("second", (sc[0] + 1) % N_SS)
                    sc[0] += 2
                else:
                    state[("role", u)] = ("single", sc[0] % N_SS)
                    sc[0] += 1
                role = state[("role", u)]
            kind, slot = role
            bk = slot
            ps3 = banks[bk].rearrange("p (s f) -> p s f", s=2)
            if qa == 0 and qb == Tt:
                mm(ps3[0:nk, :, qa:qb], kT_sb[:, h, K["kc0"]:K["kc0"] + nk], qz[:, h, :, qa:qb], True, True,
                   K["ktoks"](h) + [("qz", h, 0), ("qz", h, 1)], btok(bk))
            else:
                for s_ in range(2):
                    mm(ps3[0:nk, s_, qa:qb], kT_sb[:, h, K["kc0"]:K["kc0"] + nk], qz[:, h, s_, qa:qb], True, True,
                       K["ktoks"](h) + [("qz", h, 0), ("qz", h, 1)], btok(bk))
            if K["near"] is not None:
                na, nb, b0c = K["near"]
                P.add("dve", lambda e, o=ps3[0:nk, :, na:nb], b=B0[h][0:nk, b0c:b0c + (nb - na)].unsqueeze(1).to_broadcast(
                    [nk, 2, nb - na]): e.tensor_tensor(out=o, in0=o, in1=b, op=ALU.add),
                    btok(bk) + [("B0", h)], btok(bk))
            pi = slot
            if kind == "single":
                act(Pt[pi][0:nk, :, qa:qb], ps3[0:nk, :, qa:qb], AF.Exp, btok(bk) + [("rb",)], [("P", pi)],
                    bias=rbf[0:nk, h:h + 1])
            elif kind == "second":
                other = (slot - 1) % N_SS
                lo_, hi_ = min(slot, other), max(slot, other)
                st_ = hi_ - lo_
                act(Pt_all[:, lo_:hi_ + 1:st_, :], ps_all[:].rearrange("p (b f) -> p b f", f=512)[:, lo_:hi_ + 1:st_, :],
                    AF.Exp, btok(lo_) + btok(hi_) + [("rb",)], [("P", lo_), ("P", hi_)], bias=rbf[:, h:h + 1])
            state[u] = pi

        def emit_PV(u):
            h, ki = units[u]
            K = keytiles[ki]
            nk, qa, qb = K["nk"], K["qa"], K["qb"]
            pi = state.pop(u)
            first = ki == 0
            last = ki == len(keytiles) - 1
            if first:
                state["O"] = bank_alloc()
                state["L"] = bank_alloc()
            bo, bl = state["O"], state["L"]
            o3 = banks[bo][:].rearrange("p (s f) -> p s f", s=2)
            l3 = banks[bl][:].rearrange("p (s f) -> p s f", s=2)
            if qa == 0 and qb == Tt:
                mm(o3[:, :, qa:qb], v_sb[0:nk, K["vt"], h * 128:(h + 1) * 128], Pt[pi][0:nk, :, qa:qb], first, last,
                   [("v", K["vt"]), ("P", pi)], btok(bo), skip_group_check=True)
                mm(l3[:, :, qa:qb], ones[0:nk, :], Pt[pi][0:nk, :, qa:qb], first, last,
                   [("ones",), ("P", pi)], btok(bl), skip_group_check=True)
            else:
                assert not first
                for s_ in range(2):
                    mm(o3[:, s_, qa:qb], v_sb[0:nk, K["vt"], h * 128:(h + 1) * 128], Pt[pi][0:nk, s_, qa:qb], False,
                       last and s_ == 1, [("v", K["vt"]), ("P", pi)], btok(bo), skip_group_check=True)
                for s_ in range(2):
                    mm(l3[:, s_, qa:qb], ones[0:nk, :], Pt[pi][0:nk, s_, qa:qb], False, last and s_ == 1,
                       [("ones",), ("P", pi)], btok(bl), skip_group_check=True)
            if last:
                combine_a(h, bo, bl)
                state['cb_at'] = u + DEFER

        pending = []

        def combine_a(h, bo, bl):
            flush_pending()
            o3 = banks[bo][:].rearrange("p (s f) -> p s f", s=2)
            l3 = banks[bl][:].rearrange("p (s f) -> p s f", s=2)
            copy_op("dve", rl[:, :, 0:Tt], l3[:, :, 0:Tt], btok(bl), [("rl", 0), ("rl", 1)])
            bank_free(bl)
            tt_op("dve", ob[:, 0:Tt], o3[:, 0, 0:Tt], rl[:, 1, 0:Tt], ALU.mult, btok(bo) + [("rl", 1)], [("ob",)])
            tt_op("dve", tt[:, 0:Tt], o3[:, 1, 0:Tt], rl[:, 0, 0:Tt], ALU.mult, btok(bo) + [("rl", 0)], [("tt",)])
            bank_free(bo)
            stt_op("dve", ob[:, 0:Tt], tt[:, 0:Tt], neglam[:, 0:1], ob[:, 0:Tt], ALU.mult, ALU.add,
                   [("tt",), ("ob",), ("neglam",)], [("ob",)])
            tt_op("dve", sq[:, 0:Tt], ob[:, 0:Tt], ob[:, 0:Tt], ALU.mult, [("ob",)], [("sq",)])
            tt_op("dve", rs[:, 0:Tt], rl[:, 0, 0:Tt], rl[:, 1, 0:Tt], ALU.mult, [("rl", 0), ("rl", 1)], [("rs",)])
            stt_op("dve", rs[:, 0:Tt], rs[:, 0:Tt], EPS, rs[:, 0:Tt], ALU.mult, ALU.mult, [("rs",)], [("rs",)])
            pending.append(h)

        pending2 = []

        def combine_b(h):
            bm = bank_alloc()
            mm(banks[bm][:, 0:Tt], ones[:, :], sq[:, 0:Tt], True, True, [("ones",), ("sq",)], btok(bm))
            stt_op("dve", rs[:, 0:Tt], banks[bm][:, 0:Tt], 1.0 / 128.0, rs[:, 0:Tt], ALU.mult, ALU.add,
                   btok(bm) + [("rs",)], [("rs",)])
            bank_free(bm)
            pending2.append(h)

        def combine_c_act(h):
            act(tt[:, 0:Tt], rs[:, 0:Tt], AF.Ln, [("rs",)], [("tt",)])
            act(rs[:, 0:Tt], tt[:, 0:Tt], AF.Exp, [("tt",)], [("rs",)], scale=-0.5)

        def combine_c_dve(h):
            stt_op("dve", tt[:, 0:Tt], ob[:, 0:Tt], gsc[:, 0:1], rs[:, 0:Tt], ALU.mult, ALU.mult,
                   [("ob",), ("gsc",), ("rs",)], [("tt",)])
            tt_op("dve", mrgs[mb][:, h, 0:Tt], tt[:, 0:Tt], sza[:, h, 0:Tt], ALU.mult, [("tt",), ("sza", h)],
                  [("mrg", mb, h)])

        def combine_c(h):
            combine_c_act(h)
            combine_c_dve(h)

        def flush_b():
            while pending:
                combine_b(pending.pop(0))

        def flush_c():
            while pending2:
                combine_c(pending2.pop(0))

        def flush_pending():
            flush_b()
            flush_c()

        AHEAD = ahead
        npu = len(keytiles)
        DEFER = min(6, max(1, npu - 2))
        DEFER2 = 2 if npu >= 6 else 1
        n = len(units)
        for u in range(min(AHEAD, n)):
            emit_S(u)
        for u in range(n):
            if u + AHEAD < n:
                emit_S(u + AHEAD)
            emit_PV(u)
            if pending2 and u >= state.get('cc_at', 0):
                flush_c()
            if pending and u >= state.get('cb_at', 0):
                flush_b()
                state['cc_at'] = u + DEFER2
            if uhooks and u in uhooks:
                uhooks.pop(u)()
            if hooks and (u + 1) % len(keytiles) == 0 and ((u + 1) // len(keytiles) - 1) in hooks:
                hooks[(u + 1) // len(keytiles) - 1]()
        if uhooks:
            for k_ in sorted(uhooks):
                uhooks[k_]()

        class Fin:
            def __call__(self):
                flush_pending()

            def b(self):
                flush_c()
                flush_b()

            def c_act(self):
                self.hs = list(pending2)
                for h in self.hs:
                    combine_c_act(h)

            def c_dve(self):
                for h in self.hs:
                    combine_c_dve(h)
                del pending2[:]
        return Fin()

    def final_load(x_src, r, b=None):
        if b is None:
            b = next_big()
        dma(big[b][:r, :], x_src, [], [("big", b)])
        return b

    def final_pe(r, row0, mb, b):
        for half in range(2):
            bk = proj_alloc()
            ps = banks[bk][0:r, :]
            for kc in range(8):
                mm(ps, mrgs[mb][:, kc, row0:row0 + r], w_out_sb[:, kc, half * 512:(half + 1) * 512], kc == 0, kc == 7,
                   [("mrg", mb, kc), ("w_out", kc, half)], btok(bk))
            tt_op("dve", big[b][:r, half * 512:(half + 1) * 512], ps, big[b][:r, half * 512:(half + 1) * 512], ALU.add,
                  btok(bk) + [("big", b)], [("big", b)])
            bank_free(bk)

    def final_post1(r, xi, b):
        c0 = 8 + 4 * xi
        P.add("dve", lambda e, o=xns[xi][:r, :], i=big[b][:r, :], a=stat[:r, c0:c0 + 1]: e.scalar_tensor_tensor(
            out=o, in0=i, scalar=1.0, in1=i, op0=ALU.mult, op1=ALU.mult, accum_out=a),
            [("big", b)], [("xn", xi), ("fstat", xi, 0)])

    def final_post2(r, xi, b):
        c0 = 8 + 4 * xi
        act(stat[:r, c0 + 1:c0 + 2], stat[:r, c0:c0 + 1], AF.Ln, [("fstat", xi, 0), ("eps",)], [("fstat", xi, 1)],
            scale=1.0 / D, bias=epsc[:r, 0:1])
        act(stat[:r, c0 + 2:c0 + 3], stat[:r, c0 + 1:c0 + 2], AF.Exp, [("fstat", xi, 1)], [("fstat", xi, 2)], scale=-0.5)

    def final_post3(y_dst, r, xi, b):
        c0 = 8 + 4 * xi
        stt_op("dve", big[b][:r, :], big[b][:r, :], stat[:r, c0 + 2:c0 + 3], fg[:r, :], ALU.mult, ALU.mult,
               [("big", b), ("fstat", xi, 2), ("fg",)], [("big", b)])
        dma(y_dst, big[b][:r, :], [("big", b)], [], is_out=True)

    def final_post(y_dst, r, xi, b):
        c0 = 8 + 4 * xi
        act(xns[xi][:r, :], big[b][:r, :], AF.Square, [("big", b)], [("xn", xi), ("fstat", xi, 0)],
            accum_out=stat[:r, c0:c0 + 1])
        act(stat[:r, c0 + 1:c0 + 2], stat[:r, c0:c0 + 1], AF.Ln, [("fstat", xi, 0), ("eps",)], [("fstat", xi, 1)],
            scale=1.0 / D, bias=epsc[:r, 0:1])
        act(stat[:r, c0 + 2:c0 + 3], stat[:r, c0 + 1:c0 + 2], AF.Exp, [("fstat", xi, 1)], [("fstat", xi, 2)], scale=-0.5)
        stt_op("dve", big[b][:r, :], big[b][:r, :], stat[:r, c0 + 2:c0 + 3], fg[:r, :], ALU.mult, ALU.mult,
               [("big", b), ("fstat", xi, 2), ("fg",)], [("big", b)])
        dma(y_dst, big[b][:r, :], [("big", b)], [], is_out=True)

    def final_subtile(x_src, y_dst, r, row0, mb=0, xi=0, b=None):
        if b is None:
            b = final_load(x_src, r)
        final_pe(r, row0, mb, b)
        final_post(y_dst, r, xi, b)

    def load_w_out():
        for kc in range(8):
            for half in range(2):
                kb = next_kv()
                dma(kvst[kb][:], w_out[kc * 128:(kc + 1) * 128, half * 512:(half + 1) * 512], [], [("kvst", kb)])
                copy_op("act" if half == 0 else "dve", w_out_sb[:, kc, half * 512:(half + 1) * 512], kvst[kb][:],
                        [("kvst", kb)], [("w_out", kc, half)])

    def kT_tok_fn(blk):
        return lambda h: [("kT", h, blk)]

    def finish():
        P.emit(nc, es)
        es.close()
        return nc

    if STAGE == 0:
        return finish()
    early = {}

    def sample_main():
        for jj in range(2):
            dma(cu[:, :, jj], sconv[jj, :].rearrange("(cc p) -> p cc", p=128), [], [("cu", cc) for cc in range(4)])
        norm_subtile(x_s[:, :], DEC, 0, DEC)
        mb_ = norm_load(meta[:, :], NMETA)
        early["meta"] = True
        project_tile(DEC, [(0, DEC)], NCACHE, ntile_c, 0, True, True, k_s, v_s, khs=[kT_all(h) for h in range(4)])
        norm_a(None, NMETA, 1, b=mb_)
        for jj in range(2):
            dma(conv_s[jj, :].rearrange("(cc p) -> p cc", p=128), cu[:, :, DEC + jj],
                [("cu", cc) for cc in range(4)], [], is_out=True)
        kts = []
        for t in range(ntile_c):
            r = min(128, NCACHE - t * 128)
            if t == 7:
                near = (0, DEC, 144)
            elif t == 8:
                near = (0, DEC, 16)
            else:
                near = None
            kts.append(dict(nk=r, kc0=t * 128, ktoks=kT_all, vt=t, qa=0, qb=DEC, near=near))
        kts.append(dict(nk=DEC, kc0=NCACHE, ktoks=kT_all, vt=ntile_c, qa=0, qb=DEC, near=(0, DEC, 0)))
        attention(DEC, kts, ahead=3)()
        final_subtile(x_s[:, :], y_s[:, :], DEC, 0)

    if STAGE is None or STAGE >= 2:
        sample_main()
    if not early.get("meta"):
        norm_a(meta[:, :], NMETA, 1)
    norm_b(NMETA, 0, 1)
    if STAGE is None or STAGE >= 2:
        t0b = [norm_load(x_p[si_ * 128:(si_ + 1) * 128, :], 128) for si_ in range(2)]
        early["t0"] = True
    if STAGE == 0.5:
        return finish()
    project_tile(NMETA, [(0, NMETA)], 0, 0, 0, False, False, k_p, v_p, khs=[[("kT", h, "m")] for h in range(4)])
    if early.get("t0"):
        for si_ in range(2):
            norm_a(None, 128, si_, b=t0b[si_])
    if STAGE in (0.6, 0.7, 0.8):
        return finish()
    conv_halo(NMETA)

    if STAGE == 1:
        return finish()
    ntl = NT if N_PROMPT_TILES is None else N_PROMPT_TILES

    def tile_norm_a(j):
        for si in range(2):
            norm_a(x_p[j * T + si * 128:j * T + (si + 1) * 128, :], 128, si)

    def tile_norm_b(j):
        for si in range(2):
            norm_b(128, si * 128, si)

    def tile_project(j, mid=None, mid2=None, ph=None, ktr_out=None):
        t0 = j * T
        project_tile(T, [(0, 128), (128, 128)], NMETA + t0, 1 + 2 * j, NMETA + t0, True, True, k_p, v_p,
                     khs=[[("kT", h, 2 * j), ("kT", h, 2 * j + 1)] for h in range(4)], mb=j % 2, mid=mid, mid2=mid2, ph=ph, ktr_out=ktr_out)
        if j == NT - 1:
            for jj in range(2):
                dma(conv_p[jj, :].rearrange("(cc p) -> p cc", p=128), cu[:, :, T + jj],
                    [("cu", cc) for cc in range(4)], [], is_out=True)
        conv_halo(T)

    def tile_keytiles(j):
        kts = []
        mnear = (0, 240, 16) if j == 0 else None
        kts.append(dict(nk=NMETA, kc0=0, ktoks=kT_tok_fn("m"), vt=0, qa=0, qb=T, near=mnear))
        for kt in range(2 * j + 2):
            i = kt - 2 * j
            if i == 1:
                qa, near = 128, (128, 256, 0)
            elif i == 0:
                qa, near = 0, (0, 256, 0)
            elif i == -1:
                qa, near = 0, (0, 128, 128)
            else:
                qa, near = 0, None
            kts.append(dict(nk=128, kc0=NMETA + kt * 128, ktoks=kT_tok_fn(kt), vt=1 + kt, qa=qa, qb=T, near=near))
        return kts

    NB = 2

    def xrows(jj, si):
        return x_p[jj * T + si * 128:jj * T + (si + 1) * 128, :]

    def norm_ph(jj, ph):
        if jj >= ntl:
            return
        ph.setdefault(0, []).append(lambda: norm_load(xrows(jj, 0), 128, b=NB))
        ph.setdefault(1, []).insert(1 if ph.get(1) else 0, lambda: norm_a_act(128, 0, NB))
        ph.setdefault(3, []).insert(0, lambda: norm_a_dve(128, 0, NB))
        ph.setdefault(3, []).insert(1, lambda: norm_load(xrows(jj, 1), 128, b=NB))
        ph.setdefault(7, []).append(lambda: norm_a_act(128, 1, NB))
        ph.setdefault(7, []).append(lambda: norm_a_dve(128, 1, NB))

    if not early.get("t0"):
        tile_norm_a(0)
    tile_norm_b(0)
    ph = {}
    norm_ph(1, ph)
    tile_project(0, ph=ph)
    post = []
    for j in range(ntl):
        t0 = j * T
        nxt = j + 1 < ntl
        uhooks = {}
        if nxt:
            uhooks[0] = (lambda jj=j + 1: tile_norm_b(jj))
        for i_, pf in enumerate(post):
            uhooks[1 + i_] = pf
        post = []
        fin = attention(T, tile_keytiles(j), mb=j % 2, hooks=None, uhooks=uhooks)
        fb = [final_load(xrows(j, si), 128, b=si) for si in range(2)]
        if nxt:
            ph = {1: [fin.b, fin.c_act], 3: [fin.c_dve]}
            norm_ph(j + 2, ph)
            ktr = []
            tile_project(j + 1, ph=ph, ktr_out=ktr)
        else:
            ktr = []
            fin()
        for si in range(2):
            final_pe(128, si * 128, j % 2, fb[si])
        for f_ in ktr:
            f_()
        yd = lambda si, t0=t0: y_p[t0 + si * 128:t0 + (si + 1) * 128, :]
        post = [lambda b=fb[0]: final_post1(128, 0, b),
                lambda b=fb[1]: final_post1(128, 1, b),
                lambda b=fb[0]: final_post2(128, 0, b),
                lambda b=fb[1]: final_post2(128, 1, b),
                lambda b=fb[0], yd=yd: final_post3(yd(0), 128, 0, b),
                lambda b=fb[1], yd=yd: final_post3(yd(1), 128, 1, b)]
    for pf in post:
        pf()

    return finish()


_NC_CACHE = {}


def kernel(x_prompt, x_sample, cache_k, cache_v, state_conv, meta_tokens, rel_bias, norm_g, w_in,
           conv_w, lambda_q1, lambda_k1, lambda_q2, lambda_k2, subln_g, w_out, final_g):
    f = lambda a: np.ascontiguousarray(np.asarray(a, dtype=np.float32))
    if "nc" not in _NC_CACHE:
        _NC_CACHE["nc"] = build_nc()
    nc = _NC_CACHE["nc"]
    x_prompt = f(x_prompt); x_sample = f(x_sample)
    cache_k = f(cache_k); cache_v = f(cache_v); state_conv = f(state_conv)
    shared = {
        "meta": f(meta_tokens), "relb": f(rel_bias), "norm_g": f(norm_g).reshape(D),
        "w_in": f(w_in).reshape(D, 4096), "conv_w": f(conv_w).reshape(3, 512),
        "lq1": f(lambda_q1).reshape(64), "lk1": f(lambda_k1).reshape(64),
        "lq2": f(lambda_q2).reshape(64), "lk2": f(lambda_k2).reshape(64),
        "subln_g": f(subln_g).reshape(128), "w_out": f(w_out).reshape(D, D), "final_g": f(final_g).reshape(D),
    }
    in_maps = []
    for c in range(N_CORES):
        m = dict(shared)
        m["x_p"] = x_prompt[c]
        m["x_s"] = x_sample[c]
        m["ck"] = cache_k[0, c].reshape(NCACHE, 512)
        m["cv"] = cache_v[0, c].reshape(NCACHE, 512)
        m["sconv"] = state_conv[0, c]
        in_maps.append(m)
    res = run_bass_kernel_spmd(nc, in_maps, core_ids=list(range(N_CORES)))
    R = res.results
    st = lambda k: np.stack([np.asarray(R[c][k], dtype=np.float32) for c in range(N_CORES)])
    y_prompt = st("y_p")
    y_sample = st("y_s")
    k_prompt = st("k_p").reshape(1, N_CORES, NMETA + SEQ, 4, 128)
    v_prompt = st("v_p").reshape(1, N_CORES, NMETA + SEQ, 4, 128)
    conv_prompt = st("conv_p").reshape(1, N_CORES, 2, 512)
    k_sample = st("k_s").reshape(1, N_CORES, DEC, 4, 128)
    v_sample = st("v_s").reshape(1, N_CORES, DEC, 4, 128)
    conv_sample = st("conv_s").reshape(1, N_CORES, 2, 512)
    return (y_prompt, y_sample, k_prompt, v_prompt, conv_prompt, k_sample, v_sample, conv_sample)
```
